# Optimizing a Trainium2 kernel written in Bass

```python
import jax, jax.numpy as jnp
from jax import lax
import numpy as np

D_MODEL = 1024
BATCH = 32
SEQ = 256
DEPTH = 2
DEC_BATCH = 4
DEC_SEQ = 4096
PAST_LEN = 256

GRID_W = 64
LN_EPS = 1e-6
DN_ALPHA = float((2 * DEPTH) ** 0.25)
DN_BETA = float((8 * DEPTH) ** -0.25)

N_BRANCH = 3
BRANCH_W = D_MODEL
GLA_HEADS = 4
GLA_DK = D_MODEL // (2 * GLA_HEADS)
GLA_DV = BRANCH_W // GLA_HEADS
GLA_LR = 16
GLA_TAU = 16.0
GLA_CHUNK = 32
CONV_W = BRANCH_W
CONV_K = 3
ATT_HD = 64
ATT_HEADS = BRANCH_W // ATT_HD
ATT_KV_HEADS = 4
ATT_GROUP = ATT_HEADS // ATT_KV_HEADS
WINDOW = 128
ATT_BLOCK = 128
ROPE_THETA = 10000.0
PEER_HEADS = 8
PEER_NKEYS = 128
PEER_EXPERTS = PEER_NKEYS * PEER_NKEYS
PEER_TOPK = 16
PEER_DQ = 256
PEER_BLOCK = 128

O_GQ = 0
O_GK = O_GQ + GLA_HEADS * GLA_DK
O_GV = O_GK + GLA_HEADS * GLA_DK
O_GG = O_GV + GLA_HEADS * GLA_DV
O_GA = O_GG + GLA_HEADS * GLA_DV
O_CH = O_GA + 2 * GLA_LR
O_CB = O_CH + CONV_W
O_CC = O_CB + CONV_W
O_AQ = O_CC + CONV_W
O_AK = O_AQ + ATT_HEADS * ATT_HD
O_AV = O_AK + ATT_KV_HEADS * ATT_HD
O_MG = O_AV + ATT_KV_HEADS * ATT_HD
N_IN = O_MG + N_BRANCH * D_MODEL

kernel_name = 'hybrid_gla_conv_swa_peer_dit_step'


def layer_norm(x, g, b):
    xf = x.astype(jnp.float32)
    mu = jnp.mean(xf, -1, keepdims=True)
    var = jnp.mean(jnp.square(xf - mu), -1, keepdims=True)
    y = (xf - mu) * lax.rsqrt(var + LN_EPS) * g.astype(jnp.float32) + b.astype(jnp.float32)
    return y.astype(x.dtype)


def ada_modulation(cond, w_mod, b_mod):
    m = jax.nn.silu(cond) @ w_mod + b_mod
    return jnp.split(m[:, None, :], 6, axis=-1)


def axial_rope_tables(T):
    rows = T // GRID_W
    row = jnp.repeat(jnp.arange(rows, dtype=jnp.float32), GRID_W)
    col = jnp.tile(jnp.arange(GRID_W, dtype=jnp.float32), rows)
    half = ATT_HD // 2
    inv = ROPE_THETA ** (-jnp.arange(0, half, 2, dtype=jnp.float32) / half)
    ang = jnp.concatenate([row[:, None] * inv, col[:, None] * inv], -1)
    return jnp.cos(ang), jnp.sin(ang)


def apply_rope(x, cos, sin):
    B, T, H, hd = x.shape
    xr = x.astype(jnp.float32).reshape(B, T, H, hd // 2, 2)
    x1, x2 = xr[..., 0], xr[..., 1]
    c = cos[None, :, None, :]
    s = sin[None, :, None, :]
    return jnp.stack([x1 * c - x2 * s, x1 * s + x2 * c], -1).reshape(B, T, H, hd).astype(x.dtype)


def gla_scan(q, k, v, log_a, s0):
    B, T, H, DK = q.shape
    DV = v.shape[-1]
    n = T // GLA_CHUNK

    def chunks(z):
        return z.astype(jnp.float32).reshape(B, n, GLA_CHUNK, H, z.shape[-1]).transpose(1, 0, 3, 2, 4)

    tri = jnp.tril(jnp.ones((GLA_CHUNK, GLA_CHUNK), dtype=bool))[:, :, None]

    def step(S, inp):
        qc, kc, vc, ac = inp
        b = jnp.cumsum(ac, axis=2)
        rel = jnp.where(tri, b[:, :, :, None, :] - b[:, :, None, :, :], -jnp.inf)
        att = jnp.einsum('bhtsd,bhsd->bhts', qc[:, :, :, None, :] * jnp.exp(rel), kc)
        o = jnp.einsum('bhts,bhsv->bhtv', att, vc) + jnp.einsum('bhtd,bhdv->bhtv', qc * jnp.exp(b), S)
        b_end = b[:, :, -1:, :]
        S = jnp.exp(b_end[:, :, 0, :, None]) * S + jnp.einsum('bhsd,bhsv->bhdv', kc * jnp.exp(b_end - b), vc)
        return S, o

    S, o = lax.scan(step, s0.astype(jnp.float32), (chunks(q), chunks(k), chunks(v), chunks(log_a)))
    return o.transpose(1, 0, 3, 2, 4).reshape(B, T, H, DV), S


def short_conv(h, gate_b, gate_c, w):
    z = gate_c * h
    zp = jnp.pad(z, ((0, 0), (1, 1), (0, 0)))
    y = w[0] * zp[:, :-2] + w[1] * zp[:, 1:-1] + w[2] * zp[:, 2:]
    return gate_b * y


def context_attention(q, k, v, sink):
    B, T, _, hd = q.shape
    qg = q.reshape(B, T, ATT_KV_HEADS, ATT_GROUP, hd)
    s = jnp.einsum('bqhgd,bkhd->bhgqk', qg, k).astype(jnp.float32) * (hd ** -0.5)
    sk = sink.astype(jnp.float32).reshape(ATT_KV_HEADS, ATT_GROUP)[None, :, :, None, None]
    m = jnp.maximum(jnp.max(s, -1, keepdims=True), sk)
    p = jnp.exp(s - m)
    p = p / (jnp.sum(p, -1, keepdims=True) + jnp.exp(sk - m))
    o = jnp.einsum('bhgqk,bkhd->bqhgd', p, v.astype(jnp.float32))
    return o.reshape(B, T, ATT_HEADS * hd).astype(q.dtype)


def latent_window_attention(q, k, v, kc, vc, sink):
    B, T, _, hd = q.shape
    nb = T // ATT_BLOCK
    qb = q.reshape(B, nb, ATT_BLOCK, ATT_KV_HEADS, ATT_GROUP, hd)

    def band(z):
        zp = jnp.pad(z, ((0, 0), (ATT_BLOCK, ATT_BLOCK), (0, 0), (0, 0)))
        zp = zp.reshape(B, nb + 2, ATT_BLOCK, ATT_KV_HEADS, hd)
        return jnp.concatenate([zp[:, :-2], zp[:, 1:-1], zp[:, 2:]], axis=2)

    kb, vb = band(k), band(v)
    scale = hd ** -0.5
    s_loc = jnp.einsum('bnqhgd,bnkhd->bnhgqk', qb, kb).astype(jnp.float32) * scale
    qpos = jnp.arange(nb)[:, None] * ATT_BLOCK + jnp.arange(ATT_BLOCK)[None, :]
    kpos = (jnp.arange(nb)[:, None] - 1) * ATT_BLOCK + jnp.arange(3 * ATT_BLOCK)[None, :]
    valid = ((jnp.abs(qpos[:, :, None] - kpos[:, None, :]) <= WINDOW)
             & (kpos[:, None, :] >= 0) & (kpos[:, None, :] < T))
    s_loc = jnp.where(valid[None, :, None, None], s_loc, -jnp.inf)
    s_ctx = jnp.einsum('bnqhgd,bchd->bnhgqc', qb, kc).astype(jnp.float32) * scale
    sk = sink.astype(jnp.float32).reshape(ATT_KV_HEADS, ATT_GROUP)[None, None, :, :, None, None]
    m = jnp.maximum(jnp.maximum(jnp.max(s_loc, -1, keepdims=True), jnp.max(s_ctx, -1, keepdims=True)), sk)
    p_loc = jnp.exp(s_loc - m)
    p_ctx = jnp.exp(s_ctx - m)
    den = jnp.sum(p_loc, -1, keepdims=True) + jnp.sum(p_ctx, -1, keepdims=True) + jnp.exp(sk - m)
    o = (jnp.einsum('bnhgqk,bnkhd->bnqhgd', p_loc / den, vb.astype(jnp.float32))
         + jnp.einsum('bnhgqc,bchd->bnqhgd', p_ctx / den, vc.astype(jnp.float32)))
    return o.reshape(B, T, ATT_HEADS * hd).astype(q.dtype)


def token_mixer(h, lp, ctx):
    B, T, _ = h.shape
    p = h @ lp['w_in']
    gq = p[..., O_GQ:O_GK].reshape(B, T, GLA_HEADS, GLA_DK) * (GLA_DK ** -0.5)
    gk = p[..., O_GK:O_GV].reshape(B, T, GLA_HEADS, GLA_DK)
    gv = p[..., O_GV:O_GG].reshape(B, T, GLA_HEADS, GLA_DV)
    gg = p[..., O_GG:O_GA]

    def log_decay(i):
        z = p[..., O_GA + i * GLA_LR:O_GA + (i + 1) * GLA_LR] @ lp['w_gla_a2'][i] + lp['b_gla_a'][i]
        return (jax.nn.log_sigmoid(z.astype(jnp.float32)) / GLA_TAU).reshape(B, T, GLA_HEADS, GLA_DK)

    if ctx is None:
        s0f = jnp.zeros((B, GLA_HEADS, GLA_DK, GLA_DV), jnp.float32)
        s0b = s0f
    else:
        s0f, s0b = ctx['s_f'], ctx['s_b']
    o_f, s_f = gla_scan(gq, gk, gv, log_decay(0), s0f)
    flip = lambda z: jnp.flip(z, axis=1)
    o_b, s_b = gla_scan(flip(gq), flip(gk), flip(gv), flip(log_decay(1)), s0b)
    o = o_f + flip(o_b)
    o = o * lax.rsqrt(jnp.mean(jnp.square(o), -1, keepdims=True) + LN_EPS) * lp['gla_norm_g'].astype(jnp.float32)
    y_a = (o.reshape(B, T, BRANCH_W) * jax.nn.silu(gg.astype(jnp.float32))).astype(h.dtype)
    y_b = short_conv(p[..., O_CH:O_CB], p[..., O_CB:O_CC], p[..., O_CC:O_AQ], lp['conv_w'])
    aq = p[..., O_AQ:O_AK].reshape(B, T, ATT_HEADS, ATT_HD)
    ak = p[..., O_AK:O_AV].reshape(B, T, ATT_KV_HEADS, ATT_HD)
    av = p[..., O_AV:O_MG].reshape(B, T, ATT_KV_HEADS, ATT_HD)
    if ctx is None:
        y_c = context_attention(aq, ak, av, lp['attn_sink'])
        new = (ak, av, s_f, s_b)
    else:
        aq = apply_rope(aq, ctx['cos'], ctx['sin'])
        ak = apply_rope(ak, ctx['cos'], ctx['sin'])
        y_c = latent_window_attention(aq, ak, av, ctx['k'], ctx['v'], lp['attn_sink'])
        new = None
    ys = jnp.stack([y_a, y_b, y_c], axis=2)
    gates = jax.nn.sigmoid(p[..., O_MG:].reshape(B, T, N_BRANCH, D_MODEL))
    merged = jnp.sum(gates * jnp.einsum('btnw,nwd->btnd', ys, lp['w_branch']), axis=2)
    return merged @ lp['w_out'], new


def peer_ffn(h, w_pq, keys, u_tab, v_tab):
    B, T, D = h.shape
    nt = B * T
    xf = h.reshape(nt, D)
    q = (xf @ w_pq).reshape(nt, PEER_HEADS, 2, PEER_DQ // 2)
    s = jnp.einsum('thpc,hpkc->thpk', q, keys).astype(jnp.float32)
    s1, i1 = lax.top_k(s[:, :, 0], PEER_TOPK)
    s2, i2 = lax.top_k(s[:, :, 1], PEER_TOPK)
    cand = (s1[..., :, None] + s2[..., None, :]).reshape(nt, PEER_HEADS, PEER_TOPK * PEER_TOPK)
    cidx = (i1[..., :, None] * PEER_NKEYS + i2[..., None, :]).reshape(nt, PEER_HEADS, PEER_TOPK * PEER_TOPK)
    top_s, top_pos = lax.top_k(cand, PEER_TOPK)
    eidx = jnp.take_along_axis(cidx, top_pos, axis=-1)
    gate = jax.nn.softmax(top_s, axis=-1).astype(h.dtype)
    nb = nt // PEER_BLOCK

    def block(args):
        xb, ib, gb = args
        act = jax.nn.gelu(jnp.einsum('td,thkd->thk', xb, u_tab[ib]), approximate=False)
        return jnp.einsum('thk,thkd->td', gb * act, v_tab[ib])

    out = lax.map(block, (xf.reshape(nb, PEER_BLOCK, D),
                          eidx.reshape(nb, PEER_BLOCK, PEER_HEADS, PEER_TOPK),
                          gate.reshape(nb, PEER_BLOCK, PEER_HEADS, PEER_TOPK)))
    return out.reshape(B, T, D)


def trunk_layer(x, lp, mod, ctx):
    sh1, sc1, g1, sh2, sc2, g2 = mod
    mix, new = token_mixer(x * (1 + sc1) + sh1, lp, ctx)
    x = layer_norm(DN_ALPHA * x + g1 * mix, lp['ln1_g'], lp['ln1_b'])
    ffn = peer_ffn(x * (1 + sc2) + sh2, lp['w_pq'], lp['peer_keys'], lp['peer_u'], lp['peer_v'])
    x = layer_norm(DN_ALPHA * x + g2 * ffn, lp['ln2_g'], lp['ln2_b'])
    return x, new


def setup_inputs(seed: int = 0) -> dict:
    key = jax.random.key(seed)
    ks = jax.random.split(key, 32)
    f32 = jnp.float32

    def nrm(k, shape, s):
        return jax.random.normal(k, shape, f32) * s

    D = D_MODEL
    return {
        'x_prompt': nrm(ks[0], (BATCH, SEQ, D), 1.0),
        'x_sample': nrm(ks[1], (DEC_BATCH, DEC_SEQ, D), 1.0),
        'cache_k': nrm(ks[2], (DEC_BATCH, DEPTH, PAST_LEN, ATT_KV_HEADS, ATT_HD), 1.0),
        'cache_v': nrm(ks[3], (DEC_BATCH, DEPTH, PAST_LEN, ATT_KV_HEADS, ATT_HD), 1.0),
        'state_gla': nrm(ks[4], (DEC_BATCH, DEPTH, 2, GLA_HEADS, GLA_DK, GLA_DV), 1.0),
        'c': nrm(ks[5], (DEC_BATCH, D), 1.0),
        'c_ctx': nrm(ks[6], (D,), 1.0),
        'ln_in_g': 1.0 + nrm(ks[7], (D,), 0.02),
        'ln_in_b': nrm(ks[8], (D,), 0.02),
        'w_mod': nrm(ks[9], (DEPTH, D, 6 * D), 0.5 * D ** -0.5),
        'b_mod': nrm(ks[10], (DEPTH, 6 * D), 0.02),
        'w_in': nrm(ks[11], (DEPTH, D, N_IN), D ** -0.5),
        'w_gla_a2': nrm(ks[12], (DEPTH, 2, GLA_LR, GLA_HEADS * GLA_DK), GLA_LR ** -0.5),
        'b_gla_a': nrm(ks[13], (DEPTH, 2, GLA_HEADS * GLA_DK), 0.1),
        'gla_norm_g': 1.0 + nrm(ks[14], (DEPTH, GLA_DV), 0.02),
        'conv_w': nrm(ks[15], (DEPTH, CONV_K, CONV_W), CONV_K ** -0.5),
        'attn_sink': nrm(ks[16], (DEPTH, ATT_HEADS), 0.5),
        'w_branch': nrm(ks[17], (DEPTH, N_BRANCH, BRANCH_W, D), BRANCH_W ** -0.5),
        'w_out': nrm(ks[18], (DEPTH, D, D), DN_BETA * D ** -0.5),
        'ln1_g': 1.0 + nrm(ks[19], (DEPTH, D), 0.02),
        'ln1_b': nrm(ks[20], (DEPTH, D), 0.02),
        'w_pq': nrm(ks[21], (DEPTH, D, PEER_HEADS * PEER_DQ), D ** -0.5),
        'peer_keys': nrm(ks[22], (DEPTH, PEER_HEADS, 2, PEER_NKEYS, PEER_DQ // 2), (PEER_DQ // 2) ** -0.5),
        'peer_u': nrm(ks[23], (DEPTH, PEER_EXPERTS, D), D ** -0.5),
        'peer_v': nrm(ks[24], (DEPTH, PEER_EXPERTS, D), DN_BETA * PEER_HEADS ** -0.5),
        'ln2_g': 1.0 + nrm(ks[25], (DEPTH, D), 0.02),
        'ln2_b': nrm(ks[26], (DEPTH, D), 0.02),
    }


def reference(x_prompt, x_sample, cache_k, cache_v, state_gla, c, c_ctx, ln_in_g, ln_in_b,
              w_mod, b_mod, w_in, w_gla_a2, b_gla_a, gla_norm_g, conv_w, attn_sink, w_branch,
              w_out, ln1_g, ln1_b, w_pq, peer_keys, peer_u, peer_v, ln2_g, ln2_b):
    cos, sin = axial_rope_tables(x_sample.shape[1])
    xp = layer_norm(x_prompt, ln_in_g, ln_in_b)
    xs = layer_norm(x_sample, ln_in_g, ln_in_b)
    ks, vs, ss = [], [], []
    for l in range(DEPTH):
        lp = {'w_in': w_in[l], 'w_gla_a2': w_gla_a2[l], 'b_gla_a': b_gla_a[l], 'gla_norm_g': gla_norm_g[l],
              'conv_w': conv_w[l], 'attn_sink': attn_sink[l], 'w_branch': w_branch[l], 'w_out': w_out[l],
              'ln1_g': ln1_g[l], 'ln1_b': ln1_b[l], 'w_pq': w_pq[l], 'peer_keys': peer_keys[l],
              'peer_u': peer_u[l], 'peer_v': peer_v[l], 'ln2_g': ln2_g[l], 'ln2_b': ln2_b[l]}
        mod_ctx = ada_modulation(c_ctx[None, :], w_mod[l], b_mod[l])
        xp, (k_l, v_l, sf_l, sb_l) = trunk_layer(xp, lp, mod_ctx, None)
        ks.append(k_l)
        vs.append(v_l)
        ss.append(jnp.stack([sf_l, sb_l], axis=1))
        mod_lat = ada_modulation(c, w_mod[l], b_mod[l])
        ctx = {'k': cache_k[:, l], 'v': cache_v[:, l], 's_f': state_gla[:, l, 0], 's_b': state_gla[:, l, 1],
               'cos': cos, 'sin': sin}
        xs, _ = trunk_layer(xs, lp, mod_lat, ctx)
    new_cache_k = jnp.stack(ks, axis=1)
    new_cache_v = jnp.stack(vs, axis=1)
    new_state_gla = jnp.stack(ss, axis=1)
    return (xp, xs, new_cache_k, new_cache_v, new_state_gla)
```

```python
from contextlib import ExitStack
import numpy as np
import concourse.bass as bass
import concourse.mybir as mybir
from concourse.bass_utils import run_bass_kernel_spmd

ACT = mybir.ActivationFunctionType
ALU = mybir.AluOpType
AX = mybir.AxisListType
F32 = mybir.dt.float32
BF16 = mybir.dt.bfloat16
I32 = mybir.dt.int32
U32 = mybir.dt.uint32

EPOCH = 12000
NDSEM = {'sp': 24, 'pool': 24, 'act': 8}


class Res:
    __slots__ = ('name', 'w', 'r', 'excl')

    def __init__(self, name='r', excl=False):
        self.name = name
        self.w = {}
        self.r = {}
        self.excl = excl


class KB:
    def __init__(self, nc):
        self.nc = nc
        self.engs = {'pe': nc.tensor, 'dve': nc.vector, 'act': nc.scalar, 'pool': nc.gpsimd, 'sp': nc.sync}
        self.esem = {}
        self.ecnt = {n: 0 for n in self.engs}
        self.etot = {n: 0 for n in self.engs}
        self.eepoch = {n: 0 for n in self.engs}
        self.known = {n: {} for n in self.engs}
        self.dsems = {}
        self.dval = {}
        self.dnext = {q: 0 for q in NDSEM}
        self.dpool = {}
        for q, n in NDSEM.items():
            self.dpool[q] = []
            for i in range(n):
                s = nc.alloc_semaphore(f'd_{q}_{i}')
                self.dsems[(q, i)] = s
                self.dval[(q, i)] = 0
                self.dpool[q].append((q, i))
        self.nwaits = 0
        self.wcnt = {}
        self.out_events = []

    def _esem(self, eng, ep):
        k = (eng, ep)
        if k not in self.esem:
            self.esem[k] = self.nc.alloc_semaphore(f'e_{eng}_{ep}')
        return self.esem[k]

    def _deps(self, reads, writes):
        deps = {}
        for r in reads:
            for k, v in r.w.items():
                if deps.get(k, 0) < v:
                    deps[k] = v
        for w in writes:
            for k, v in w.w.items():
                if deps.get(k, 0) < v:
                    deps[k] = v
            for k, v in w.r.items():
                if deps.get(k, 0) < v:
                    deps[k] = v
        return deps

    def _wait(self, eng, deps):
        h = self.engs[eng]
        kn = self.known[eng]
        for k, v in deps.items():
            if kn.get(k, 0) >= v:
                continue
            if k[0] == 'E':
                sem = self._esem(k[1], k[2])
            else:
                sem = self.dsems[(k[1], k[2])]
            h.wait_ge(sem, v)
            self.nwaits += 1
            self.wcnt[eng] = self.wcnt.get(eng, 0) + 1
            kn[k] = v

    def _record(self, key, val, reads, writes):
        for r in reads:
            if r.r.get(key, 0) < val:
                r.r[key] = val
        for w in writes:
            w.w = {key: val}
            w.r = {}

    def op(self, eng, fn, reads=(), writes=()):
        if any(r.excl for r in reads):
            writes = list(writes) + [r for r in reads if r.excl]
            reads = [r for r in reads if not r.excl]
        deps = self._deps(reads, writes)
        if eng == 'pe':
            for k in [k for k in deps if k[0] == 'E' and k[1] == 'pe']:
                del deps[k]
        else:
            cur_ep = self.eepoch[eng]
            for k in [k for k in deps if k[0] == 'E' and k[1] == eng]:
                if k[2] < cur_ep or (self.ecnt[eng] + 1 - deps[k]) >= 2:
                    del deps[k]
        self._wait(eng, deps)
        ins = fn(self.engs[eng])
        if self.ecnt[eng] >= EPOCH:
            self.eepoch[eng] += 1
            self.ecnt[eng] = 0
        self.ecnt[eng] += 1
        self.etot[eng] += 1
        ep = self.eepoch[eng]
        ins.then_inc(self._esem(eng, ep), 1)
        self._record(('E', eng, ep), self.ecnt[eng], reads, writes)
        return ins

    def dma(self, q, fn, reads=(), writes=(), is_output=False):
        deps = self._deps(reads, writes)
        self._wait(q, deps)
        slot = self.dpool[q][self.dnext[q]]
        self.dnext[q] = (self.dnext[q] + 1) % len(self.dpool[q])
        prev = self.dval[slot]
        key = ('D', slot[0], slot[1])
        if prev > 0:
            self._wait(q, {key: prev})
        ins = fn(self.engs[q])
        ins.then_inc(self.dsems[slot], 16)
        self.dval[slot] = prev + 16
        self._record(key, prev + 16, reads, writes)
        if is_output:
            self.out_events.append((key, prev + 16))
        return ins

    def dma_group(self, q, fns, reads=(), writes=()):
        deps = self._deps(reads, writes)
        self._wait(q, deps)
        slot = self.dpool[q][self.dnext[q]]
        self.dnext[q] = (self.dnext[q] + 1) % len(self.dpool[q])
        prev = self.dval[slot]
        key = ('D', slot[0], slot[1])
        if prev > 0:
            self._wait(q, {key: prev})
        for fn in fns:
            ins = fn(self.engs[q])
            ins.then_inc(self.dsems[slot], 16)
        self.dval[slot] = prev + 16 * len(fns)
        self._record(key, self.dval[slot], reads, writes)

    def finish(self):
        deps = {}
        for slot, v in self.dval.items():
            if v > 0:
                deps[('D', slot[0], slot[1])] = v
        self._wait('sp', deps)
        deps = {}
        for eng in self.engs:
            if eng == 'sp':
                continue
            if self.etot[eng] > 0:
                deps[('E', eng, self.eepoch[eng])] = self.ecnt[eng]
        self._wait('sp', deps)


def _barrier(self):
    ev = {}
    for eng in self.engs:
        if self.etot[eng] > 0:
            ev[('E', eng, self.eepoch[eng])] = self.ecnt[eng]
    for slot, v in self.dval.items():
        if v > 0:
            ev[('D', slot[0], slot[1])] = v
    for eng in self.engs:
        self._wait(eng, dict(ev))


KB.barrier = _barrier


class Ring:
    def __init__(self, alloc, name, shape, dt, n, excl=False):
        self.t = [alloc(f"{name}{i}", shape, dt) for i in range(n)]
        self.r = [Res(f"{name}{i}", excl) for i in range(n)]
        self.i = 0

    def get(self):
        k = self.i
        self.i = (self.i + 1) % len(self.t)
        return self.t[k], self.r[k]


D = 1024
O_GQ, O_GK, O_GV, O_GG, O_GA, O_CH, O_CB, O_CC, O_AQ, O_AK, O_AV, O_MG = (
    0, 512, 1024, 2048, 3072, 3104, 4128, 5152, 6176, 7200, 7456, 7712)
ALPHA = float(4 ** 0.25)
EPS = 1e-6
NBLK = 24
NOMI = False
PERM = np.concatenate([np.arange(0, 64, 2), np.arange(1, 64, 2)])
GRID_W = 64


def _blockify(W, cols):
    out = np.zeros((128, 8, 512), np.float32)
    n = len(cols)
    out[:, :, :n] = W[:, cols].reshape(8, 128, n).transpose(1, 0, 2)
    return out


def win_blocks(w):
    r = np.arange
    blks = [r(O_GQ, O_GQ + 512), r(O_GK, O_GK + 512), r(O_GG, O_GG + 512), r(O_GG + 512, O_GG + 1024),
            r(O_GA, O_GA + 32),
            r(O_CH, O_CH + 512), r(O_CH + 512, O_CH + 1024), r(O_CB, O_CB + 512), r(O_CB + 512, O_CB + 1024),
            r(O_CC, O_CC + 512), r(O_CC + 512, O_CC + 1024)]
    for half in range(2):
        blks.append(np.concatenate([O_AQ + (half * 8 + h) * 64 + PERM for h in range(8)]))
    blks.append(np.concatenate([O_AK + j * 64 + PERM for j in range(4)]))
    for i in range(6):
        blks.append(r(O_MG + i * 512, O_MG + (i + 1) * 512))
    blks += [r(O_GK, O_GK + 512), r(O_GV, O_GV + 512), r(O_GV + 512, O_GV + 1024),
             np.concatenate([r(O_AK, O_AK + 256), r(O_AV, O_AV + 256)])]
    assert len(blks) == NBLK
    return np.stack([_blockify(w, c) for c in blks])


def rope_tables(T):
    rows = T // GRID_W
    row = np.repeat(np.arange(rows, dtype=np.float32), GRID_W)
    col = np.tile(np.arange(GRID_W, dtype=np.float32), rows)
    half = 32
    inv = (np.float32(10000.0) ** (-np.arange(0, half, 2, dtype=np.float32) / np.float32(half))).astype(np.float32)
    ang = np.concatenate([row[:, None] * inv, col[:, None] * inv], -1).astype(np.float32)
    c = np.cos(ang).astype(np.float32).T
    s = np.sin(ang).astype(np.float32).T
    return np.ascontiguousarray(np.concatenate([c, c], 0)), np.ascontiguousarray(np.concatenate([s, s], 0))


def make_consts():
    i = np.arange(128)
    ident = np.eye(128, dtype=np.float32)
    Mf = (i[:, None] <= i[None, :]).astype(np.float32)
    Mb = (i[:, None] >= i[None, :]).astype(np.float32)
    Nf = (i[:, None] > i[None, :]).astype(np.float32)
    Nb = (i[:, None] < i[None, :]).astype(np.float32)
    return np.ascontiguousarray(np.stack([ident, Mf, Mb, Nf, Nb], 1))


def prep_shared(inp):
    f = lambda a: np.ascontiguousarray(np.asarray(a, dtype=np.float32))
    sh = {}
    sh['win'] = f(np.stack([win_blocks(np.asarray(inp['w_in'][l])) for l in range(2)]))
    wm = np.asarray(inp['w_mod'])
    sh['wmod'] = f(np.stack([np.stack([_blockify(wm[l], np.arange(b * 512, (b + 1) * 512)) for b in range(12)]) for l in range(2)]))
    sh['bmod'] = f(inp['b_mod'])
    wa = np.zeros((2, 33, 1024), np.float32)
    for l in range(2):
        wa[l, 0:16, 0:512] = inp['w_gla_a2'][l, 0]
        wa[l, 16:32, 512:1024] = inp['w_gla_a2'][l, 1]
        wa[l, 32, 0:512] = inp['b_gla_a'][l, 0]
        wa[l, 32, 512:1024] = inp['b_gla_a'][l, 1]
    sh['wa2'] = wa
    sh['glag'] = f(np.asarray(inp['gla_norm_g']).reshape(2, 2, 128).transpose(0, 2, 1))
    sh['convw'] = f(np.asarray(inp['conv_w']).reshape(2, 3, 8, 128).transpose(0, 3, 1, 2))
    sh['sink'] = f(inp['attn_sink'])
    wb = np.asarray(inp['w_branch'])
    sh['wb'] = f(wb.reshape(2, 3, 8, 128, 1024).transpose(0, 3, 1, 2, 4))
    sh['wout'] = f(np.asarray(inp['w_out']).reshape(2, 8, 128, 1024).transpose(0, 2, 1, 3))
    sh['wpq'] = f(np.asarray(inp['w_pq']).reshape(2, 8, 128, 2048).transpose(0, 2, 1, 3))
    pk = np.asarray(inp['peer_keys'])
    sh['pkeys'] = f(pk.reshape(2, 16, 128, 128).transpose(0, 3, 1, 2))
    for l in range(2):
        sh[f'pu{l}'] = f(np.asarray(inp['peer_u'])[l])
        sh[f'pv{l}'] = f(np.asarray(inp['peer_v'])[l])
    lnv = np.stack([np.asarray(inp[k]) for k in ('ln1_g', 'ln1_b', 'ln2_g', 'ln2_b')], 1)
    sh['lnv'] = f(lnv)
    sh['lnin'] = f(np.stack([np.asarray(inp['ln_in_g']), np.asarray(inp['ln_in_b'])]))
    sh['consts'] = make_consts()
    ci = np.zeros((128, 4, 256), np.int32)
    ci[:, 0, :] = -128
    ci[:, 1, :] = np.arange(256, dtype=np.int32)[None, :]
    ci[:, 2, :] = 127
    ci[:, 3, :] = -256
    sh['cint'] = ci
    c, s = rope_tables(4096)
    sh['cosd'] = c
    sh['sind'] = s
    return sh


def prep_core(inp, sh, core):
    f = lambda a: np.ascontiguousarray(np.asarray(a, dtype=np.float32))
    b = core // 2
    m = dict(sh)
    m['xp'] = f(np.asarray(inp['x_prompt'])[core * 4:(core + 1) * 4].reshape(1024, 1024))
    m['xs'] = f(np.asarray(inp['x_sample'])[b])
    ck = np.asarray(inp['cache_k'])[b]
    m['ck'] = f(ck[:, :, :, PERM].transpose(0, 3, 2, 1))
    m['cv'] = f(np.asarray(inp['cache_v'])[b].reshape(2, 256, 256))
    st = np.asarray(inp['state_gla'])[b]
    m['st'] = f(st.transpose(0, 1, 3, 2, 4))
    m['cc'] = f(np.stack([np.asarray(inp['c_ctx']), np.asarray(inp['c'])[b]]))
    return m


def _stage_ctx(flag):
    if flag:
        es = ExitStack()
        yield es
        es.close()


def build(do_p=True, do_s=True, nlayers=2, dbg=None, peer=True, TM=4096, stages='MAGTCP'):
    nc = bass.Bass("TRN2", target_bir_lowering=False)
    kb = KB(nc)
    ctx_nc = nc.allow_non_contiguous_dma(reason="small strided loads")
    ctx_nc.__enter__()

    def din(name, shape, dt=F32):
        return nc.dram_tensor(name, list(shape), dt, kind="ExternalInput").ap()

    def dout(name, shape, dt=F32):
        return nc.dram_tensor(name, list(shape), dt, kind="ExternalOutput").ap()

    def dscr(name, shape, dt):
        return nc.dram_tensor(name, list(shape), dt, kind="Internal").ap()

    NEXP = 16384 if peer else 128
    I = {}
    for name, shape in [('xp', (1024, 1024)), ('xs', (4096, 1024)), ('ck', (2, 64, 4, 256)), ('cv', (2, 256, 256)),
                        ('st', (2, 2, 128, 4, 256)), ('cc', (2, 1024)), ('win', (2, NBLK, 128, 8, 512)),
                        ('wmod', (2, 12, 128, 8, 512)), ('bmod', (2, 6144)), ('wa2', (2, 33, 1024)),
                        ('glag', (2, 128, 2)), ('convw', (2, 128, 3, 8)), ('sink', (2, 16)),
                        ('wb', (2, 128, 3, 8, 1024)), ('wout', (2, 128, 8, 1024)),
                        ('wpq', (2, 128, 8, 2048)), ('pkeys', (2, 128, 16, 128)), ('pu0', (NEXP, 1024)), ('pu1', (NEXP, 1024)),
                        ('pv0', (NEXP, 1024)), ('pv1', (NEXP, 1024)), ('lnv', (2, 4, 1024)), ('lnin', (2, 1024)),
                        ('consts', (128, 5, 128)), ('cosd', (64, 4096)), ('sind', (64, 4096))]:
        I[name] = din(name, shape)
    I['cint'] = din('cint', (128, 4, 256), I32)
    O = {'yp': dout('yp', (1024, 1024)), 'ys': dout('ys', (4096, 1024)),
         'nk': dout('nk', (4, 2, 256, 256)), 'nv': dout('nv', (4, 2, 256, 256)),
         'nst': dout('nst', (4, 2, 2, 4, 128, 256))}
    if peer == 'idx':
        O['dbg_ei'] = dout('dbg_ei', (2, 16, 128, 128))
        O['dbg_eii'] = dout('dbg_eii', (2, 16, 128, 128), I32)
        O['dbg_ts'] = dout('dbg_ts', (2, 16, 128, 128))
    S = {}
    for name, shape, dt in [('qT', (4, 128, TM), BF16), ('kT', (4, 128, TM), BF16), ('ggT', (8, 128, TM), BF16),
                            ('chT', (8, 128, TM), BF16), ('cbT', (8, 128, TM), BF16), ('ccT', (8, 128, TM), BF16),
                            ('qaT', (16, 64, TM), BF16), ('kaT', (4, 64, TM), BF16), ('gT', (24, 128, TM), BF16),
                            ('ktm', (TM, 512), BF16), ('vtm', (TM, 1024), BF16), ('avtm', (TM, 256), BF16),
                            ('la', (TM, 1024), F32), ('of', (8, 128, TM), F32), ('ob', (8, 128, TM), F32),
                            ('ycT', (16, 64, TM), BF16), ('qrT', (16, 64, TM), BF16), ('krT', (4, 64, TM), BF16),
                            ('xres', (TM, 1024), F32)]:
        S[name] = dscr('s_' + name, shape, dt)
    SR = {k: Res(k) for k in S}
    xres_r = [Res(f'xres{i}') for i in range(TM // 128)]
    if dbg:
        for name in dbg:
            O['dbg_' + name] = dout('dbg_' + name, S[name].shape, S[name].dtype)

    _uid = [0]

    def uq(n):
        _uid[0] += 1
        return f"t{_uid[0]}_{n}"

    A = lambda n, s, d: nc.alloc_sbuf_tensor(uq(n), s, d)
    cst = A("cst", [128, 5, 128], F32); r_cst = Res('cst')
    kb.dma('sp', lambda e: e.dma_start(out=cst[:], in_=I['consts']), writes=[r_cst])
    ident = cst[:, 0, :]; Mf = cst[:, 1, :]; Mb = cst[:, 2, :]; Nf = cst[:, 3, :]; Nb = cst[:, 4, :]
    cint = A("cint", [128, 4, 256], I32); r_cint = Res('cint')
    kb.dma('sp', lambda e: e.dma_start(out=cint[:], in_=I['cint']), writes=[r_cint])
    ones_bf = A("ones_bf", [128, 128], BF16); r_ones = Res('ones')
    kb.op('dve', lambda e: e.memset(ones_bf[:], 1.0), writes=[r_ones])
    ms_bf = A("ms_bf", [128, 128], BF16); r_msbf = Res('msbf')
    kb.op('dve', lambda e: e.memset(ms_bf[:], 1.0 / 256.0), writes=[r_msbf])
    mcol = A("mcol", [128, 48], F32); r_mcol = Res('mcol')
    mbc = A("mbc", [128, 4, 1024], F32); r_mbc = Res('mbc')
    lnbc = A("lnbc", [128, 4, 1024], F32); r_lnbc = Res('lnbc')
    PS = Ring(nc.alloc_psum_tensor, "ps", [128, 512], F32, 8, excl=True)

    def V(fn, r=(), w=()):
        return kb.op('dve', fn, r, w)

    def Sc(fn, r=(), w=()):
        return kb.op('act', fn, r, w)

    def G(fn, r=(), w=()):
        return kb.op('pool', fn, r, w)

    def T(fn, r=(), w=()):
        return kb.op('pe', fn, r, w)

    def DM(q, out, in_, r=(), w=()):
        return kb.dma(q, lambda e: e.dma_start(out=out, in_=in_), r, w)

    def ln_tm(st, xt, r_xt, gi, bi, small, r_small, r_gb=None):
        r_gb = r_gb or r_lnbc
        stats = small[:, 0:12].rearrange("p (a b) -> p a b", a=2)
        V(lambda e: e.bn_stats(out=small[:, 0:6], in_=xt[:, 0:512]), [r_xt], [r_small])
        V(lambda e: e.bn_stats(out=small[:, 6:12], in_=xt[:, 512:1024]), [r_xt], [r_small])
        V(lambda e: e.bn_aggr(out=small[:, 12:14], in_=stats), [r_small], [r_small])
        Sc(lambda e: e.activation(out=small[:, 14:15], in_=small[:, 13:14], func=ACT.Sqrt, bias=EPS), [r_small], [r_small])
        V(lambda e: e.reciprocal(out=small[:, 14:15], in_=small[:, 14:15]), [r_small], [r_small])
        V(lambda e: e.tensor_scalar(out=xt[:], in0=xt[:], scalar1=small[:, 12:13], scalar2=small[:, 14:15], op0=ALU.subtract, op1=ALU.mult), [r_xt, r_small], [r_xt])
        V(lambda e: e.tensor_tensor(out=xt[:], in0=xt[:], in1=gi, op=ALU.mult), [r_xt, r_gb], [r_xt])
        V(lambda e: e.tensor_tensor(out=xt[:], in0=xt[:], in1=bi, op=ALU.add), [r_xt, r_gb], [r_xt])

    groups = []
    if do_p:
        groups.append(dict(name='P', T=1024, L=256, nseq=4, samp=False, x=I['xp'], y=O['yp'], ci=0))
    if do_s:
        groups.append(dict(name='S', T=4096, L=4096, nseq=1, samp=True, x=I['xs'], y=O['ys'], ci=1))


    for g in groups:
        Tg, L, nseq, samp = g['T'], g['L'], g['nseq'], g['samp']
        NT = Tg // 128
        for l in range(nlayers):
            last = (l == nlayers - 1)
            kb.barrier()
            for es in _stage_ctx('M' in stages):
                AL = lambda n, s, d: es.enter_context(nc.sbuf_tensor(uq(n), s, d))
                ccol = AL("ccol", [128, 8], F32); scol = AL("scol", [128, 8], BF16); scb = AL("scb", [128, 8, 128], BF16)
                bcol = AL("bcol", [128, 48], F32); brow = AL("brow", [1, 6144], F32); browb = AL("browb", [1, 6144], BF16)
                r_m = Res('mtmp')
                DM('sp', ccol[:], I['cc'][g['ci']].rearrange("(kc p) -> p kc", p=128), w=[r_m])
                DM('sp', bcol[:], I['bmod'][l].rearrange("(j p) -> p j", p=128), w=[r_m])
                DM('sp', brow[:], I['bmod'][l:l + 1, :], w=[r_m])
                DM('sp', lnbc[:], I['lnv'][l].partition_broadcast(128), w=[r_lnbc])
                Sc(lambda e: e.activation(out=scol[:], in_=ccol[:], func=ACT.Silu), [r_m], [r_m])
                V(lambda e: e.tensor_copy(out=scb[:], in_=scol[:].unsqueeze(2).to_broadcast([128, 8, 128])), [r_m], [r_m])
                V(lambda e: e.tensor_copy(out=browb[:], in_=brow[:]), [r_m], [r_m])
                WR = Ring(AL, "wm", [128, 8, 512], BF16, 3)
                for b in range(12):
                    v = b // 2
                    half = b % 2
                    W, rW = WR.get()
                    DM('pool', W[:], I['wmod'][l, b], w=[rW])
                    if v >= 2:
                        ps, rps = PS.get()
                        for kc in range(8):
                            T(lambda e, kc=kc: e.matmul(ps[:, :], lhsT=scb[:, kc, :], rhs=W[:, kc, :], start=(kc == 0), stop=False), [r_m, rW], [rps])
                        T(lambda e: e.matmul(ps[:, :], lhsT=ones_bf[0:1, :], rhs=browb[0:1, b * 512:(b + 1) * 512], start=False, stop=True), [r_m, r_ones], [rps])
                        Sc(lambda e: e.activation(out=mbc[:, v - 2, half * 512:(half + 1) * 512], in_=ps[:, :], func=ACT.Identity,
                                                  bias=(1.0 if v == 4 else 0.0), scale=1.0), [rps], [r_mbc])
                    if v in (0, 1, 3, 4):
                        ps, rps = PS.get()
                        for jj in range(4):
                            for kc in range(8):
                                T(lambda e, kc=kc, jj=jj: e.matmul(ps[:, jj:jj + 1], lhsT=W[:, kc, jj * 128:(jj + 1) * 128], rhs=scol[:, kc:kc + 1],
                                                                   start=(kc == 0), stop=(kc == 7)), [r_m, rW], [rps])
                        V(lambda e: e.tensor_tensor(out=mcol[:, b * 4:(b + 1) * 4], in0=ps[:, 0:4], in1=bcol[:, b * 4:(b + 1) * 4], op=ALU.add), [rps, r_m], [r_mcol])
                        if v in (1, 4):
                            V(lambda e: e.tensor_scalar(out=mcol[:, b * 4:(b + 1) * 4], in0=mcol[:, b * 4:(b + 1) * 4], scalar1=1.0, scalar2=None, op0=ALU.add), [r_mcol], [r_mcol])
                kb.barrier()
            for es in _stage_ctx('A' in stages):
                AL = lambda n, s, d: es.enter_context(nc.sbuf_tensor(uq(n), s, d))
                hT = AL("hT", [128, 8, Tg], BF16)
                lnin = AL("lnin", [128, 2, 1024], F32); r_lnin = Res('lnin')
                if l == 0:
                    DM('sp', lnin[:], I['lnin'].partition_broadcast(128), w=[r_lnin])
                r_hT = [Res(f'hT{i}') for i in range(NT)]
                XR = Ring(AL, "xt", [128, 1024], F32, 3)
                SM = Ring(AL, "sm", [128, 16], F32, 3)
                for i in range(NT):
                    xt, rx = XR.get()
                    sm, rsm = SM.get()
                    rows = slice(i * 128, (i + 1) * 128)
                    if l == 0:
                        DM('sp', xt[:], g['x'][rows, :], w=[rx])
                        ln_tm(None, xt, rx, lnin[:, 0, :], lnin[:, 1, :], sm, rsm, r_lnin)
                        DM('sp', S['xres'][rows, :], xt[:], r=[rx], w=[xres_r[i]])
                    else:
                        DM('sp', xt[:], S['xres'][rows, :], r=[xres_r[i]], w=[rx])
                    for hb in range(2):
                        ps, rps = PS.get()
                        for k4 in range(4):
                            kc = hb * 4 + k4
                            T(lambda e, kc=kc, k4=k4: e.transpose(ps[:, k4 * 128:(k4 + 1) * 128], xt[:, kc * 128:(kc + 1) * 128], ident), [rx, r_cst], [rps])
                        for k4 in range(4):
                            kc = hb * 4 + k4
                            if k4 % 2 == 0:
                                Sc(lambda e, kc=kc, k4=k4: e.activation(out=hT[:, kc, rows], in_=ps[:, k4 * 128:(k4 + 1) * 128], func=ACT.Identity,
                                                                        scale=mcol[:, 8 + kc:9 + kc], bias=mcol[:, kc:kc + 1]), [rps, r_mcol], [r_hT[i]])
                            else:
                                V(lambda e, kc=kc, k4=k4: e.tensor_scalar(out=hT[:, kc, rows], in0=ps[:, k4 * 128:(k4 + 1) * 128], scalar1=mcol[:, 8 + kc:9 + kc],
                                                                          scalar2=mcol[:, kc:kc + 1], op0=ALU.mult, op1=ALU.add), [rps, r_mcol], [r_hT[i]])
                WR = Ring(AL, "wi", [128, 8, 512], BF16, 3)
                EV = Ring(AL, "ev", [128, 512], BF16, 4)
                EVF = Ring(AL, "evf", [128, 1024], F32, 2)
                aT = AL("aT", [33, Tg], BF16); r_aT = Res('aT')
                wa2 = AL("wa2", [33, 1024], BF16); r_wa2 = Res('wa2')
                DM('pool', wa2[:], I['wa2'][l], w=[r_wa2])
                G(lambda e: e.memset(aT[32:33, :], 1.0), [], [r_aT])
                NB4 = Tg // 512
                fm_jobs = [(0, [(h * 128, 128) for h in range(4)], 'qT', 0, ('scale', 128 ** -0.5)),
                           (1, [(h * 128, 128) for h in range(4)], 'kT', 0, None),
                           (2, [(h * 128, 128) for h in range(4)], 'ggT', 0, ('act', ACT.Silu)),
                           (3, [(h * 128, 128) for h in range(4)], 'ggT', 4, ('act', ACT.Silu)),
                           (4, [(0, 32)], 'aT', 0, None),
                           (5, [(h * 128, 128) for h in range(4)], 'chT', 0, None),
                           (6, [(h * 128, 128) for h in range(4)], 'chT', 4, None),
                           (7, [(h * 128, 128) for h in range(4)], 'cbT', 0, None),
                           (8, [(h * 128, 128) for h in range(4)], 'cbT', 4, None),
                           (9, [(h * 128, 128) for h in range(4)], 'ccT', 0, None),
                           (10, [(h * 128, 128) for h in range(4)], 'ccT', 4, None),
                           (11, [(h * 64, 64) for h in range(8)], 'qaT', 0, None),
                           (12, [(h * 64, 64) for h in range(8)], 'qaT', 8, None),
                           (13, [(h * 64, 64) for h in range(4)], 'kaT', 0, None)]
                for i6 in range(6):
                    fm_jobs.append((14 + i6, [(h * 128, 128) for h in range(4)], 'gT', i6 * 4, ('act', ACT.Sigmoid)))
                cnt = 0
                for (blk, chunks, dst, cbase, post) in (fm_jobs if 'b' not in stages else fm_jobs[:int(stages[stages.index('b') + 1:stages.index('b') + 3])]):
                    W, rW = WR.get()
                    DM('pool', W[:], I['win'][l, blk], w=[rW])
                    for tb in range(NB4):
                        cols = slice(tb * 512, (tb + 1) * 512)
                        rh = r_hT[tb * 4:(tb + 1) * 4]
                        for ci, (off, M) in enumerate(chunks):
                            ps, rps = PS.get()
                            for kc in range(8):
                                T(lambda e, kc=kc: e.matmul(ps[0:M, :], lhsT=W[:, kc, off:off + M], rhs=hT[:, kc, cols], start=(kc == 0), stop=(kc == 7)), [rW] + rh, [rps])
                            if dst == 'aT':
                                V(lambda e: e.tensor_copy(out=aT[0:32, cols], in_=ps[0:32, :]), [rps], [r_aT])
                                continue
                            ev, rev = EV.get()
                            cnt += 1
                            if post is None:
                                if cnt % 2 == 0:
                                    V(lambda e: e.tensor_copy(out=ev[0:M, :], in_=ps[0:M, :]), [rps], [rev])
                                else:
                                    Sc(lambda e: e.copy(out=ev[0:M, :], in_=ps[0:M, :]), [rps], [rev])
                            elif post[0] == 'scale':
                                Sc(lambda e: e.mul(out=ev[0:M, :], in_=ps[0:M, :], mul=post[1]), [rps], [rev])
                            else:
                                Sc(lambda e: e.activation(out=ev[0:M, :], in_=ps[0:M, :], func=post[1]), [rps], [rev])
                            DM('sp', S[dst][cbase + ci, :, cols], ev[0:M, :], r=[rev], w=[SR[dst]])
                for tt in range(NT if 'c' not in stages else 0):
                    rows = slice(tt * 128, (tt + 1) * 128)
                    evf, revf = EVF.get()
                    for hf in range(2):
                        ps, rps = PS.get()
                        T(lambda e: e.matmul(ps[:, :], lhsT=aT[0:33, rows], rhs=wa2[0:33, hf * 512:(hf + 1) * 512], start=True, stop=True), [r_aT, r_wa2], [rps])
                        Sc(lambda e: e.activation(out=evf[:, hf * 512:(hf + 1) * 512], in_=ps[:, :], func=ACT.Exp, scale=-1.0), [rps], [revf])
                    Sc(lambda e: e.activation(out=evf[:], in_=evf[:], func=ACT.Ln, bias=1.0), [revf], [revf])
                    G(lambda e: e.tensor_scalar(out=evf[:], in0=evf[:], scalar1=-1.0 / 16.0, scalar2=None, op0=ALU.mult), [revf], [revf])
                    DM('sp', S['la'][rows, :], evf[:], r=[revf], w=[SR['la']])
                for (blk, dst) in ([(20, 'ktm'), (21, 'vtm0'), (22, 'vtm1'), (23, 'kv')] if 'd' not in stages else []):
                    W, rW = WR.get()
                    DM('pool', W[:], I['win'][l, blk], w=[rW])
                    for tt in range(NT):
                        rows = slice(tt * 128, (tt + 1) * 128)
                        ps, rps = PS.get()
                        for kc in range(8):
                            T(lambda e, kc=kc: e.matmul(ps[:, :], lhsT=hT[:, kc, rows], rhs=W[:, kc, :], start=(kc == 0), stop=(kc == 7)), [rW, r_hT[tt]], [rps])
                        if dst == 'kv':
                            ev, rev = EV.get()
                            V(lambda e: e.tensor_copy(out=ev[:, 0:256], in_=ps[:, 256:512]), [rps], [rev])
                            DM('sp', S['avtm'][rows, :], ev[:, 0:256], r=[rev], w=[SR['avtm']])
                            if not samp:
                                evf, revf = EVF.get()
                                Sc(lambda e: e.copy(out=evf[:, 0:512], in_=ps[:, :]), [rps], [revf])
                                sq, t0 = divmod(tt * 128, L)
                                DM('sp', O['nk'][sq, l, t0:t0 + 128, :], evf[:, 0:256], r=[revf])
                                DM('sp', O['nv'][sq, l, t0:t0 + 128, :], evf[:, 256:512], r=[revf])
                        else:
                            ev, rev = EV.get()
                            if tt % 2 == 0:
                                V(lambda e: e.tensor_copy(out=ev[:], in_=ps[:, :]), [rps], [rev])
                            else:
                                Sc(lambda e: e.copy(out=ev[:], in_=ps[:, :]), [rps], [rev])
                            if dst == 'ktm':
                                DM('sp', S['ktm'][rows, :], ev[:], r=[rev], w=[SR['ktm']])
                            else:
                                c0 = 0 if dst == 'vtm0' else 512
                                DM('sp', S['vtm'][rows, c0:c0 + 512], ev[:], r=[rev], w=[SR['vtm']])
                kb.barrier()
            for es in _stage_ctx('G' in stages):
                AL = lambda n, s, d: es.enter_context(nc.sbuf_tensor(uq(n), s, d))
                St = AL("St", [128, 4, 256], F32); Sb = AL("Sb", [128, 4, 256], BF16)
                r_St = Res('St'); r_Sb = Res('Sb')
                LA = Ring(AL, "la", [128, 512], F32, 2)
                KT = Ring(AL, "kt", [128, 512], BF16, 2)
                VT = Ring(AL, "vt", [128, 1024], BF16, 2)
                QC = Ring(AL, "qc", [128, 4, 128], BF16, 2)
                KC = Ring(AL, "kc", [128, 4, 128], BF16, 2)
                ED = Ring(AL, "ed", [128, 512], F32, 2)
                KD = Ring(AL, "kd", [128, 512], BF16, 2)
                EB = Ring(AL, "eb", [128, 256], F32, 3)
                QE = Ring(AL, "qe", [128, 256], BF16, 3)
                AM = Ring(AL, "am", [128, 128], BF16, 3)
                OT = Ring(AL, "ot", [128, 8, 128], F32, 2)
                nch = L // 128
                for sq in range(nseq):
                    for d in (1, 0):
                        Md = Mf if d == 0 else Mb
                        Nd = Nf if d == 0 else Nb
                        endc = 127 if d == 0 else 0
                        odst = 'of' if d == 0 else 'ob'
                        if samp:
                            DM('sp', St[:], I['st'][l, d], w=[r_St])
                        else:
                            V(lambda e: e.memset(St[:], 0.0), [], [r_St])
                        Sc(lambda e: e.copy(out=Sb[:], in_=St[:]), [r_St], [r_Sb])
                        order = range(nch) if d == 0 else range(nch - 1, -1, -1)
                        for c in order:
                            t0 = sq * L + c * 128
                            tk = slice(t0, t0 + 128)
                            la, rla = LA.get(); kt, rkt = KT.get(); vt, rvt = VT.get(); qc, rqc = QC.get(); kc_, rkc = KC.get()
                            DM('sp', la[:], S['la'][tk, d * 512:(d + 1) * 512], r=[SR['la']], w=[rla])
                            DM('sp', kt[:], S['ktm'][tk, :], r=[SR['ktm']], w=[rkt])
                            DM('sp', vt[:], S['vtm'][tk, :], r=[SR['vtm']], w=[rvt])
                            DM('sp', qc[:], S['qT'][:, :, tk].rearrange("h p t -> p h t"), r=[SR['qT']], w=[rqc])
                            DM('sp', kc_[:], S['kT'][:, :, tk].rearrange("h p t -> p h t"), r=[SR['kT']], w=[rkc])
                            ps, rps = PS.get()
                            T(lambda e: e.matmul(ps[:, :], lhsT=Nd, rhs=la[:], start=True, stop=True), [r_cst, rla], [rps])
                            ed, red = ED.get(); kd, rkd = KD.get()
                            Sc(lambda e: e.activation(out=ed[:], in_=ps[:, :], func=ACT.Exp), [rps], [red])
                            G(lambda e: e.tensor_tensor(out=kd[:], in0=kt[:], in1=ed[:], op=ALU.mult), [rkt, red], [rkd])
                            ot, rot = OT.get()
                            for h in range(4):
                                hs = slice(h * 128, (h + 1) * 128)
                                psb, rpsb = PS.get()
                                T(lambda e: e.matmul(psb[:, 0:128], lhsT=la[:, hs], rhs=Md, start=True, stop=True), [r_cst, rla], [rpsb])
                                eb, reb = EB.get()
                                Sc(lambda e: e.activation(out=eb[:, 0:128], in_=psb[:, 0:128], func=ACT.Exp), [rpsb], [reb])
                                Sc(lambda e: e.activation(out=eb[:, 128:256], in_=psb[:, 0:128], func=ACT.Exp, scale=-1.0), [rpsb], [reb])
                                qe, rqe = QE.get()
                                V(lambda e: e.tensor_tensor(out=qe[:, 0:128], in0=qc[:, h, :], in1=eb[:, 0:128], op=ALU.mult), [rqc, reb], [rqe])
                                V(lambda e: e.tensor_tensor(out=qe[:, 128:256], in0=kc_[:, h, :], in1=eb[:, 128:256], op=ALU.mult), [rkc, reb], [rqe])
                                psa, rpsa = PS.get()
                                T(lambda e: e.matmul(psa[:, 0:128], lhsT=qe[:, 128:256], rhs=qe[:, 0:128], start=True, stop=True), [rqe], [rpsa])
                                am, ram = AM.get()
                                V(lambda e: e.tensor_tensor(out=am[:], in0=psa[:, 0:128], in1=Md, op=ALU.mult), [rpsa, r_cst], [ram])
                                pso, rpso = PS.get()
                                for dvc in range(2):
                                    vs = slice(h * 256 + dvc * 128, h * 256 + (dvc + 1) * 128)
                                    T(lambda e: e.matmul(pso[:, dvc * 128:(dvc + 1) * 128], lhsT=vt[:, vs], rhs=am[:], start=True, stop=False), [rvt, ram], [rpso])
                                    T(lambda e: e.matmul(pso[:, dvc * 128:(dvc + 1) * 128], lhsT=Sb[:, h, dvc * 128:(dvc + 1) * 128], rhs=qe[:, 0:128], start=False, stop=True), [r_Sb, rqe], [rpso])
                                Sc(lambda e: e.copy(out=ot[:, 2 * h:2 * h + 2, :], in_=pso[:, 0:256].rearrange("p (a b) -> p a b", a=2)), [rpso], [rot])
                                pss, rpss = PS.get()
                                T(lambda e: e.matmul(pss[:, 0:256], lhsT=kd[:, hs], rhs=vt[:, h * 256:(h + 1) * 256], start=True, stop=True), [rkd, rvt], [rpss])
                                V(lambda e: e.scalar_tensor_tensor(out=St[:, h, :], in0=St[:, h, :], scalar=eb[:, endc:endc + 1], in1=pss[:, 0:256], op0=ALU.mult, op1=ALU.add),
                                  [r_St, reb, rpss], [r_St])
                                G(lambda e: e.tensor_copy(out=Sb[:, h, :], in_=St[:, h, :]), [r_St], [r_Sb])
                            DM('sp', S[odst][:, :, tk].rearrange("j p t -> p j t"), ot[:], r=[rot], w=[SR[odst]])
                        if not samp:
                            DM('sp', O['nst'][sq, l, d].rearrange("h k v -> k h v"), St[:], r=[r_St])
                kb.barrier()
            for es in _stage_ctx('T' in stages):
                AL = lambda n, s, d: es.enter_context(nc.sbuf_tensor(uq(n), s, d))
                esk = AL("esk", [128, 16], F32); r_esk = Res('esk')
                DM('sp', esk[:], I['sink'][l].partition_broadcast(128), w=[r_esk])
                Sc(lambda e: e.activation(out=esk[:], in_=esk[:], func=ACT.Exp), [r_esk], [r_esk])
                nkb = Tg // 128
                VO = AL("VO", [128, nkb, 4, 128], BF16); r_VO = Res('VO')
                G(lambda e: e.memset(VO[:], 1.0), [], [r_VO])
                AVR = Ring(AL, "avr", [128, 256], BF16, 2)
                for m in range(nkb):
                    av, rav = AVR.get()
                    DM('sp', av[:], S['avtm'][m * 128:(m + 1) * 128, :], r=[SR['avtm']], w=[rav])
                    V(lambda e, m=m: e.tensor_copy(out=VO[:, m, :, 0:64], in_=av[:].rearrange("p (j d) -> p j d", j=4)), [rav, r_VO], [r_VO])
                RD = Ring(AL, "rd", [64, 4, 128], F32, 2)
                YC = Ring(AL, "yc", [64, 4, 128], BF16, 2)

                def normalize(pso, rpso, j, t0):
                    rd, rrd = RD.get()
                    yc, ryc = YC.get()
                    for hh in range(4):
                        h = j * 4 + hh
                        V(lambda e, hh=hh, h=h: e.tensor_scalar(out=rd[:, hh, :], in0=pso[64:128, hh * 128:(hh + 1) * 128], scalar1=esk[64:128, h:h + 1], scalar2=None,
                                                                op0=ALU.add), [rpso, r_esk], [rrd])
                    V(lambda e: e.reciprocal(out=rd[:], in_=rd[:]), [rrd], [rrd])
                    V(lambda e: e.tensor_tensor(out=yc[:], in0=pso[0:64, :].rearrange("p (a b) -> p a b", a=4), in1=rd[:], op=ALU.mult), [rpso, rrd], [ryc])
                    DM('sp', S['ycT'][j * 4:(j + 1) * 4, :, t0:t0 + 128].rearrange("h p t -> p h t"), yc[:], r=[ryc], w=[SR['ycT']])

                if not samp:
                    QA = Ring(AL, "qa", [64, 16, 256], BF16, 2)
                    KA = Ring(AL, "ka", [64, 4, 256], BF16, 2)
                    PT = Ring(AL, "pt", [128, 4, 256], BF16, 4)
                    for sq in range(nseq):
                        ts_ = slice(sq * L, (sq + 1) * L)
                        qa, rqa = QA.get(); ka, rka = KA.get()
                        DM('sp', qa[:], S['qaT'][:, :, ts_].rearrange("h p t -> p h t"), r=[SR['qaT']], w=[rqa])
                        DM('sp', ka[:], S['kaT'][:, :, ts_].rearrange("h p t -> p h t"), r=[SR['kaT']], w=[rka])
                        for j in range(4):
                            pts = []
                            for kbk in range(2):
                                pt, rpt = PT.get()
                                pts.append((pt, rpt))
                                for hh in range(4):
                                    ps, rps = PS.get()
                                    T(lambda e: e.matmul(ps[:, 0:256], lhsT=ka[:, j, kbk * 128:(kbk + 1) * 128], rhs=qa[:, j * 4 + hh, :], start=True, stop=True), [rka, rqa], [rps])
                                    Sc(lambda e: e.activation(out=pt[:, hh, :], in_=ps[:, 0:256], func=ACT.Exp, scale=0.125), [rps], [rpt])
                            for qt in range(2):
                                pso, rpso = PS.get()
                                for kbk in range(2):
                                    pt, rpt = pts[kbk]
                                    T(lambda e: e.matmul(pso[:, :].rearrange("p (a b) -> p a b", a=4), lhsT=VO[:, sq * 2 + kbk, j, :], rhs=pt[:, :, qt * 128:(qt + 1) * 128],
                                                         start=(kbk == 0), stop=(kbk == 1)), [r_VO, rpt], [rpso])
                                normalize(pso, rpso, j, sq * L + qt * 128)
                else:
                    with ExitStack() as es2:
                        AL2 = lambda n, s_, d: es2.enter_context(nc.sbuf_tensor(uq(n), s_, d))
                        cs = AL2("cs", [64, 2, 4096], F32); r_cs = Res('cs')
                        DM('sp', cs[:, 0, :], I['cosd'], w=[r_cs])
                        DM('sp', cs[:, 1, :], I['sind'], w=[r_cs])
                        RB = 256
                        XQ = Ring(AL2, "xq", [64, 16, RB], BF16, 2)
                        XO = Ring(AL2, "xo", [64, 16, RB], BF16, 2)
                        T1 = Ring(AL2, "t1", [64, 16, RB], F32, 1)
                        T2 = Ring(AL2, "t2", [64, 16, RB], F32, 1)
                        for (src, dstn, nh) in (('qaT', 'qrT', 16), ('kaT', 'krT', 4)):
                            for tb in range(Tg // RB):
                                cols = slice(tb * RB, (tb + 1) * RB)
                                xq, rxq = XQ.get(); xo, rxo = XO.get(); t1, rt1 = T1.get(); t2, rt2 = T2.get()
                                DM('sp', xq[:, 0:nh, :], S[src][:, :, cols].rearrange("h p t -> p h t"), r=[SR[src]], w=[rxq])
                                cb_lo = cs[0:32, 0, cols].unsqueeze(1).to_broadcast([32, nh, RB])
                                sb_lo = cs[0:32, 1, cols].unsqueeze(1).to_broadcast([32, nh, RB])
                                cb_hi = cs[32:64, 0, cols].unsqueeze(1).to_broadcast([32, nh, RB])
                                sb_hi = cs[32:64, 1, cols].unsqueeze(1).to_broadcast([32, nh, RB])
                                V(lambda e: e.tensor_tensor(out=t1[0:32, 0:nh, :], in0=xq[0:32, 0:nh, :], in1=cb_lo, op=ALU.mult), [rxq, r_cs], [rt1])
                                G(lambda e: e.tensor_tensor(out=t2[0:32, 0:nh, :], in0=xq[32:64, 0:nh, :], in1=sb_hi, op=ALU.mult), [rxq, r_cs], [rt2])
                                V(lambda e: e.tensor_tensor(out=xo[0:32, 0:nh, :], in0=t1[0:32, 0:nh, :], in1=t2[0:32, 0:nh, :], op=ALU.subtract), [rt1, rt2], [rxo])
                                V(lambda e: e.tensor_tensor(out=t1[32:64, 0:nh, :], in0=xq[0:32, 0:nh, :], in1=sb_lo, op=ALU.mult), [rxq, r_cs, rxo], [rt1])
                                G(lambda e: e.tensor_tensor(out=t2[32:64, 0:nh, :], in0=xq[32:64, 0:nh, :], in1=cb_hi, op=ALU.mult), [rxq, r_cs, rxo], [rt2])
                                V(lambda e: e.tensor_tensor(out=xo[32:64, 0:nh, :], in0=t1[32:64, 0:nh, :], in1=t2[32:64, 0:nh, :], op=ALU.add), [rt1, rt2], [rxo])
                                DM('sp', S[dstn][:, :, cols].rearrange("h p t -> p h t"), xo[:, 0:nh, :], r=[rxo], w=[SR[dstn]])
                        kb.barrier()
                    kc_t = AL("kc_t", [64, 4, 256], BF16); r_kct = Res('kct')
                    DM('pool', kc_t[:], I['ck'][l], w=[r_kct])
                    VOc = AL("VOc", [128, 2, 4, 128], BF16); r_VOc = Res('VOc')
                    G(lambda e: e.memset(VOc[:], 1.0), [], [r_VOc])
                    for cbk in range(2):
                        av, rav = AVR.get()
                        DM('pool', av[:], I['cv'][l, cbk * 128:(cbk + 1) * 128, :], w=[rav])
                        V(lambda e, cbk=cbk: e.tensor_copy(out=VOc[:, cbk, :, 0:64], in_=av[:].rearrange("p (j d) -> p j d", j=4)), [rav, r_VOc], [r_VOc])
                    QR = Ring(AL, "qr", [64, 4, 4096], BF16, 1)
                    KR = Ring(AL, "kr", [64, 4096], BF16, 2)
                    PT = Ring(AL, "pt", [128, 4, 384], BF16, 5)
                    PC = Ring(AL, "pc", [128, 2, 4, 512], BF16, 2)
                    for j in range(4):
                        qr, rqr = QR.get(); kr, rkr = KR.get()
                        DM('sp', qr[:], S['qrT'][j * 4:(j + 1) * 4].rearrange("h p t -> p h t"), r=[SR['qrT']], w=[rqr])
                        DM('sp', kr[:], S['krT'][j], r=[SR['krT']], w=[rkr])
                        pts = {}
                        pc = None

                        def pv(n):
                            pso, rpso = PS.get()
                            ms = [m for m in (n - 1, n, n + 1) if 0 <= m < nkb]
                            for ii, m in enumerate(ms):
                                pt, rpt = pts[m]
                                c0 = (n - m + 1) * 128
                                T(lambda e: e.matmul(pso[:, :].rearrange("p (a b) -> p a b", a=4), lhsT=VO[:, m, j, :], rhs=pt[:, :, c0:c0 + 128], start=(ii == 0), stop=False), [r_VO, rpt], [rpso])
                            pcc, rpcc = pc
                            for cbk in range(2):
                                c0 = (n % 4) * 128
                                T(lambda e: e.matmul(pso[:, :].rearrange("p (a b) -> p a b", a=4), lhsT=VOc[:, cbk, j, :], rhs=pcc[:, cbk, :, c0:c0 + 128], start=False, stop=(cbk == 1)), [r_VOc, rpcc], [rpso])
                            normalize(pso, rpso, j, n * 128)

                        pcs = {}
                        for m in range(nkb):
                            if m % 4 == 0:
                                pcn, rpcn = PC.get()
                                pcs[m // 4] = (pcn, rpcn)
                                for cbk in range(2):
                                    for hh in range(4):
                                        ps, rps = PS.get()
                                        T(lambda e: e.matmul(ps[:, :], lhsT=kc_t[:, j, cbk * 128:(cbk + 1) * 128], rhs=qr[:, hh, m * 128:m * 128 + 512], start=True, stop=True), [r_kct, rqr], [rps])
                                        Sc(lambda e: e.activation(out=pcn[:, cbk, hh, :], in_=ps[:, :], func=ACT.Exp, scale=0.125), [rps], [rpcn])
                            pt, rpt = PT.get()
                            pts[m] = (pt, rpt)
                            q0 = max(0, (m - 1) * 128)
                            q1 = min(Tg, (m + 2) * 128)
                            off = q0 - (m - 1) * 128
                            nq = q1 - q0
                            for hh in range(4):
                                ps, rps = PS.get()
                                T(lambda e: e.matmul(ps[:, 0:nq], lhsT=kr[:, m * 128:(m + 1) * 128], rhs=qr[:, hh, q0:q1], start=True, stop=True), [rkr, rqr], [rps])
                                Sc(lambda e: e.activation(out=pt[:, hh, off:off + nq], in_=ps[:, 0:nq], func=ACT.Exp, scale=0.125), [rps], [rpt])
                            if m > 0:
                                G(lambda e: e.tensor_tensor(out=pt[:, :, 0:128], in0=pt[:, :, 0:128], in1=Mf.unsqueeze(1).to_broadcast([128, 4, 128]), op=ALU.mult), [rpt, r_cst], [rpt])
                            if m < nkb - 1:
                                G(lambda e: e.tensor_tensor(out=pt[:, :, 256:384], in0=pt[:, :, 256:384], in1=Mb.unsqueeze(1).to_broadcast([128, 4, 128]), op=ALU.mult), [rpt, r_cst], [rpt])
                            if m >= 1:
                                pc = pcs[(m - 1) // 4]
                                pv(m - 1)
                        pc = pcs[(nkb - 1) // 4]
                        pv(nkb - 1)
                kb.barrier()
            for es in _stage_ctx('C' in stages):
                AL = lambda n, s, d: es.enter_context(nc.sbuf_tensor(uq(n), s, d))
                BS = 256
                wb01 = AL("wb01", [128, 3, 8, 1024], BF16); wo = AL("wo", [128, 8, 1024], BF16)
                r_w = Res('wC')
                for b3 in range(3):
                    DM('pool', wb01[:, b3], I['wb'][l, :, b3], w=[r_w])
                DM('pool', wo[:], I['wout'][l], w=[r_w])
                gcol = AL("gcol", [128, 2], F32); cw = AL("cw", [128, 3, 8], F32); r_gc = Res('gc')
                DM('sp', gcol[:], I['glag'][l], w=[r_gc])
                DM('sp', cw[:], I['convw'][l], w=[r_gc])
                OF = Ring(AL, "of", [128, 8, BS], F32, 1)
                OB = Ring(AL, "ob", [128, 8, BS], F32, 1)
                SQ = Ring(AL, "sq", [128, 8, BS], BF16, 1)
                GGr = Ring(AL, "gg", [128, 8, BS], BF16, 1)
                YA = Ring(AL, "ya", [128, 8, BS], BF16, 1)
                YB = Ring(AL, "yb", [128, 8, BS], BF16, 1)
                YCr = Ring(AL, "ycc", [128, 8, BS], BF16, 1)
                CH = Ring(AL, "ch", [128, 8, BS + 2], BF16, 1)
                CC = Ring(AL, "ccx", [128, 8, BS + 2], BF16, 1)
                CB = Ring(AL, "cbx", [128, 8, BS], BF16, 1)
                Z = Ring(AL, "z", [128, 8, BS + 2], F32, 1)
                ACC = Ring(AL, "acc", [128, BS], F32, 2)
                RS = Ring(AL, "rs", [128, BS], F32, 2)
                GT = Ring(AL, "gt", [128, 24, BS], BF16, 1)
                MG = Ring(AL, "mg", [128, 8, BS], BF16, 1)
                TA = Ring(AL, "ta", [128, BS], F32, 2)
                TB = Ring(AL, "tb", [128, BS], F32, 2)
                XR = Ring(AL, "xt", [128, 1024], F32, 1)
                MX = Ring(AL, "mx", [128, 1024], F32, 1)
                SM = Ring(AL, "sm", [128, 16], F32, 2)
                for bi in range(Tg // BS):
                    t0 = bi * BS
                    cols = slice(t0, t0 + BS)
                    sq0 = (t0 // L) * L
                    of, rof = OF.get(); ob, rob = OB.get(); gg, rgg = GGr.get(); ya, rya = YA.get(); sqt, rsq = SQ.get()
                    DM('sp', of[:], S['of'][:, :, cols].rearrange("j p t -> p j t"), r=[SR['of']], w=[rof])
                    DM('sp', ob[:], S['ob'][:, :, cols].rearrange("j p t -> p j t"), r=[SR['ob']], w=[rob])
                    DM('sp', gg[:], S['ggT'][:, :, cols].rearrange("j p t -> p j t"), r=[SR['ggT']], w=[rgg])
                    V(lambda e: e.tensor_tensor(out=of[:], in0=of[:], in1=ob[:], op=ALU.add), [rof, rob], [rof])
                    Sc(lambda e: e.activation(out=sqt[:], in_=of[:], func=ACT.Square), [rof], [rsq])
                    for h in range(4):
                        ps, rps = PS.get()
                        for dvc in range(2):
                            T(lambda e: e.matmul(ps[:, 0:BS], lhsT=ms_bf[:], rhs=sqt[:, 2 * h + dvc, :], start=(dvc == 0), stop=(dvc == 1)), [r_msbf, rsq], [rps])
                        rs, rrs = RS.get()
                        Sc(lambda e: e.activation(out=rs[:], in_=ps[:, 0:BS], func=ACT.Sqrt, bias=EPS), [rps], [rrs])
                        V(lambda e: e.reciprocal(out=rs[:], in_=rs[:]), [rrs], [rrs])
                        for dvc in range(2):
                            jx = 2 * h + dvc
                            V(lambda e: e.tensor_tensor(out=of[:, jx, :], in0=of[:, jx, :], in1=rs[:], op=ALU.mult), [rof, rrs], [rof])
                            V(lambda e: e.scalar_tensor_tensor(out=ya[:, jx, :], in0=of[:, jx, :], scalar=gcol[:, dvc:dvc + 1], in1=gg[:, jx, :], op0=ALU.mult, op1=ALU.mult),
                              [rof, r_gc, rgg], [rya])
                    chh, rch = CH.get(); ccx, rcc = CC.get(); cbx, rcb = CB.get(); z, rz = Z.get(); yb, ryb = YB.get()
                    lo = 1 if t0 == sq0 else 0
                    hi = 1 if t0 + BS == sq0 + L else 0
                    if lo:
                        G(lambda e: e.memset(chh[:, :, 0:1], 0.0), [], [rch])
                        G(lambda e: e.memset(ccx[:, :, 0:1], 0.0), [], [rcc])
                    if hi:
                        G(lambda e: e.memset(chh[:, :, BS + 1:BS + 2], 0.0), [], [rch])
                        G(lambda e: e.memset(ccx[:, :, BS + 1:BS + 2], 0.0), [], [rcc])
                    src = slice(t0 - 1 + lo, t0 + BS + 1 - hi)
                    dsl = slice(lo, BS + 2 - hi)
                    DM('sp', chh[:, :, dsl], S['chT'][:, :, src].rearrange("j p t -> p j t"), r=[SR['chT']], w=[rch])
                    DM('sp', ccx[:, :, dsl], S['ccT'][:, :, src].rearrange("j p t -> p j t"), r=[SR['ccT']], w=[rcc])
                    DM('sp', cbx[:], S['cbT'][:, :, cols].rearrange("j p t -> p j t"), r=[SR['cbT']], w=[rcb])
                    G(lambda e: e.tensor_tensor(out=z[:], in0=chh[:], in1=ccx[:], op=ALU.mult), [rch, rcc], [rz])
                    for jx in range(8):
                        acc, racc = ACC.get()
                        G(lambda e: e.tensor_scalar(out=acc[:], in0=z[:, jx, 0:BS], scalar1=cw[:, 0, jx:jx + 1], scalar2=None, op0=ALU.mult), [rz, r_gc], [racc])
                        V(lambda e: e.scalar_tensor_tensor(out=acc[:], in0=z[:, jx, 1:BS + 1], scalar=cw[:, 1, jx:jx + 1], in1=acc[:], op0=ALU.mult, op1=ALU.add), [rz, r_gc, racc], [racc])
                        V(lambda e: e.scalar_tensor_tensor(out=acc[:], in0=z[:, jx, 2:BS + 2], scalar=cw[:, 2, jx:jx + 1], in1=acc[:], op0=ALU.mult, op1=ALU.add), [rz, r_gc, racc], [racc])
                        G(lambda e: e.tensor_tensor(out=yb[:, jx, :], in0=acc[:], in1=cbx[:, jx, :], op=ALU.mult), [racc, rcb], [ryb])
                    ycc, rycc = YCr.get(); gt, rgt = GT.get(); mg, rmg = MG.get()
                    ycv = S['ycT'][:, :, cols].rearrange("(c two) p t -> two p c t", two=2)
                    DM('sp', ycc[0:64], ycv[0], r=[SR['ycT']], w=[rycc])
                    DM('sp', ycc[64:128], ycv[1], r=[SR['ycT']], w=[rycc])
                    DM('sp', gt[:], S['gT'][:, :, cols].rearrange("j p t -> p j t"), r=[SR['gT']], w=[rgt])
                    for dch in range(8):
                        dsl_ = slice(dch * 128, (dch + 1) * 128)
                        p0, rp0 = PS.get(); p1, rp1 = PS.get(); p2, rp2 = PS.get()
                        for kc in range(8):
                            T(lambda e, kc=kc: e.matmul(p0[:, 0:BS], lhsT=wb01[:, 0, kc, dsl_], rhs=ya[:, kc, :], start=(kc == 0), stop=(kc == 7)), [r_w, rya], [rp0])
                        for kc in range(8):
                            T(lambda e, kc=kc: e.matmul(p1[:, 0:BS], lhsT=wb01[:, 1, kc, dsl_], rhs=yb[:, kc, :], start=(kc == 0), stop=(kc == 7)), [r_w, ryb], [rp1])
                        for kc in range(8):
                            T(lambda e, kc=kc: e.matmul(p2[:, 0:BS], lhsT=wb01[:, 2, kc, dsl_], rhs=ycc[:, kc, :], start=(kc == 0), stop=(kc == 7)), [r_w, rycc], [rp2])
                        ta, rta = TA.get(); tb_, rtb = TB.get()
                        V(lambda e: e.tensor_tensor(out=ta[:], in0=p0[:, 0:BS], in1=gt[:, dch, :], op=ALU.mult), [rp0, rgt], [rta])
                        V(lambda e: e.tensor_tensor(out=tb_[:], in0=p1[:, 0:BS], in1=gt[:, 8 + dch, :], op=ALU.mult), [rp1, rgt], [rtb])
                        G(lambda e: e.tensor_tensor(out=ta[:], in0=ta[:], in1=tb_[:], op=ALU.add), [rta, rtb], [rta])
                        V(lambda e: e.tensor_tensor(out=tb_[:], in0=p2[:, 0:BS], in1=gt[:, 16 + dch, :], op=ALU.mult), [rp2, rgt, rta], [rtb])
                        G(lambda e: e.tensor_tensor(out=mg[:, dch, :], in0=ta[:], in1=tb_[:], op=ALU.add), [rta, rtb], [rmg])
                    for tt in range(BS // 128):
                        ti = (t0 // 128) + tt
                        rows = slice(ti * 128, (ti + 1) * 128)
                        xt, rx = XR.get(); mx, rmx = MX.get(); sm, rsm = SM.get()
                        DM('sp', xt[:], S['xres'][rows, :], r=[xres_r[ti]], w=[rx])
                        for hf in range(2):
                            ps, rps = PS.get()
                            for kc in range(8):
                                T(lambda e, kc=kc: e.matmul(ps[:, :], lhsT=mg[:, kc, tt * 128:(tt + 1) * 128], rhs=wo[:, kc, hf * 512:(hf + 1) * 512], start=(kc == 0), stop=(kc == 7)), [rmg, r_w], [rps])
                            V(lambda e: e.tensor_tensor(out=mx[:, hf * 512:(hf + 1) * 512], in0=ps[:, :], in1=mbc[:, 0, hf * 512:(hf + 1) * 512], op=ALU.mult), [rps, r_mbc], [rmx])
                        V(lambda e: e.scalar_tensor_tensor(out=xt[:], in0=xt[:], scalar=ALPHA, in1=mx[:], op0=ALU.mult, op1=ALU.add), [rx, rmx], [rx])
                        ln_tm(None, xt, rx, lnbc[:, 0, :], lnbc[:, 1, :], sm, rsm)
                        DM('sp', S['xres'][rows, :], xt[:], r=[rx], w=[xres_r[ti]])
                kb.barrier()
            for es in _stage_ctx('P' in stages):
                AL = lambda n, s, d: es.enter_context(nc.sbuf_tensor(uq(n), s, d))
                wpq = AL("wpq", [128, 8, 2048], BF16); r_wpq = Res('wpq')
                DM('pool', wpq[:], I['wpq'][l], w=[r_wpq])
                pk = AL("pk", [128, 16, 128], F32); r_pk = Res('pk')
                DM('sp', pk[:], I['pkeys'][l], w=[r_pk])
                XR = Ring(AL, "xt", [128, 1024], F32, 2)
                H2 = Ring(AL, "h2", [128, 1024], F32, 2)
                H2T = Ring(AL, "h2T", [128, 8, 128], BF16, 2)
                QT = Ring(AL, "qT", [128, 16, 128], F32, 1)
                SCr = Ring(AL, "scr", [128, 16, 128], F32, 1)
                WK = Ring(AL, "wk", [128, 256], F32, 2)
                TV = Ring(AL, "tv", [128, 2, 16], F32, 2)
                TI = Ring(AL, "ti", [128, 2, 16], I32, 2)
                TF = Ring(AL, "tf", [128, 2, 16], F32, 2)
                CD = Ring(AL, "cd", [128, 16, 16], F32, 2)
                CI = Ring(AL, "ci", [128, 16, 16], F32, 2)
                TS = Ring(AL, "ts", [128, 8, 16], F32, 2)
                EI = Ring(AL, "ei", [128, 128], F32, 2)
                EII = Ring(AL, "eii", [128, 128], I32, 2)
                GA = Ring(AL, "ga", [128, 8, 16], F32, 2)
                SM = Ring(AL, "sm", [128, 16], F32, 2)
                JK = Ring(AL, "jk", [128, 1024], F32, 2)
                UB = Ring(AL, "ub", [128, 4, 1024], F32, 3)
                AA = Ring(AL, "aa", [128, 128], F32, 2)
                ACCr = Ring(AL, "acc", [128, 1024], F32, 2)
                acc2 = AL("acc2", [128, 2, 1024], F32); racc2 = [Res('acc2a'), Res('acc2b')]
                for ti in range(NT):
                    rows = slice(ti * 128, (ti + 1) * 128)
                    xt, rx = XR.get(); h2, rh2 = H2.get(); h2T, rh2T = H2T.get()
                    DM('sp', xt[:], S['xres'][rows, :], r=[xres_r[ti]], w=[rx])
                    V(lambda e: e.tensor_tensor(out=h2[:], in0=xt[:], in1=mbc[:, 2, :], op=ALU.mult), [rx, r_mbc], [rh2])
                    V(lambda e: e.tensor_tensor(out=h2[:], in0=h2[:], in1=mbc[:, 1, :], op=ALU.add), [rh2, r_mbc], [rh2])
                    for hb in range(2):
                        ps, rps = PS.get()
                        for k4 in range(4):
                            kc = hb * 4 + k4
                            T(lambda e, kc=kc, k4=k4: e.transpose(ps[:, k4 * 128:(k4 + 1) * 128], h2[:, kc * 128:(kc + 1) * 128], ident), [rh2, r_cst], [rps])
                        Sc(lambda e: e.copy(out=h2T[:, hb * 4:(hb + 1) * 4, :], in_=ps[:, :].rearrange("p (a b) -> p a b", a=4)), [rps], [rh2T])
                    if not peer:
                        acc, racc = ACCr.get()
                        V(lambda e: e.memset(acc[:], 0.0), [], [racc])
                    else:
                        qT, rqT = QT.get()
                        for c4 in range(4):
                            ps, rps = PS.get()
                            for cc_ in range(4):
                                ch = c4 * 4 + cc_
                                for kc in range(8):
                                    T(lambda e, kc=kc, ch=ch, cc_=cc_: e.matmul(ps[:, cc_ * 128:(cc_ + 1) * 128], lhsT=wpq[:, kc, ch * 128:(ch + 1) * 128], rhs=h2T[:, kc, :],
                                                                                start=(kc == 0), stop=(kc == 7)), [r_wpq, rh2T], [rps])
                            Sc(lambda e: e.copy(out=qT[:, c4 * 4:(c4 + 1) * 4, :], in_=ps[:, :].rearrange("p (a b) -> p a b", a=4)), [rps], [rqT])
                        sc, rsc = SCr.get()
                        for c4 in range(4):
                            ps, rps = PS.get()
                            for cc_ in range(4):
                                ch = c4 * 4 + cc_
                                T(lambda e, ch=ch, cc_=cc_: e.matmul(ps[:, cc_ * 128:(cc_ + 1) * 128], lhsT=qT[:, ch, :], rhs=pk[:, ch, :], start=True, stop=True), [rqT, r_pk], [rps])
                            V(lambda e: e.tensor_copy(out=sc[:, c4 * 4:(c4 + 1) * 4, :], in_=ps[:, :].rearrange("p (a b) -> p a b", a=4)), [rps], [rsc])
                        sci = sc[:].bitcast(I32)
                        V(lambda e: e.tensor_tensor(out=sci, in0=sci, in1=cint[:, 0, 0:128].unsqueeze(1).to_broadcast([128, 16, 128]), op=ALU.bitwise_and), [rsc, r_cint], [rsc])
                        V(lambda e: e.tensor_tensor(out=sci, in0=sci, in1=cint[:, 1, 0:128].unsqueeze(1).to_broadcast([128, 16, 128]), op=ALU.bitwise_or), [rsc, r_cint], [rsc])
                        tsa, rts = TS.get(); ei, rei = EI.get()
                        for h in range(8):
                            tv, rtv = TV.get(); tix, rti = TI.get(); tf, rtf = TF.get()
                            for sd in range(2):
                                ch = h * 2 + sd
                                wk, rwk = WK.get()
                                V(lambda e: e.max(out=tv[:, sd, 0:8], in_=sc[:, ch, :]), [rsc], [rtv])
                                V(lambda e: e.match_replace(out=wk[:, 0:128], in_to_replace=tv[:, sd, 0:8], in_values=sc[:, ch, :], imm_value=-1e30), [rsc, rtv], [rwk])
                                V(lambda e: e.max(out=tv[:, sd, 8:16], in_=wk[:, 0:128]), [rwk], [rtv])
                            V(lambda e: e.tensor_tensor(out=tix[:], in0=tv[:].bitcast(I32), in1=cint[:, 2, 0:32].rearrange("p (a b) -> p a b", a=2), op=ALU.bitwise_and), [rtv, r_cint], [rti])
                            V(lambda e: e.tensor_copy(out=tf[:], in_=tix[:]), [rti], [rtf])
                            cd, rcd = CD.get(); ci_, rci = CI.get()
                            V(lambda e: e.tensor_tensor(out=cd[:], in0=tv[:, 0, :].unsqueeze(2).to_broadcast([128, 16, 16]), in1=tv[:, 1, :].unsqueeze(1).to_broadcast([128, 16, 16]), op=ALU.add), [rtv], [rcd])
                            V(lambda e: e.scalar_tensor_tensor(out=ci_[:], in0=tf[:, 0, :].unsqueeze(2).to_broadcast([128, 16, 16]), scalar=128.0, in1=tf[:, 1, :].unsqueeze(1).to_broadcast([128, 16, 16]),
                                                               op0=ALU.mult, op1=ALU.add), [rtf], [rci])
                            cdf = cd[:].rearrange("p a b -> p (a b)")
                            cdi = cdf.bitcast(I32)
                            V(lambda e: e.tensor_tensor(out=cdi, in0=cdi, in1=cint[:, 3, :], op=ALU.bitwise_and), [rcd, r_cint], [rcd])
                            V(lambda e: e.tensor_tensor(out=cdi, in0=cdi, in1=cint[:, 1, :], op=ALU.bitwise_or), [rcd, r_cint], [rcd])
                            cif = ci_[:].rearrange("p a b -> p (a b)")
                            wk, rwk = WK.get()
                            V(lambda e: e.max(out=tsa[:, h, 0:8], in_=cdf), [rcd], [rts])
                            V(lambda e: e.match_replace(out=wk[:], in_to_replace=tsa[:, h, 0:8], in_values=cdf, imm_value=-1e30), [rcd, rts], [rwk])
                            V(lambda e: e.max(out=tsa[:, h, 8:16], in_=wk[:]), [rwk], [rts])
                            for k in range(16):
                                jk, rjk = JK.get()
                                V(lambda e, k=k: e.scalar_tensor_tensor(out=jk[:, 0:256], in0=cdf, scalar=tsa[:, h, k:k + 1], in1=cif, op0=ALU.is_equal, op1=ALU.mult,
                                                                        accum_out=ei[:, h * 16 + k:h * 16 + k + 1]), [rcd, rci, rts], ([rei] if k in (0, 15) else []))
                        ga, rga = GA.get(); sm, rsm = SM.get()
                        V(lambda e: e.tensor_tensor(out=ga[:], in0=tsa[:], in1=tsa[:, :, 0:1].to_broadcast([128, 8, 16]), op=ALU.subtract), [rts], [rga])
                        Sc(lambda e: e.activation(out=ga[:], in_=ga[:], func=ACT.Exp), [rga], [rga])
                        V(lambda e: e.tensor_reduce(out=sm[:, 0:8], in_=ga[:], axis=AX.X, op=ALU.add), [rga], [rsm])
                        V(lambda e: e.reciprocal(out=sm[:, 0:8], in_=sm[:, 0:8]), [rsm], [rsm])
                        V(lambda e: e.tensor_tensor(out=ga[:], in0=ga[:], in1=sm[:, 0:8].unsqueeze(2).to_broadcast([128, 8, 16]), op=ALU.mult), [rga, rsm], [rga])
                        eii, reii = EII.get()
                        V(lambda e: e.tensor_scalar(out=ei[:], in0=ei[:], scalar1=16383.0, scalar2=0.0, op0=ALU.min, op1=ALU.max), [rei], [rei])
                        V(lambda e: e.tensor_copy(out=eii[:], in_=ei[:]), [rei], [reii])
                        if peer == 'idx':
                            gi = 0 if g['name'] == 'P' else 8
                            tix_ = gi + min(ti, 7)
                            DM('sp', O['dbg_ei'][l, tix_], ei[:], r=[rei])
                            DM('sp', O['dbg_eii'][l, tix_], eii[:], r=[reii])
                            DM('sp', O['dbg_ts'][l, tix_], tsa[:].rearrange("p a b -> p (a b)"), r=[rts])
                            acc, racc = ACCr.get()
                            V(lambda e: e.memset(acc[:], 0.0), [], [racc])
                        aa, raa = AA.get()
                        for j4 in range(32 if peer is True else 0):
                            ub, rub = UB.get()
                            kb.dma_group('pool', [(lambda e, jx=j4 * 4 + q4, q4=q4: e.indirect_dma_start(out=ub[:, q4, :], out_offset=None, in_=I[f'pu{l}'],
                                                   in_offset=bass.IndirectOffsetOnAxis(ap=eii[:, jx:jx + 1], axis=0))) for q4 in range(4)], [reii], [rub])
                            for q4 in range(4):
                                jx = j4 * 4 + q4
                                jk, rjk = JK.get()
                                V(lambda e, jx=jx, q4=q4: e.scalar_tensor_tensor(out=jk[:], in0=h2[:], scalar=1.0, in1=ub[:, q4, :], op0=ALU.mult, op1=ALU.mult, accum_out=aa[:, jx:jx + 1]),
                                  [rh2, rub], ([raa] if jx in (0, 127) else []))
                        if peer is True:
                            Sc(lambda e: e.activation(out=aa[:], in_=aa[:], func=ACT.Gelu), [raa], [raa])
                            V(lambda e: e.tensor_tensor(out=aa[:], in0=aa[:], in1=ga[:].rearrange("p a b -> p (a b)"), op=ALU.mult), [raa, rga], [raa])
                            acc, racc = ACCr.get()
                        for j4 in range(32 if peer is True else 0):
                            ub, rub = UB.get()
                            kb.dma_group('pool', [(lambda e, jx=j4 * 4 + q4, q4=q4: e.indirect_dma_start(out=ub[:, q4, :], out_offset=None, in_=I[f'pv{l}'],
                                                   in_offset=bass.IndirectOffsetOnAxis(ap=eii[:, jx:jx + 1], axis=0))) for q4 in range(4)], [reii], [rub])
                            for q4 in range(4):
                                jx = j4 * 4 + q4
                                a2 = acc2[:, jx % 2, :]
                                ra2 = racc2[jx % 2]
                                if jx < 2:
                                    V(lambda e, jx=jx, q4=q4: e.tensor_scalar(out=a2, in0=ub[:, q4, :], scalar1=aa[:, jx:jx + 1], scalar2=None, op0=ALU.mult), [rub, raa], [ra2])
                                else:
                                    V(lambda e, jx=jx, q4=q4: e.scalar_tensor_tensor(out=a2, in0=ub[:, q4, :], scalar=aa[:, jx:jx + 1], in1=a2, op0=ALU.mult, op1=ALU.add), [rub, raa, ra2], [ra2])
                        if peer is True:
                            V(lambda e: e.tensor_tensor(out=acc[:], in0=acc2[:, 0, :], in1=acc2[:, 1, :], op=ALU.add), racc2, [racc])
                    sm, rsm = SM.get()
                    V(lambda e: e.tensor_tensor(out=acc[:], in0=acc[:], in1=mbc[:, 3, :], op=ALU.mult), [racc, r_mbc], [racc])
                    V(lambda e: e.scalar_tensor_tensor(out=xt[:], in0=xt[:], scalar=ALPHA, in1=acc[:], op0=ALU.mult, op1=ALU.add), [rx, racc], [rx])
                    ln_tm(None, xt, rx, lnbc[:, 2, :], lnbc[:, 3, :], sm, rsm)
                    if last:
                        DM('sp', g['y'][rows, :], xt[:], r=[rx])
                    else:
                        DM('sp', S['xres'][rows, :], xt[:], r=[rx], w=[xres_r[ti]])
                kb.barrier()
    if dbg:
        kb.barrier()
        for name in dbg:
            DM('sp', O['dbg_' + name], S[name], r=[SR[name]])
    kb.finish()
    ctx_nc.__exit__(None, None, None)
    print("instr counts", kb.etot, "waits", kb.nwaits, kb.wcnt)
    return nc


_CACHE = {}


def kernel(**inputs):
    inp = {k: np.asarray(v) for k, v in inputs.items()}
    sh = prep_shared(inp)
    in_maps = [prep_core(inp, sh, c) for c in range(8)]
    if 'nc' not in _CACHE:
        _CACHE['nc'] = build(do_p=True, do_s=True, nlayers=2)
    nc = _CACHE['nc']
    res = run_bass_kernel_spmd(nc, in_maps, core_ids=list(range(8)))
    R = res.results
    y_prompt = np.concatenate([np.asarray(R[c]['yp']).reshape(4, 256, 1024) for c in range(8)], 0).astype(np.float32)
    ys = []
    for b in range(4):
        ys.append(np.concatenate([np.asarray(R[2 * b]['ys'])[0:2048], np.asarray(R[2 * b + 1]['ys'])[2048:4096]], 0))
    y_sample = np.stack(ys, 0).astype(np.float32)
    nk = np.concatenate([np.asarray(R[c]['nk']).reshape(4, 2, 256, 4, 64) for c in range(8)], 0).astype(np.float32)
    nv = np.concatenate([np.asarray(R[c]['nv']).reshape(4, 2, 256, 4, 64) for c in range(8)], 0).astype(np.float32)
    nst = np.concatenate([np.asarray(R[c]['nst']) for c in range(8)], 0).astype(np.float32)
    return (y_prompt, y_sample, nk, nv, nst)
```

```python
from contextlib import ExitStack
import numpy as np
import concourse.bass as bass
import concourse.mybir as mybir
from concourse.bass_utils import run_bass_kernel_spmd

ACT = mybir.ActivationFunctionType
ALU = mybir.AluOpType
AX = mybir.AxisListType
F32 = mybir.dt.float32
BF16 = mybir.dt.bfloat16
I32 = mybir.dt.int32
U32 = mybir.dt.uint32

EPOCH = 12000
NDSEM = {'sp': 24, 'pool': 24, 'act': 8}


class Res:
    __slots__ = ('name', 'w', 'r', 'excl', 'dram')

    def __init__(self, name='r', excl=False, dram=False):
        self.name = name
        self.w = {}
        self.r = {}
        self.excl = excl
        self.dram = dram


class KB:
    def __init__(self, nc):
        self.nc = nc
        self.engs = {'pe': nc.tensor, 'dve': nc.vector, 'act': nc.scalar, 'pool': nc.gpsimd, 'sp': nc.sync}
        self.esem = {}
        self.ecnt = {n: 0 for n in self.engs}
        self.etot = {n: 0 for n in self.engs}
        self.eepoch = {n: 0 for n in self.engs}
        self.known = {n: {} for n in self.engs}
        self.dsems = {}
        self.dval = {}
        self.dnext = {q: 0 for q in NDSEM}
        self.dpool = {}
        for q, n in NDSEM.items():
            self.dpool[q] = []
            for i in range(n):
                s = nc.alloc_semaphore(f'd_{q}_{i}')
                self.dsems[(q, i)] = s
                self.dval[(q, i)] = 0
                self.dpool[q].append((q, i))
        self.nwaits = 0
        self.wcnt = {}
        self.out_events = []

    def _esem(self, eng, ep):
        k = (eng, ep)
        if k not in self.esem:
            self.esem[k] = self.nc.alloc_semaphore(f'e_{eng}_{ep}')
        return self.esem[k]

    def _deps(self, reads, writes):
        deps = {}
        for r in reads:
            for k, v in r.w.items():
                if deps.get(k, 0) < v:
                    deps[k] = v
        for w in writes:
            for k, v in w.w.items():
                if deps.get(k, 0) < v:
                    deps[k] = v
            for k, v in w.r.items():
                if deps.get(k, 0) < v:
                    deps[k] = v
        return deps

    def _wait(self, eng, deps):
        h = self.engs[eng]
        kn = self.known[eng]
        for k, v in deps.items():
            if kn.get(k, 0) >= v:
                continue
            if k[0] == 'E':
                sem = self._esem(k[1], k[2])
            else:
                sem = self.dsems[(k[1], k[2])]
            h.wait_ge(sem, v)
            self.nwaits += 1
            self.wcnt[eng] = self.wcnt.get(eng, 0) + 1
            kn[k] = v

    def _record(self, key, val, reads, writes):
        for r in reads:
            if r.r.get(key, 0) < val:
                r.r[key] = val
        for w in writes:
            w.w = {key: val}
            w.r = {}

    def op(self, eng, fn, reads=(), writes=()):
        if any(r.excl for r in reads):
            writes = list(writes) + [r for r in reads if r.excl]
            reads = [r for r in reads if not r.excl]
        deps = self._deps(reads, writes)
        if eng == 'pe':
            for k in [k for k in deps if k[0] == 'E' and k[1] == 'pe']:
                del deps[k]
        else:
            cur_ep = self.eepoch[eng]
            for k in [k for k in deps if k[0] == 'E' and k[1] == eng]:
                if k[2] < cur_ep or (self.ecnt[eng] + 1 - deps[k]) >= 2:
                    del deps[k]
        self._wait(eng, deps)
        ins = fn(self.engs[eng])
        if self.ecnt[eng] >= EPOCH:
            self.eepoch[eng] += 1
            self.ecnt[eng] = 0
        self.ecnt[eng] += 1
        self.etot[eng] += 1
        ep = self.eepoch[eng]
        ins.then_inc(self._esem(eng, ep), 1)
        self._record(('E', eng, ep), self.ecnt[eng], reads, writes)
        return ins

    def dma(self, q, fn, reads=(), writes=(), is_output=False):
        deps = self._deps(reads, [w for w in writes if not w.dram])
        for w in writes:
            if w.dram:
                for k, v in w.r.items():
                    if deps.get(k, 0) < v:
                        deps[k] = v
        self._wait(q, deps)
        slot = self.dpool[q][self.dnext[q]]
        self.dnext[q] = (self.dnext[q] + 1) % len(self.dpool[q])
        prev = self.dval[slot]
        key = ('D', slot[0], slot[1])
        if prev > 0:
            self._wait(q, {key: prev})
        ins = fn(self.engs[q])
        ins.then_inc(self.dsems[slot], 16)
        self.dval[slot] = prev + 16
        self._record(key, prev + 16, reads, [w for w in writes if not w.dram])
        for w in writes:
            if w.dram:
                w.w[key] = prev + 16
                w.r = {}
        if is_output:
            self.out_events.append((key, prev + 16))
        return ins

    def dma_group(self, q, fns, reads=(), writes=()):
        deps = self._deps(reads, writes)
        self._wait(q, deps)
        slot = self.dpool[q][self.dnext[q]]
        self.dnext[q] = (self.dnext[q] + 1) % len(self.dpool[q])
        prev = self.dval[slot]
        key = ('D', slot[0], slot[1])
        if prev > 0:
            self._wait(q, {key: prev})
        for fn in fns:
            ins = fn(self.engs[q])
            ins.then_inc(self.dsems[slot], 16)
        self.dval[slot] = prev + 16 * len(fns)
        self._record(key, self.dval[slot], reads, writes)

    def finish(self):
        deps = {}
        for slot, v in self.dval.items():
            if v > 0:
                deps[('D', slot[0], slot[1])] = v
        self._wait('sp', deps)
        deps = {}
        for eng in self.engs:
            if eng == 'sp':
                continue
            if self.etot[eng] > 0:
                deps[('E', eng, self.eepoch[eng])] = self.ecnt[eng]
        self._wait('sp', deps)


def _barrier(self):
    ev = {}
    for eng in self.engs:
        if self.etot[eng] > 0:
            ev[('E', eng, self.eepoch[eng])] = self.ecnt[eng]
    for slot, v in self.dval.items():
        if v > 0:
            ev[('D', slot[0], slot[1])] = v
    for eng in self.engs:
        self._wait(eng, dict(ev))


KB.barrier = _barrier


class Ring:
    def __init__(self, alloc, name, shape, dt, n, excl=False):
        self.t = [alloc(f"{name}{i}", shape, dt) for i in range(n)]
        self.r = [Res(f"{name}{i}", excl) for i in range(n)]
        self.i = 0

    def get(self):
        k = self.i
        self.i = (self.i + 1) % len(self.t)
        return self.t[k], self.r[k]


D = 1024
O_GQ, O_GK, O_GV, O_GG, O_GA, O_CH, O_CB, O_CC, O_AQ, O_AK, O_AV, O_MG = (
    0, 512, 1024, 2048, 3072, 3104, 4128, 5152, 6176, 7200, 7456, 7712)
ALPHA = float(4 ** 0.25)
EPS = 1e-6
NBLK = 24
NOMI = False
PERM = np.concatenate([np.arange(0, 64, 2), np.arange(1, 64, 2)])
GRID_W = 64


def _blockify(W, cols):
    out = np.zeros((128, 8, 512), np.float32)
    n = len(cols)
    out[:, :, :n] = W[:, cols].reshape(8, 128, n).transpose(1, 0, 2)
    return out


def win_blocks(w):
    r = np.arange
    blks = [r(O_GQ, O_GQ + 512), r(O_GK, O_GK + 512), r(O_GG, O_GG + 512), r(O_GG + 512, O_GG + 1024),
            r(O_GA, O_GA + 32),
            r(O_CH, O_CH + 512), r(O_CH + 512, O_CH + 1024), r(O_CB, O_CB + 512), r(O_CB + 512, O_CB + 1024),
            r(O_CC, O_CC + 512), r(O_CC + 512, O_CC + 1024)]
    for half in range(2):
        blks.append(np.concatenate([O_AQ + (half * 8 + h) * 64 + PERM for h in range(8)]))
    blks.append(np.concatenate([O_AK + j * 64 + PERM for j in range(4)]))
    for i in range(6):
        blks.append(r(O_MG + i * 512, O_MG + (i + 1) * 512))
    blks += [r(O_GK, O_GK + 512), r(O_GV, O_GV + 512), r(O_GV + 512, O_GV + 1024),
             np.concatenate([r(O_AK, O_AK + 256), r(O_AV, O_AV + 256)])]
    assert len(blks) == NBLK
    return np.stack([_blockify(w, c) for c in blks])


def rope_tables(T):
    rows = T // GRID_W
    row = np.repeat(np.arange(rows, dtype=np.float32), GRID_W)
    col = np.tile(np.arange(GRID_W, dtype=np.float32), rows)
    half = 32
    inv = (np.float32(10000.0) ** (-np.arange(0, half, 2, dtype=np.float32) / np.float32(half))).astype(np.float32)
    ang = np.concatenate([row[:, None] * inv, col[:, None] * inv], -1).astype(np.float32)
    c = np.cos(ang).astype(np.float32).T
    s = np.sin(ang).astype(np.float32).T
    return np.ascontiguousarray(np.concatenate([c, c], 0)), np.ascontiguousarray(np.concatenate([s, s], 0))


def make_consts():
    i = np.arange(128)
    ident = np.eye(128, dtype=np.float32)
    Mf = (i[:, None] <= i[None, :]).astype(np.float32)
    Mb = (i[:, None] >= i[None, :]).astype(np.float32)
    Nf = (i[:, None] > i[None, :]).astype(np.float32)
    Nb = (i[:, None] < i[None, :]).astype(np.float32)
    return np.ascontiguousarray(np.stack([ident, Mf, Mb, Nf, Nb], 1))


def prep_shared(inp):
    f = lambda a: np.ascontiguousarray(np.asarray(a, dtype=np.float32))
    sh = {}
    sh['win'] = f(np.stack([win_blocks(np.asarray(inp['w_in'][l])) for l in range(2)]))
    wm = np.asarray(inp['w_mod'])
    sh['wmod'] = f(np.stack([np.stack([_blockify(wm[l], np.arange(b * 512, (b + 1) * 512)) for b in range(12)]) for l in range(2)]))
    sh['bmod'] = f(inp['b_mod'])
    wa = np.zeros((2, 33, 1024), np.float32)
    for l in range(2):
        wa[l, 0:16, 0:512] = inp['w_gla_a2'][l, 0]
        wa[l, 16:32, 512:1024] = inp['w_gla_a2'][l, 1]
        wa[l, 32, 0:512] = inp['b_gla_a'][l, 0]
        wa[l, 32, 512:1024] = inp['b_gla_a'][l, 1]
    sh['wa2'] = wa
    sh['glag'] = f(np.asarray(inp['gla_norm_g']).reshape(2, 2, 128).transpose(0, 2, 1))
    sh['convw'] = f(np.asarray(inp['conv_w']).reshape(2, 3, 8, 128).transpose(0, 3, 1, 2))
    sh['sink'] = f(inp['attn_sink'])
    wb = np.asarray(inp['w_branch'])
    sh['wb'] = f(wb.reshape(2, 3, 8, 128, 1024).transpose(0, 3, 1, 2, 4))
    sh['wout'] = f(np.asarray(inp['w_out']).reshape(2, 8, 128, 1024).transpose(0, 2, 1, 3))
    sh['wpq'] = f(np.asarray(inp['w_pq']).reshape(2, 8, 128, 2048).transpose(0, 2, 1, 3))
    pk = np.asarray(inp['peer_keys'])
    sh['pkeys'] = f(pk.reshape(2, 16, 128, 128).transpose(0, 3, 1, 2))
    for l in range(2):
        sh[f'pu{l}'] = f(np.asarray(inp['peer_u'])[l])
        sh[f'pv{l}'] = f(np.asarray(inp['peer_v'])[l])
    lnv = np.stack([np.asarray(inp[k]) for k in ('ln1_g', 'ln1_b', 'ln2_g', 'ln2_b')], 1)
    sh['lnv'] = f(lnv)
    sh['lnin'] = f(np.stack([np.asarray(inp['ln_in_g']), np.asarray(inp['ln_in_b'])]))
    sh['consts'] = make_consts()
    ci = np.zeros((128, 4, 256), np.int32)
    ci[:, 0, :] = -128
    ci[:, 1, :] = np.arange(256, dtype=np.int32)[None, :]
    ci[:, 2, :] = 127
    ci[:, 3, :] = -256
    sh['cint'] = ci
    c, s = rope_tables(4096)
    sh['cosd'] = c
    sh['sind'] = s
    return sh


def prep_core(inp, sh, core):
    f = lambda a: np.ascontiguousarray(np.asarray(a, dtype=np.float32))
    b = core // 2
    m = dict(sh)
    m['xp'] = f(np.asarray(inp['x_prompt'])[core * 4:(core + 1) * 4].reshape(1024, 1024))
    m['xs'] = f(np.asarray(inp['x_sample'])[b])
    ck = np.asarray(inp['cache_k'])[b]
    m['ck'] = f(ck[:, :, :, PERM].transpose(0, 3, 2, 1))
    m['cv'] = f(np.asarray(inp['cache_v'])[b].reshape(2, 256, 256))
    st = np.asarray(inp['state_gla'])[b]
    m['st'] = f(st.transpose(0, 1, 3, 2, 4))
    m['cc'] = f(np.stack([np.asarray(inp['c_ctx']), np.asarray(inp['c'])[b]]))
    return m


def _stage_ctx(flag):
    if flag:
        es = ExitStack()
        yield es
        es.close()


def build(do_p=True, do_s=True, nlayers=2, dbg=None, peer=True, TM=4096, stages='MAGTCP'):
    nc = bass.Bass("TRN2", target_bir_lowering=False)
    kb = KB(nc)
    ctx_nc = nc.allow_non_contiguous_dma(reason="small strided loads")
    ctx_nc.__enter__()

    def din(name, shape, dt=F32):
        return nc.dram_tensor(name, list(shape), dt, kind="ExternalInput").ap()

    def dout(name, shape, dt=F32):
        return nc.dram_tensor(name, list(shape), dt, kind="ExternalOutput").ap()

    def dscr(name, shape, dt):
        return nc.dram_tensor(name, list(shape), dt, kind="Internal").ap()

    NEXP = 16384 if peer else 128
    I = {}
    for name, shape in [('xp', (1024, 1024)), ('xs', (4096, 1024)), ('ck', (2, 64, 4, 256)), ('cv', (2, 256, 256)),
                        ('st', (2, 2, 128, 4, 256)), ('cc', (2, 1024)), ('win', (2, NBLK, 128, 8, 512)),
                        ('wmod', (2, 12, 128, 8, 512)), ('bmod', (2, 6144)), ('wa2', (2, 33, 1024)),
                        ('glag', (2, 128, 2)), ('convw', (2, 128, 3, 8)), ('sink', (2, 16)),
                        ('wb', (2, 128, 3, 8, 1024)), ('wout', (2, 128, 8, 1024)),
                        ('wpq', (2, 128, 8, 2048)), ('pkeys', (2, 128, 16, 128)), ('pu0', (NEXP, 1024)), ('pu1', (NEXP, 1024)),
                        ('pv0', (NEXP, 1024)), ('pv1', (NEXP, 1024)), ('lnv', (2, 4, 1024)), ('lnin', (2, 1024)),
                        ('consts', (128, 5, 128)), ('cosd', (64, 4096)), ('sind', (64, 4096))]:
        I[name] = din(name, shape)
    I['cint'] = din('cint', (128, 4, 256), I32)
    O = {'yp': dout('yp', (1024, 1024)), 'ys': dout('ys', (4096, 1024)),
         'nk': dout('nk', (4, 2, 256, 256)), 'nv': dout('nv', (4, 2, 256, 256)),
         'nst': dout('nst', (4, 2, 2, 4, 128, 256))}
    if peer == 'idx':
        O['dbg_ei'] = dout('dbg_ei', (2, 16, 128, 128))
        O['dbg_eii'] = dout('dbg_eii', (2, 16, 128, 128), I32)
        O['dbg_ts'] = dout('dbg_ts', (2, 16, 128, 128))
    S = {}
    for name, shape, dt in [('qT', (4, 128, TM), BF16), ('kT', (4, 128, TM), BF16), ('ggT', (8, 128, TM), BF16),
                            ('chT', (8, 128, TM), BF16), ('cbT', (8, 128, TM), BF16), ('ccT', (8, 128, TM), BF16),
                            ('qaT', (16, 64, TM), BF16), ('kaT', (4, 64, TM), BF16), ('gT', (24, 128, TM), BF16),
                            ('ktm', (TM, 512), BF16), ('vtm', (TM, 1024), BF16), ('avtm', (TM, 256), BF16),
                            ('la', (TM, 1024), F32), ('of', (8, 128, TM), F32), ('ob', (8, 128, TM), F32),
                            ('ycT', (16, 64, TM), BF16), ('qrT', (16, 64, TM), BF16), ('krT', (4, 64, TM), BF16),
                            ('xres', (TM, 1024), F32)]:
        S[name] = dscr('s_' + name, shape, dt)
    SR = {k: Res(k, dram=True) for k in S}
    xres_r = [Res(f'xres{i}') for i in range(TM // 128)]
    if dbg:
        for name in dbg:
            O['dbg_' + name] = dout('dbg_' + name, S[name].shape, S[name].dtype)

    _uid = [0]

    def uq(n):
        _uid[0] += 1
        return f"t{_uid[0]}_{n}"

    A = lambda n, s, d: nc.alloc_sbuf_tensor(uq(n), s, d)
    cst = A("cst", [128, 5, 128], F32); r_cst = Res('cst')
    kb.dma('sp', lambda e: e.dma_start(out=cst[:], in_=I['consts']), writes=[r_cst])
    ident = cst[:, 0, :]; Mf = cst[:, 1, :]; Mb = cst[:, 2, :]; Nf = cst[:, 3, :]; Nb = cst[:, 4, :]
    cint = A("cint", [128, 4, 256], I32); r_cint = Res('cint')
    kb.dma('sp', lambda e: e.dma_start(out=cint[:], in_=I['cint']), writes=[r_cint])
    ones_bf = A("ones_bf", [128, 128], BF16); r_ones = Res('ones')
    kb.op('dve', lambda e: e.memset(ones_bf[:], 1.0), writes=[r_ones])
    ms_bf = A("ms_bf", [128, 128], BF16); r_msbf = Res('msbf')
    kb.op('dve', lambda e: e.memset(ms_bf[:], 1.0 / 256.0), writes=[r_msbf])
    mcol = A("mcol", [128, 48], F32); r_mcol = Res('mcol')
    mbc = A("mbc", [128, 4, 1024], F32); r_mbc = Res('mbc')
    lnbc = A("lnbc", [128, 4, 1024], F32); r_lnbc = Res('lnbc')
    PS = Ring(nc.alloc_psum_tensor, "ps", [128, 512], F32, 6, excl=True)
    pacc = [nc.alloc_psum_tensor(f"pacc{i}", [128, 512], F32) for i in range(2)]
    r_pacc = [Res(f"pacc{i}", True) for i in range(2)]

    def V(fn, r=(), w=()):
        return kb.op('dve', fn, r, w)

    def Sc(fn, r=(), w=()):
        return kb.op('act', fn, r, w)

    def G(fn, r=(), w=()):
        return kb.op('pool', fn, r, w)

    def T(fn, r=(), w=()):
        return kb.op('pe', fn, r, w)

    def DM(q, out, in_, r=(), w=()):
        return kb.dma(q, lambda e: e.dma_start(out=out, in_=in_), r, w)

    def ln_tm(st, xt, r_xt, gi, bi, small, r_small, r_gb=None):
        r_gb = r_gb or r_lnbc
        stats = small[:, 0:12].rearrange("p (a b) -> p a b", a=2)
        V(lambda e: e.bn_stats(out=small[:, 0:6], in_=xt[:, 0:512]), [r_xt], [r_small])
        V(lambda e: e.bn_stats(out=small[:, 6:12], in_=xt[:, 512:1024]), [r_xt], [r_small])
        V(lambda e: e.bn_aggr(out=small[:, 12:14], in_=stats), [r_small], [r_small])
        Sc(lambda e: e.activation(out=small[:, 14:15], in_=small[:, 13:14], func=ACT.Sqrt, bias=EPS), [r_small], [r_small])
        V(lambda e: e.reciprocal(out=small[:, 14:15], in_=small[:, 14:15]), [r_small], [r_small])
        V(lambda e: e.tensor_scalar(out=xt[:], in0=xt[:], scalar1=small[:, 12:13], scalar2=small[:, 14:15], op0=ALU.subtract, op1=ALU.mult), [r_xt, r_small], [r_xt])
        V(lambda e: e.tensor_tensor(out=xt[:], in0=xt[:], in1=gi, op=ALU.mult), [r_xt, r_gb], [r_xt])
        V(lambda e: e.tensor_tensor(out=xt[:], in0=xt[:], in1=bi, op=ALU.add), [r_xt, r_gb], [r_xt])

    TBL = {}
    r_tab = Res('tab')
    if peer is True:
        for l_ in range(2):
            TBL[f'uv{l_}'] = dscr(f'tb_uv{l_}', (16384, 2048), BF16)
            for hv, k_ in enumerate((f'pu{l_}', f'pv{l_}')):
                for i8 in range(8):
                    DM('pool', TBL[f'uv{l_}'][i8 * 2048:(i8 + 1) * 2048, hv * 1024:(hv + 1) * 1024], I[k_][i8 * 2048:(i8 + 1) * 2048, :], w=[r_tab])
    groups = []
    if do_p:
        groups.append(dict(name='P', T=1024, L=256, nseq=4, samp=False, x=I['xp'], y=O['yp'], ci=0))
    if do_s:
        groups.append(dict(name='S', T=4096, L=4096, nseq=1, samp=True, x=I['xs'], y=O['ys'], ci=1))


    for g in groups:
        Tg, L, nseq, samp = g['T'], g['L'], g['nseq'], g['samp']
        NT = Tg // 128
        for l in range(nlayers):
            last = (l == nlayers - 1)
            kb.barrier()
            for es in _stage_ctx('M' in stages):
                AL = lambda n, s, d: es.enter_context(nc.sbuf_tensor(uq(n), s, d))
                ccol = AL("ccol", [128, 8], F32); scol = AL("scol", [128, 8], BF16); scb = AL("scb", [128, 8, 128], BF16)
                bcol = AL("bcol", [128, 48], F32); brow = AL("brow", [1, 6144], F32); browb = AL("browb", [1, 6144], BF16)
                r_m = Res('mtmp')
                DM('sp', ccol[:], I['cc'][g['ci']].rearrange("(kc p) -> p kc", p=128), w=[r_m])
                DM('sp', bcol[:], I['bmod'][l].rearrange("(j p) -> p j", p=128), w=[r_m])
                DM('sp', brow[:], I['bmod'][l:l + 1, :], w=[r_m])
                DM('sp', lnbc[:], I['lnv'][l].partition_broadcast(128), w=[r_lnbc])
                Sc(lambda e: e.activation(out=scol[:], in_=ccol[:], func=ACT.Silu), [r_m], [r_m])
                V(lambda e: e.tensor_copy(out=scb[:], in_=scol[:].unsqueeze(2).to_broadcast([128, 8, 128])), [r_m], [r_m])
                V(lambda e: e.tensor_copy(out=browb[:], in_=brow[:]), [r_m], [r_m])
                WR = Ring(AL, "wm", [128, 8, 512], BF16, 3)
                for b in range(12):
                    v = b // 2
                    half = b % 2
                    W, rW = WR.get()
                    DM('pool', W[:], I['wmod'][l, b], w=[rW])
                    if v >= 2:
                        ps, rps = PS.get()
                        for kc in range(8):
                            T(lambda e, kc=kc: e.matmul(ps[:, :], lhsT=scb[:, kc, :], rhs=W[:, kc, :], start=(kc == 0), stop=False), [r_m, rW], [rps])
                        T(lambda e: e.matmul(ps[:, :], lhsT=ones_bf[0:1, :], rhs=browb[0:1, b * 512:(b + 1) * 512], start=False, stop=True), [r_m, r_ones], [rps])
                        Sc(lambda e: e.activation(out=mbc[:, v - 2, half * 512:(half + 1) * 512], in_=ps[:, :], func=ACT.Identity,
                                                  bias=(1.0 if v == 4 else 0.0), scale=1.0), [rps], [r_mbc])
                    if v in (0, 1, 3, 4):
                        ps, rps = PS.get()
                        for jj in range(4):
                            for kc in range(8):
                                T(lambda e, kc=kc, jj=jj: e.matmul(ps[:, jj:jj + 1], lhsT=W[:, kc, jj * 128:(jj + 1) * 128], rhs=scol[:, kc:kc + 1],
                                                                   start=(kc == 0), stop=(kc == 7)), [r_m, rW], [rps])
                        V(lambda e: e.tensor_tensor(out=mcol[:, b * 4:(b + 1) * 4], in0=ps[:, 0:4], in1=bcol[:, b * 4:(b + 1) * 4], op=ALU.add), [rps, r_m], [r_mcol])
                        if v in (1, 4):
                            V(lambda e: e.tensor_scalar(out=mcol[:, b * 4:(b + 1) * 4], in0=mcol[:, b * 4:(b + 1) * 4], scalar1=1.0, scalar2=None, op0=ALU.add), [r_mcol], [r_mcol])
                kb.barrier()
            for es in _stage_ctx('A' in stages):
                AL = lambda n, s, d: es.enter_context(nc.sbuf_tensor(uq(n), s, d))
                hT = AL("hT", [128, 8, Tg], BF16)
                lnin = AL("lnin", [128, 2, 1024], F32); r_lnin = Res('lnin')
                if l == 0:
                    DM('sp', lnin[:], I['lnin'].partition_broadcast(128), w=[r_lnin])
                r_hT = [Res(f'hT{i}') for i in range(NT)]
                XR = Ring(AL, "xt", [128, 1024], F32, 3)
                SM = Ring(AL, "sm", [128, 16], F32, 3)
                for i in range(NT):
                    xt, rx = XR.get()
                    sm, rsm = SM.get()
                    rows = slice(i * 128, (i + 1) * 128)
                    if l == 0:
                        DM('sp', xt[:], g['x'][rows, :], w=[rx])
                        ln_tm(None, xt, rx, lnin[:, 0, :], lnin[:, 1, :], sm, rsm, r_lnin)
                        DM('sp', S['xres'][rows, :], xt[:], r=[rx], w=[xres_r[i]])
                    else:
                        DM('sp', xt[:], S['xres'][rows, :], r=[xres_r[i]], w=[rx])
                    for hb in range(2):
                        ps, rps = PS.get()
                        for k4 in range(4):
                            kc = hb * 4 + k4
                            T(lambda e, kc=kc, k4=k4: e.transpose(ps[:, k4 * 128:(k4 + 1) * 128], xt[:, kc * 128:(kc + 1) * 128], ident), [rx, r_cst], [rps])
                        for k4 in range(4):
                            kc = hb * 4 + k4
                            if k4 % 2 == 0:
                                Sc(lambda e, kc=kc, k4=k4: e.activation(out=hT[:, kc, rows], in_=ps[:, k4 * 128:(k4 + 1) * 128], func=ACT.Identity,
                                                                        scale=mcol[:, 8 + kc:9 + kc], bias=mcol[:, kc:kc + 1]), [rps, r_mcol], [r_hT[i]])
                            else:
                                V(lambda e, kc=kc, k4=k4: e.tensor_scalar(out=hT[:, kc, rows], in0=ps[:, k4 * 128:(k4 + 1) * 128], scalar1=mcol[:, 8 + kc:9 + kc],
                                                                          scalar2=mcol[:, kc:kc + 1], op0=ALU.mult, op1=ALU.add), [rps, r_mcol], [r_hT[i]])
                WR = Ring(AL, "wi", [128, 8, 512], BF16, 3)
                EV = Ring(AL, "ev", [128, 512], BF16, 4)
                EVF = Ring(AL, "evf", [128, 1024], F32, 2)
                aT = AL("aT", [33, Tg], BF16); r_aT = Res('aT')
                wa2 = AL("wa2", [33, 1024], BF16); r_wa2 = Res('wa2')
                DM('pool', wa2[:], I['wa2'][l], w=[r_wa2])
                G(lambda e: e.memset(aT[32:33, :], 1.0), [], [r_aT])
                NB4 = Tg // 512
                fm_jobs = [(0, [(h * 128, 128) for h in range(4)], 'qT', 0, ('scale', 128 ** -0.5)),
                           (1, [(h * 128, 128) for h in range(4)], 'kT', 0, None),
                           (2, [(h * 128, 128) for h in range(4)], 'ggT', 0, ('act', ACT.Silu)),
                           (3, [(h * 128, 128) for h in range(4)], 'ggT', 4, ('act', ACT.Silu)),
                           (4, [(0, 32)], 'aT', 0, None),
                           (5, [(h * 128, 128) for h in range(4)], 'chT', 0, None),
                           (6, [(h * 128, 128) for h in range(4)], 'chT', 4, None),
                           (7, [(h * 128, 128) for h in range(4)], 'cbT', 0, None),
                           (8, [(h * 128, 128) for h in range(4)], 'cbT', 4, None),
                           (9, [(h * 128, 128) for h in range(4)], 'ccT', 0, None),
                           (10, [(h * 128, 128) for h in range(4)], 'ccT', 4, None),
                           (11, [(h * 64, 64) for h in range(8)], 'qaT', 0, None),
                           (12, [(h * 64, 64) for h in range(8)], 'qaT', 8, None),
                           (13, [(h * 64, 64) for h in range(4)], 'kaT', 0, None)]
                for i6 in range(6):
                    fm_jobs.append((14 + i6, [(h * 128, 128) for h in range(4)], 'gT', i6 * 4, ('act', ACT.Sigmoid)))
                cnt = 0
                for (blk, chunks, dst, cbase, post) in (fm_jobs if 'b' not in stages else fm_jobs[:int(stages[stages.index('b') + 1:stages.index('b') + 3])]):
                    W, rW = WR.get()
                    DM('pool', W[:], I['win'][l, blk], w=[rW])
                    for tb in range(NB4):
                        cols = slice(tb * 512, (tb + 1) * 512)
                        rh = r_hT[tb * 4:(tb + 1) * 4]
                        for ci, (off, M) in enumerate(chunks):
                            ps, rps = PS.get()
                            for kc in range(8):
                                T(lambda e, kc=kc: e.matmul(ps[0:M, :], lhsT=W[:, kc, off:off + M], rhs=hT[:, kc, cols], start=(kc == 0), stop=(kc == 7)), [rW] + rh, [rps])
                            if dst == 'aT':
                                V(lambda e: e.tensor_copy(out=aT[0:32, cols], in_=ps[0:32, :]), [rps], [r_aT])
                                continue
                            ev, rev = EV.get()
                            cnt += 1
                            if post is None:
                                if cnt % 2 == 0:
                                    V(lambda e: e.tensor_copy(out=ev[0:M, :], in_=ps[0:M, :]), [rps], [rev])
                                else:
                                    Sc(lambda e: e.copy(out=ev[0:M, :], in_=ps[0:M, :]), [rps], [rev])
                            elif post[0] == 'scale':
                                Sc(lambda e: e.mul(out=ev[0:M, :], in_=ps[0:M, :], mul=post[1]), [rps], [rev])
                            else:
                                Sc(lambda e: e.activation(out=ev[0:M, :], in_=ps[0:M, :], func=post[1]), [rps], [rev])
                            DM('sp', S[dst][cbase + ci, :, cols], ev[0:M, :], r=[rev], w=[SR[dst]])
                for tt in range(NT if 'c' not in stages else 0):
                    rows = slice(tt * 128, (tt + 1) * 128)
                    evf, revf = EVF.get()
                    for hf in range(2):
                        ps, rps = PS.get()
                        T(lambda e: e.matmul(ps[:, :], lhsT=aT[0:33, rows], rhs=wa2[0:33, hf * 512:(hf + 1) * 512], start=True, stop=True), [r_aT, r_wa2], [rps])
                        Sc(lambda e: e.activation(out=evf[:, hf * 512:(hf + 1) * 512], in_=ps[:, :], func=ACT.Exp, scale=-1.0), [rps], [revf])
                    Sc(lambda e: e.activation(out=evf[:], in_=evf[:], func=ACT.Ln, bias=1.0), [revf], [revf])
                    G(lambda e: e.tensor_scalar(out=evf[:], in0=evf[:], scalar1=-1.0 / 16.0, scalar2=None, op0=ALU.mult), [revf], [revf])
                    DM('sp', S['la'][rows, :], evf[:], r=[revf], w=[SR['la']])
                for (blk, dst) in ([(20, 'ktm'), (21, 'vtm0'), (22, 'vtm1'), (23, 'kv')] if 'd' not in stages else []):
                    W, rW = WR.get()
                    DM('pool', W[:], I['win'][l, blk], w=[rW])
                    for tt in range(NT):
                        rows = slice(tt * 128, (tt + 1) * 128)
                        ps, rps = PS.get()
                        for kc in range(8):
                            T(lambda e, kc=kc: e.matmul(ps[:, :], lhsT=hT[:, kc, rows], rhs=W[:, kc, :], start=(kc == 0), stop=(kc == 7)), [rW, r_hT[tt]], [rps])
                        if dst == 'kv':
                            ev, rev = EV.get()
                            V(lambda e: e.tensor_copy(out=ev[:, 0:256], in_=ps[:, 256:512]), [rps], [rev])
                            DM('sp', S['avtm'][rows, :], ev[:, 0:256], r=[rev], w=[SR['avtm']])
                            if not samp:
                                evf, revf = EVF.get()
                                Sc(lambda e: e.copy(out=evf[:, 0:512], in_=ps[:, :]), [rps], [revf])
                                sq, t0 = divmod(tt * 128, L)
                                DM('sp', O['nk'][sq, l, t0:t0 + 128, :], evf[:, 0:256], r=[revf])
                                DM('sp', O['nv'][sq, l, t0:t0 + 128, :], evf[:, 256:512], r=[revf])
                        else:
                            ev, rev = EV.get()
                            if tt % 2 == 0:
                                V(lambda e: e.tensor_copy(out=ev[:], in_=ps[:, :]), [rps], [rev])
                            else:
                                Sc(lambda e: e.copy(out=ev[:], in_=ps[:, :]), [rps], [rev])
                            if dst == 'ktm':
                                DM('sp', S['ktm'][rows, :], ev[:], r=[rev], w=[SR['ktm']])
                            else:
                                c0 = 0 if dst == 'vtm0' else 512
                                DM('sp', S['vtm'][rows, c0:c0 + 512], ev[:], r=[rev], w=[SR['vtm']])
                kb.barrier()
            for es in _stage_ctx('G' in stages):
                AL = lambda n, s, d: es.enter_context(nc.sbuf_tensor(uq(n), s, d))
                St = AL("St", [128, 4, 256], F32); Sb = AL("Sb", [128, 4, 256], BF16)
                r_St = Res('St'); r_Sb = Res('Sb')
                LA = Ring(AL, "la", [128, 512], F32, 2)
                KT = Ring(AL, "kt", [128, 512], BF16, 2)
                VT = Ring(AL, "vt", [128, 1024], BF16, 2)
                QC = Ring(AL, "qc", [128, 4, 128], BF16, 2)
                KC = Ring(AL, "kc", [128, 4, 128], BF16, 2)
                ED = Ring(AL, "ed", [128, 512], F32, 2)
                KD = Ring(AL, "kd", [128, 512], BF16, 2)
                EB = Ring(AL, "eb", [128, 256], F32, 3)
                QE = Ring(AL, "qe", [128, 256], BF16, 3)
                AM = Ring(AL, "am", [128, 128], BF16, 3)
                OT = Ring(AL, "ot", [128, 8, 128], F32, 2)
                nch = L // 128
                for sq in range(nseq):
                    for d in (1, 0):
                        Md = Mf if d == 0 else Mb
                        Nd = Nf if d == 0 else Nb
                        endc = 127 if d == 0 else 0
                        odst = 'of' if d == 0 else 'ob'
                        if samp:
                            DM('sp', St[:], I['st'][l, d], w=[r_St])
                        else:
                            V(lambda e: e.memset(St[:], 0.0), [], [r_St])
                        Sc(lambda e: e.copy(out=Sb[:], in_=St[:]), [r_St], [r_Sb])
                        order = range(nch) if d == 0 else range(nch - 1, -1, -1)
                        for c in order:
                            t0 = sq * L + c * 128
                            tk = slice(t0, t0 + 128)
                            la, rla = LA.get(); kt, rkt = KT.get(); vt, rvt = VT.get(); qc, rqc = QC.get(); kc_, rkc = KC.get()
                            DM('sp', la[:], S['la'][tk, d * 512:(d + 1) * 512], r=[SR['la']], w=[rla])
                            DM('sp', kt[:], S['ktm'][tk, :], r=[SR['ktm']], w=[rkt])
                            DM('sp', vt[:], S['vtm'][tk, :], r=[SR['vtm']], w=[rvt])
                            DM('sp', qc[:], S['qT'][:, :, tk].rearrange("h p t -> p h t"), r=[SR['qT']], w=[rqc])
                            DM('sp', kc_[:], S['kT'][:, :, tk].rearrange("h p t -> p h t"), r=[SR['kT']], w=[rkc])
                            ps, rps = PS.get()
                            T(lambda e: e.matmul(ps[:, :], lhsT=Nd, rhs=la[:], start=True, stop=True), [r_cst, rla], [rps])
                            ed, red = ED.get(); kd, rkd = KD.get()
                            Sc(lambda e: e.activation(out=ed[:], in_=ps[:, :], func=ACT.Exp), [rps], [red])
                            G(lambda e: e.tensor_tensor(out=kd[:], in0=kt[:], in1=ed[:], op=ALU.mult), [rkt, red], [rkd])
                            ot, rot = OT.get()
                            for h in range(4):
                                hs = slice(h * 128, (h + 1) * 128)
                                psb, rpsb = PS.get()
                                T(lambda e: e.matmul(psb[:, 0:128], lhsT=la[:, hs], rhs=Md, start=True, stop=True), [r_cst, rla], [rpsb])
                                eb, reb = EB.get()
                                Sc(lambda e: e.activation(out=eb[:, 0:128], in_=psb[:, 0:128], func=ACT.Exp), [rpsb], [reb])
                                Sc(lambda e: e.activation(out=eb[:, 128:256], in_=psb[:, 0:128], func=ACT.Exp, scale=-1.0), [rpsb], [reb])
                                qe, rqe = QE.get()
                                V(lambda e: e.tensor_tensor(out=qe[:, 0:128], in0=qc[:, h, :], in1=eb[:, 0:128], op=ALU.mult), [rqc, reb], [rqe])
                                V(lambda e: e.tensor_tensor(out=qe[:, 128:256], in0=kc_[:, h, :], in1=eb[:, 128:256], op=ALU.mult), [rkc, reb], [rqe])
                                psa, rpsa = PS.get()
                                T(lambda e: e.matmul(psa[:, 0:128], lhsT=qe[:, 128:256], rhs=qe[:, 0:128], start=True, stop=True), [rqe], [rpsa])
                                am, ram = AM.get()
                                V(lambda e: e.tensor_tensor(out=am[:], in0=psa[:, 0:128], in1=Md, op=ALU.mult), [rpsa, r_cst], [ram])
                                pso, rpso = PS.get()
                                for dvc in range(2):
                                    vs = slice(h * 256 + dvc * 128, h * 256 + (dvc + 1) * 128)
                                    T(lambda e: e.matmul(pso[:, dvc * 128:(dvc + 1) * 128], lhsT=vt[:, vs], rhs=am[:], start=True, stop=False), [rvt, ram], [rpso])
                                    T(lambda e: e.matmul(pso[:, dvc * 128:(dvc + 1) * 128], lhsT=Sb[:, h, dvc * 128:(dvc + 1) * 128], rhs=qe[:, 0:128], start=False, stop=True), [r_Sb, rqe], [rpso])
                                Sc(lambda e: e.copy(out=ot[:, 2 * h:2 * h + 2, :], in_=pso[:, 0:256].rearrange("p (a b) -> p a b", a=2)), [rpso], [rot])
                                pss, rpss = PS.get()
                                T(lambda e: e.matmul(pss[:, 0:256], lhsT=kd[:, hs], rhs=vt[:, h * 256:(h + 1) * 256], start=True, stop=True), [rkd, rvt], [rpss])
                                V(lambda e: e.scalar_tensor_tensor(out=St[:, h, :], in0=St[:, h, :], scalar=eb[:, endc:endc + 1], in1=pss[:, 0:256], op0=ALU.mult, op1=ALU.add),
                                  [r_St, reb, rpss], [r_St])
                                G(lambda e: e.tensor_copy(out=Sb[:, h, :], in_=St[:, h, :]), [r_St], [r_Sb])
                            DM('sp', S[odst][:, :, tk].rearrange("j p t -> p j t"), ot[:], r=[rot], w=[SR[odst]])
                        if not samp:
                            DM('sp', O['nst'][sq, l, d].rearrange("h k v -> k h v"), St[:], r=[r_St])
                kb.barrier()
            for es in _stage_ctx('T' in stages):
                AL = lambda n, s, d: es.enter_context(nc.sbuf_tensor(uq(n), s, d))
                esk = AL("esk", [128, 16], F32); r_esk = Res('esk')
                DM('sp', esk[:], I['sink'][l].partition_broadcast(128), w=[r_esk])
                Sc(lambda e: e.activation(out=esk[:], in_=esk[:], func=ACT.Exp), [r_esk], [r_esk])
                nkb = Tg // 128
                VO = AL("VO", [128, nkb, 4, 128], BF16); r_VO = Res('VO')
                G(lambda e: e.memset(VO[:], 1.0), [], [r_VO])
                AVR = Ring(AL, "avr", [128, 256], BF16, 2)
                for m in range(nkb):
                    av, rav = AVR.get()
                    DM('sp', av[:], S['avtm'][m * 128:(m + 1) * 128, :], r=[SR['avtm']], w=[rav])
                    V(lambda e, m=m: e.tensor_copy(out=VO[:, m, :, 0:64], in_=av[:].rearrange("p (j d) -> p j d", j=4)), [rav, r_VO], [r_VO])
                RD = Ring(AL, "rd", [64, 4, 128], F32, 2)
                YC = Ring(AL, "yc", [64, 4, 128], BF16, 2)

                def normalize(pso, rpso, j, t0):
                    rd, rrd = RD.get()
                    yc, ryc = YC.get()
                    for hh in range(4):
                        h = j * 4 + hh
                        V(lambda e, hh=hh, h=h: e.tensor_scalar(out=rd[:, hh, :], in0=pso[64:128, hh * 128:(hh + 1) * 128], scalar1=esk[64:128, h:h + 1], scalar2=None,
                                                                op0=ALU.add), [rpso, r_esk], [rrd])
                    V(lambda e: e.reciprocal(out=rd[:], in_=rd[:]), [rrd], [rrd])
                    V(lambda e: e.tensor_tensor(out=yc[:], in0=pso[0:64, :].rearrange("p (a b) -> p a b", a=4), in1=rd[:], op=ALU.mult), [rpso, rrd], [ryc])
                    DM('sp', S['ycT'][j * 4:(j + 1) * 4, :, t0:t0 + 128].rearrange("h p t -> p h t"), yc[:], r=[ryc], w=[SR['ycT']])

                if not samp:
                    QA = Ring(AL, "qa", [64, 16, 256], BF16, 2)
                    KA = Ring(AL, "ka", [64, 4, 256], BF16, 2)
                    PT = Ring(AL, "pt", [128, 4, 256], BF16, 4)
                    for sq in range(nseq):
                        ts_ = slice(sq * L, (sq + 1) * L)
                        qa, rqa = QA.get(); ka, rka = KA.get()
                        DM('sp', qa[:], S['qaT'][:, :, ts_].rearrange("h p t -> p h t"), r=[SR['qaT']], w=[rqa])
                        DM('sp', ka[:], S['kaT'][:, :, ts_].rearrange("h p t -> p h t"), r=[SR['kaT']], w=[rka])
                        for j in range(4):
                            pts = []
                            for kbk in range(2):
                                pt, rpt = PT.get()
                                pts.append((pt, rpt))
                                for hh in range(4):
                                    ps, rps = PS.get()
                                    T(lambda e: e.matmul(ps[:, 0:256], lhsT=ka[:, j, kbk * 128:(kbk + 1) * 128], rhs=qa[:, j * 4 + hh, :], start=True, stop=True), [rka, rqa], [rps])
                                    Sc(lambda e: e.activation(out=pt[:, hh, :], in_=ps[:, 0:256], func=ACT.Exp, scale=0.125), [rps], [rpt])
                            for qt in range(2):
                                pso, rpso = PS.get()
                                for kbk in range(2):
                                    pt, rpt = pts[kbk]
                                    T(lambda e: e.matmul(pso[:, :].rearrange("p (a b) -> p a b", a=4), lhsT=VO[:, sq * 2 + kbk, j, :], rhs=pt[:, :, qt * 128:(qt + 1) * 128],
                                                         start=(kbk == 0), stop=(kbk == 1)), [r_VO, rpt], [rpso])
                                normalize(pso, rpso, j, sq * L + qt * 128)
                else:
                    with ExitStack() as es2:
                        AL2 = lambda n, s_, d: es2.enter_context(nc.sbuf_tensor(uq(n), s_, d))
                        cs = AL2("cs", [64, 2, 4096], F32); r_cs = Res('cs')
                        DM('sp', cs[:, 0, :], I['cosd'], w=[r_cs])
                        DM('sp', cs[:, 1, :], I['sind'], w=[r_cs])
                        RB = 256
                        XQ = Ring(AL2, "xq", [64, 16, RB], BF16, 2)
                        XO = Ring(AL2, "xo", [64, 16, RB], BF16, 2)
                        T1 = Ring(AL2, "t1", [64, 16, RB], F32, 1)
                        T2 = Ring(AL2, "t2", [64, 16, RB], F32, 1)
                        for (src, dstn, nh) in (('qaT', 'qrT', 16), ('kaT', 'krT', 4)):
                            for tb in range(Tg // RB):
                                cols = slice(tb * RB, (tb + 1) * RB)
                                xq, rxq = XQ.get(); xo, rxo = XO.get(); t1, rt1 = T1.get(); t2, rt2 = T2.get()
                                DM('sp', xq[:, 0:nh, :], S[src][:, :, cols].rearrange("h p t -> p h t"), r=[SR[src]], w=[rxq])
                                cb_lo = cs[0:32, 0, cols].unsqueeze(1).to_broadcast([32, nh, RB])
                                sb_lo = cs[0:32, 1, cols].unsqueeze(1).to_broadcast([32, nh, RB])
                                cb_hi = cs[32:64, 0, cols].unsqueeze(1).to_broadcast([32, nh, RB])
                                sb_hi = cs[32:64, 1, cols].unsqueeze(1).to_broadcast([32, nh, RB])
                                V(lambda e: e.tensor_tensor(out=t1[0:32, 0:nh, :], in0=xq[0:32, 0:nh, :], in1=cb_lo, op=ALU.mult), [rxq, r_cs], [rt1])
                                G(lambda e: e.tensor_tensor(out=t2[0:32, 0:nh, :], in0=xq[32:64, 0:nh, :], in1=sb_hi, op=ALU.mult), [rxq, r_cs], [rt2])
                                V(lambda e: e.tensor_tensor(out=xo[0:32, 0:nh, :], in0=t1[0:32, 0:nh, :], in1=t2[0:32, 0:nh, :], op=ALU.subtract), [rt1, rt2], [rxo])
                                V(lambda e: e.tensor_tensor(out=t1[32:64, 0:nh, :], in0=xq[0:32, 0:nh, :], in1=sb_lo, op=ALU.mult), [rxq, r_cs, rxo], [rt1])
                                G(lambda e: e.tensor_tensor(out=t2[32:64, 0:nh, :], in0=xq[32:64, 0:nh, :], in1=cb_hi, op=ALU.mult), [rxq, r_cs, rxo], [rt2])
                                V(lambda e: e.tensor_tensor(out=xo[32:64, 0:nh, :], in0=t1[32:64, 0:nh, :], in1=t2[32:64, 0:nh, :], op=ALU.add), [rt1, rt2], [rxo])
                                DM('sp', S[dstn][:, :, cols].rearrange("h p t -> p h t"), xo[:, 0:nh, :], r=[rxo], w=[SR[dstn]])
                        kb.barrier()
                    kc_t = AL("kc_t", [64, 4, 256], BF16); r_kct = Res('kct')
                    DM('pool', kc_t[:], I['ck'][l], w=[r_kct])
                    VOc = AL("VOc", [128, 2, 4, 128], BF16); r_VOc = Res('VOc')
                    G(lambda e: e.memset(VOc[:], 1.0), [], [r_VOc])
                    for cbk in range(2):
                        av, rav = AVR.get()
                        DM('pool', av[:], I['cv'][l, cbk * 128:(cbk + 1) * 128, :], w=[rav])
                        V(lambda e, cbk=cbk: e.tensor_copy(out=VOc[:, cbk, :, 0:64], in_=av[:].rearrange("p (j d) -> p j d", j=4)), [rav, r_VOc], [r_VOc])
                    QR = Ring(AL, "qr", [64, 4, 4096], BF16, 1)
                    KR = Ring(AL, "kr", [64, 4096], BF16, 2)
                    PT = Ring(AL, "pt", [128, 4, 384], BF16, 5)
                    PC = Ring(AL, "pc", [128, 2, 4, 512], BF16, 2)
                    for j in range(4):
                        qr, rqr = QR.get(); kr, rkr = KR.get()
                        DM('sp', qr[:], S['qrT'][j * 4:(j + 1) * 4].rearrange("h p t -> p h t"), r=[SR['qrT']], w=[rqr])
                        DM('sp', kr[:], S['krT'][j], r=[SR['krT']], w=[rkr])
                        pts = {}
                        pc = None

                        def pv(n):
                            pso, rpso = PS.get()
                            ms = [m for m in (n - 1, n, n + 1) if 0 <= m < nkb]
                            for ii, m in enumerate(ms):
                                pt, rpt = pts[m]
                                c0 = (n - m + 1) * 128
                                T(lambda e: e.matmul(pso[:, :].rearrange("p (a b) -> p a b", a=4), lhsT=VO[:, m, j, :], rhs=pt[:, :, c0:c0 + 128], start=(ii == 0), stop=False), [r_VO, rpt], [rpso])
                            pcc, rpcc = pc
                            for cbk in range(2):
                                c0 = (n % 4) * 128
                                T(lambda e: e.matmul(pso[:, :].rearrange("p (a b) -> p a b", a=4), lhsT=VOc[:, cbk, j, :], rhs=pcc[:, cbk, :, c0:c0 + 128], start=False, stop=(cbk == 1)), [r_VOc, rpcc], [rpso])
                            normalize(pso, rpso, j, n * 128)

                        pcs = {}
                        for m in range(nkb):
                            if m % 4 == 0:
                                pcn, rpcn = PC.get()
                                pcs[m // 4] = (pcn, rpcn)
                                for cbk in range(2):
                                    for hh in range(4):
                                        ps, rps = PS.get()
                                        T(lambda e: e.matmul(ps[:, :], lhsT=kc_t[:, j, cbk * 128:(cbk + 1) * 128], rhs=qr[:, hh, m * 128:m * 128 + 512], start=True, stop=True), [r_kct, rqr], [rps])
                                        Sc(lambda e: e.activation(out=pcn[:, cbk, hh, :], in_=ps[:, :], func=ACT.Exp, scale=0.125), [rps], [rpcn])
                            pt, rpt = PT.get()
                            pts[m] = (pt, rpt)
                            q0 = max(0, (m - 1) * 128)
                            q1 = min(Tg, (m + 2) * 128)
                            off = q0 - (m - 1) * 128
                            nq = q1 - q0
                            for hh in range(4):
                                ps, rps = PS.get()
                                T(lambda e: e.matmul(ps[:, 0:nq], lhsT=kr[:, m * 128:(m + 1) * 128], rhs=qr[:, hh, q0:q1], start=True, stop=True), [rkr, rqr], [rps])
                                Sc(lambda e: e.activation(out=pt[:, hh, off:off + nq], in_=ps[:, 0:nq], func=ACT.Exp, scale=0.125), [rps], [rpt])
                            if m > 0:
                                G(lambda e: e.tensor_tensor(out=pt[:, :, 0:128], in0=pt[:, :, 0:128], in1=Mf.unsqueeze(1).to_broadcast([128, 4, 128]), op=ALU.mult), [rpt, r_cst], [rpt])
                            if m < nkb - 1:
                                G(lambda e: e.tensor_tensor(out=pt[:, :, 256:384], in0=pt[:, :, 256:384], in1=Mb.unsqueeze(1).to_broadcast([128, 4, 128]), op=ALU.mult), [rpt, r_cst], [rpt])
                            if m >= 1:
                                pc = pcs[(m - 1) // 4]
                                pv(m - 1)
                        pc = pcs[(nkb - 1) // 4]
                        pv(nkb - 1)
                kb.barrier()
            for es in _stage_ctx('C' in stages):
                AL = lambda n, s, d: es.enter_context(nc.sbuf_tensor(uq(n), s, d))
                BS = 256
                wb01 = AL("wb01", [128, 3, 8, 1024], BF16); wo = AL("wo", [128, 8, 1024], BF16)
                r_w = Res('wC')
                for b3 in range(3):
                    DM('pool', wb01[:, b3], I['wb'][l, :, b3], w=[r_w])
                DM('pool', wo[:], I['wout'][l], w=[r_w])
                gcol = AL("gcol", [128, 2], F32); cw = AL("cw", [128, 3, 8], F32); r_gc = Res('gc')
                DM('sp', gcol[:], I['glag'][l], w=[r_gc])
                DM('sp', cw[:], I['convw'][l], w=[r_gc])
                OF = Ring(AL, "of", [128, 8, BS], F32, 1)
                OB = Ring(AL, "ob", [128, 8, BS], F32, 1)
                SQ = Ring(AL, "sq", [128, 8, BS], BF16, 1)
                GGr = Ring(AL, "gg", [128, 8, BS], BF16, 1)
                YA = Ring(AL, "ya", [128, 8, BS], BF16, 1)
                YB = Ring(AL, "yb", [128, 8, BS], BF16, 1)
                YCr = Ring(AL, "ycc", [128, 8, BS], BF16, 1)
                CH = Ring(AL, "ch", [128, 8, BS + 2], BF16, 1)
                CC = Ring(AL, "ccx", [128, 8, BS + 2], BF16, 1)
                CB = Ring(AL, "cbx", [128, 8, BS], BF16, 1)
                Z = Ring(AL, "z", [128, 8, BS + 2], F32, 1)
                ACC = Ring(AL, "acc", [128, BS], F32, 2)
                RS = Ring(AL, "rs", [128, BS], F32, 2)
                GT = Ring(AL, "gt", [128, 24, BS], BF16, 1)
                MG = Ring(AL, "mg", [128, 8, BS], BF16, 1)
                TA = Ring(AL, "ta", [128, BS], F32, 2)
                TB = Ring(AL, "tb", [128, BS], F32, 2)
                XR = Ring(AL, "xt", [128, 1024], F32, 1)
                MX = Ring(AL, "mx", [128, 1024], F32, 1)
                SM = Ring(AL, "sm", [128, 16], F32, 2)
                for bi in range(Tg // BS):
                    t0 = bi * BS
                    cols = slice(t0, t0 + BS)
                    sq0 = (t0 // L) * L
                    of, rof = OF.get(); ob, rob = OB.get(); gg, rgg = GGr.get(); ya, rya = YA.get(); sqt, rsq = SQ.get()
                    DM('sp', of[:], S['of'][:, :, cols].rearrange("j p t -> p j t"), r=[SR['of']], w=[rof])
                    DM('sp', ob[:], S['ob'][:, :, cols].rearrange("j p t -> p j t"), r=[SR['ob']], w=[rob])
                    DM('sp', gg[:], S['ggT'][:, :, cols].rearrange("j p t -> p j t"), r=[SR['ggT']], w=[rgg])
                    V(lambda e: e.tensor_tensor(out=of[:], in0=of[:], in1=ob[:], op=ALU.add), [rof, rob], [rof])
                    Sc(lambda e: e.activation(out=sqt[:], in_=of[:], func=ACT.Square), [rof], [rsq])
                    for h in range(4):
                        ps, rps = PS.get()
                        for dvc in range(2):
                            T(lambda e: e.matmul(ps[:, 0:BS], lhsT=ms_bf[:], rhs=sqt[:, 2 * h + dvc, :], start=(dvc == 0), stop=(dvc == 1)), [r_msbf, rsq], [rps])
                        rs, rrs = RS.get()
                        Sc(lambda e: e.activation(out=rs[:], in_=ps[:, 0:BS], func=ACT.Sqrt, bias=EPS), [rps], [rrs])
                        V(lambda e: e.reciprocal(out=rs[:], in_=rs[:]), [rrs], [rrs])
                        for dvc in range(2):
                            jx = 2 * h + dvc
                            V(lambda e: e.tensor_tensor(out=of[:, jx, :], in0=of[:, jx, :], in1=rs[:], op=ALU.mult), [rof, rrs], [rof])
                            V(lambda e: e.scalar_tensor_tensor(out=ya[:, jx, :], in0=of[:, jx, :], scalar=gcol[:, dvc:dvc + 1], in1=gg[:, jx, :], op0=ALU.mult, op1=ALU.mult),
                              [rof, r_gc, rgg], [rya])
                    chh, rch = CH.get(); ccx, rcc = CC.get(); cbx, rcb = CB.get(); z, rz = Z.get(); yb, ryb = YB.get()
                    lo = 1 if t0 == sq0 else 0
                    hi = 1 if t0 + BS == sq0 + L else 0
                    if lo:
                        G(lambda e: e.memset(chh[:, :, 0:1], 0.0), [], [rch])
                        G(lambda e: e.memset(ccx[:, :, 0:1], 0.0), [], [rcc])
                    if hi:
                        G(lambda e: e.memset(chh[:, :, BS + 1:BS + 2], 0.0), [], [rch])
                        G(lambda e: e.memset(ccx[:, :, BS + 1:BS + 2], 0.0), [], [rcc])
                    src = slice(t0 - 1 + lo, t0 + BS + 1 - hi)
                    dsl = slice(lo, BS + 2 - hi)
                    DM('sp', chh[:, :, dsl], S['chT'][:, :, src].rearrange("j p t -> p j t"), r=[SR['chT']], w=[rch])
                    DM('sp', ccx[:, :, dsl], S['ccT'][:, :, src].rearrange("j p t -> p j t"), r=[SR['ccT']], w=[rcc])
                    DM('sp', cbx[:], S['cbT'][:, :, cols].rearrange("j p t -> p j t"), r=[SR['cbT']], w=[rcb])
                    G(lambda e: e.tensor_tensor(out=z[:], in0=chh[:], in1=ccx[:], op=ALU.mult), [rch, rcc], [rz])
                    for jx in range(8):
                        acc, racc = ACC.get()
                        G(lambda e: e.tensor_scalar(out=acc[:], in0=z[:, jx, 0:BS], scalar1=cw[:, 0, jx:jx + 1], scalar2=None, op0=ALU.mult), [rz, r_gc], [racc])
                        V(lambda e: e.scalar_tensor_tensor(out=acc[:], in0=z[:, jx, 1:BS + 1], scalar=cw[:, 1, jx:jx + 1], in1=acc[:], op0=ALU.mult, op1=ALU.add), [rz, r_gc, racc], [racc])
                        V(lambda e: e.scalar_tensor_tensor(out=acc[:], in0=z[:, jx, 2:BS + 2], scalar=cw[:, 2, jx:jx + 1], in1=acc[:], op0=ALU.mult, op1=ALU.add), [rz, r_gc, racc], [racc])
                        G(lambda e: e.tensor_tensor(out=yb[:, jx, :], in0=acc[:], in1=cbx[:, jx, :], op=ALU.mult), [racc, rcb], [ryb])
                    ycc, rycc = YCr.get(); gt, rgt = GT.get(); mg, rmg = MG.get()
                    ycv = S['ycT'][:, :, cols].rearrange("(c two) p t -> two p c t", two=2)
                    DM('sp', ycc[0:64], ycv[0], r=[SR['ycT']], w=[rycc])
                    DM('sp', ycc[64:128], ycv[1], r=[SR['ycT']], w=[rycc])
                    DM('sp', gt[:], S['gT'][:, :, cols].rearrange("j p t -> p j t"), r=[SR['gT']], w=[rgt])
                    for dch in range(8):
                        dsl_ = slice(dch * 128, (dch + 1) * 128)
                        p0, rp0 = PS.get(); p1, rp1 = PS.get(); p2, rp2 = PS.get()
                        for kc in range(8):
                            T(lambda e, kc=kc: e.matmul(p0[:, 0:BS], lhsT=wb01[:, 0, kc, dsl_], rhs=ya[:, kc, :], start=(kc == 0), stop=(kc == 7)), [r_w, rya], [rp0])
                        for kc in range(8):
                            T(lambda e, kc=kc: e.matmul(p1[:, 0:BS], lhsT=wb01[:, 1, kc, dsl_], rhs=yb[:, kc, :], start=(kc == 0), stop=(kc == 7)), [r_w, ryb], [rp1])
                        for kc in range(8):
                            T(lambda e, kc=kc: e.matmul(p2[:, 0:BS], lhsT=wb01[:, 2, kc, dsl_], rhs=ycc[:, kc, :], start=(kc == 0), stop=(kc == 7)), [r_w, rycc], [rp2])
                        ta, rta = TA.get(); tb_, rtb = TB.get()
                        V(lambda e: e.tensor_tensor(out=ta[:], in0=p0[:, 0:BS], in1=gt[:, dch, :], op=ALU.mult), [rp0, rgt], [rta])
                        V(lambda e: e.tensor_tensor(out=tb_[:], in0=p1[:, 0:BS], in1=gt[:, 8 + dch, :], op=ALU.mult), [rp1, rgt], [rtb])
                        G(lambda e: e.tensor_tensor(out=ta[:], in0=ta[:], in1=tb_[:], op=ALU.add), [rta, rtb], [rta])
                        V(lambda e: e.tensor_tensor(out=tb_[:], in0=p2[:, 0:BS], in1=gt[:, 16 + dch, :], op=ALU.mult), [rp2, rgt, rta], [rtb])
                        G(lambda e: e.tensor_tensor(out=mg[:, dch, :], in0=ta[:], in1=tb_[:], op=ALU.add), [rta, rtb], [rmg])
                    for tt in range(BS // 128):
                        ti = (t0 // 128) + tt
                        rows = slice(ti * 128, (ti + 1) * 128)
                        xt, rx = XR.get(); mx, rmx = MX.get(); sm, rsm = SM.get()
                        DM('sp', xt[:], S['xres'][rows, :], r=[xres_r[ti]], w=[rx])
                        for hf in range(2):
                            ps, rps = PS.get()
                            for kc in range(8):
                                T(lambda e, kc=kc: e.matmul(ps[:, :], lhsT=mg[:, kc, tt * 128:(tt + 1) * 128], rhs=wo[:, kc, hf * 512:(hf + 1) * 512], start=(kc == 0), stop=(kc == 7)), [rmg, r_w], [rps])
                            V(lambda e: e.tensor_tensor(out=mx[:, hf * 512:(hf + 1) * 512], in0=ps[:, :], in1=mbc[:, 0, hf * 512:(hf + 1) * 512], op=ALU.mult), [rps, r_mbc], [rmx])
                        V(lambda e: e.scalar_tensor_tensor(out=xt[:], in0=xt[:], scalar=ALPHA, in1=mx[:], op0=ALU.mult, op1=ALU.add), [rx, rmx], [rx])
                        ln_tm(None, xt, rx, lnbc[:, 0, :], lnbc[:, 1, :], sm, rsm)
                        DM('sp', S['xres'][rows, :], xt[:], r=[rx], w=[xres_r[ti]])
                kb.barrier()
            for es in _stage_ctx('P' in stages):
                AL = lambda n, s, d: es.enter_context(nc.sbuf_tensor(uq(n), s, d))
                wpq = AL("wpq", [128, 8, 2048], BF16); r_wpq = Res('wpq')
                DM('pool', wpq[:], I['wpq'][l], w=[r_wpq])
                pk = AL("pk", [128, 16, 128], F32); r_pk = Res('pk')
                DM('sp', pk[:], I['pkeys'][l], w=[r_pk])
                identb = AL("identb", [128, 128], BF16); r_idb = Res('identb')
                V(lambda e: e.tensor_copy(out=identb[:], in_=ident), [r_cst], [r_idb])
                XR = Ring(AL, "xt", [128, 1024], F32, 2)
                H2 = Ring(AL, "h2", [128, 1024], F32, 2)
                H2T = Ring(AL, "h2T", [128, 8, 128], BF16, 1)
                QT = Ring(AL, "qT", [128, 16, 128], F32, 1)
                SCr = Ring(AL, "scr", [128, 16, 128], F32, 1)
                WK = Ring(AL, "wk", [128, 256], F32, 2)
                TV = Ring(AL, "tv", [128, 2, 16], F32, 2)
                TI = Ring(AL, "ti", [128, 2, 16], I32, 2)
                TF = Ring(AL, "tf", [128, 2, 16], F32, 2)
                CD = Ring(AL, "cd", [128, 16, 16], F32, 2)
                CI = Ring(AL, "ci", [128, 16, 16], F32, 2)
                TS = Ring(AL, "ts", [128, 8, 16], F32, 2)
                EI = Ring(AL, "ei", [128, 128], F32, 2)
                EII = Ring(AL, "eii", [128, 128], I32, 2)
                GA = Ring(AL, "ga", [128, 8, 16], F32, 2)
                SM = Ring(AL, "sm", [128, 16], F32, 2)
                jk = AL("jk", [128, 1024], F32)
                UB = Ring(AL, "ub", [128, 4, 2048], BF16, 3)
                A4 = Ring(AL, "a4", [128, 12], F32, 4)
                DG = Ring(AL, "dg", [128, 4, 128], BF16, 3)
                ACCr = Ring(AL, "acc", [128, 1024], F32, 1)

                def topk_phase(ti):
                    st = {}
                    rows = slice(ti * 128, (ti + 1) * 128)
                    xt, rx = XR.get(); h2, rh2 = H2.get(); h2T, rh2T = H2T.get()
                    st.update(xt=xt, rx=rx, h2=h2, rh2=rh2)
                    DM('sp', xt[:], S['xres'][rows, :], r=[xres_r[ti]], w=[rx])
                    V(lambda e: e.tensor_tensor(out=h2[:], in0=xt[:], in1=mbc[:, 2, :], op=ALU.mult), [rx, r_mbc], [rh2])
                    V(lambda e: e.tensor_tensor(out=h2[:], in0=h2[:], in1=mbc[:, 1, :], op=ALU.add), [rh2, r_mbc], [rh2])
                    if peer is not True:
                        return st
                    for hb in range(2):
                        ps, rps = PS.get()
                        for k4 in range(4):
                            kc = hb * 4 + k4
                            T(lambda e, kc=kc, k4=k4: e.transpose(ps[:, k4 * 128:(k4 + 1) * 128], h2[:, kc * 128:(kc + 1) * 128], ident), [rh2, r_cst], [rps])
                        Sc(lambda e: e.copy(out=h2T[:, hb * 4:(hb + 1) * 4, :], in_=ps[:, :].rearrange("p (a b) -> p a b", a=4)), [rps], [rh2T])
                    qT, rqT = QT.get()
                    for c4 in range(4):
                        ps, rps = PS.get()
                        for cc_ in range(4):
                            ch = c4 * 4 + cc_
                            for kc in range(8):
                                T(lambda e, kc=kc, ch=ch, cc_=cc_: e.matmul(ps[:, cc_ * 128:(cc_ + 1) * 128], lhsT=wpq[:, kc, ch * 128:(ch + 1) * 128], rhs=h2T[:, kc, :],
                                                                            start=(kc == 0), stop=(kc == 7)), [r_wpq, rh2T], [rps])
                        Sc(lambda e: e.copy(out=qT[:, c4 * 4:(c4 + 1) * 4, :], in_=ps[:, :].rearrange("p (a b) -> p a b", a=4)), [rps], [rqT])
                    sc, rsc = SCr.get()
                    for c4 in range(4):
                        ps, rps = PS.get()
                        for cc_ in range(4):
                            ch = c4 * 4 + cc_
                            T(lambda e, ch=ch, cc_=cc_: e.matmul(ps[:, cc_ * 128:(cc_ + 1) * 128], lhsT=qT[:, ch, :], rhs=pk[:, ch, :], start=True, stop=True), [rqT, r_pk], [rps])
                        V(lambda e: e.tensor_copy(out=sc[:, c4 * 4:(c4 + 1) * 4, :], in_=ps[:, :].rearrange("p (a b) -> p a b", a=4)), [rps], [rsc])
                    sci = sc[:].bitcast(I32)
                    V(lambda e: e.tensor_tensor(out=sci, in0=sci, in1=cint[:, 0, 0:128].unsqueeze(1).to_broadcast([128, 16, 128]), op=ALU.bitwise_and), [rsc, r_cint], [rsc])
                    V(lambda e: e.tensor_tensor(out=sci, in0=sci, in1=cint[:, 1, 0:128].unsqueeze(1).to_broadcast([128, 16, 128]), op=ALU.bitwise_or), [rsc, r_cint], [rsc])
                    tsa, rts = TS.get(); ei, rei = EI.get()
                    for h in range(8):
                        tv, rtv = TV.get(); tix, rti = TI.get(); tf, rtf = TF.get()
                        for sd in range(2):
                            ch = h * 2 + sd
                            wk, rwk = WK.get()
                            V(lambda e: e.max(out=tv[:, sd, 0:8], in_=sc[:, ch, :]), [rsc], [rtv])
                            V(lambda e: e.match_replace(out=wk[:, 0:128], in_to_replace=tv[:, sd, 0:8], in_values=sc[:, ch, :], imm_value=-1e30), [rsc, rtv], [rwk])
                            V(lambda e: e.max(out=tv[:, sd, 8:16], in_=wk[:, 0:128]), [rwk], [rtv])
                        V(lambda e: e.tensor_tensor(out=tix[:], in0=tv[:].bitcast(I32), in1=cint[:, 2, 0:32].rearrange("p (a b) -> p a b", a=2), op=ALU.bitwise_and), [rtv, r_cint], [rti])
                        V(lambda e: e.tensor_copy(out=tf[:], in_=tix[:]), [rti], [rtf])
                        cd, rcd = CD.get(); ci_, rci = CI.get()
                        V(lambda e: e.tensor_tensor(out=cd[:], in0=tv[:, 0, :].unsqueeze(2).to_broadcast([128, 16, 16]), in1=tv[:, 1, :].unsqueeze(1).to_broadcast([128, 16, 16]), op=ALU.add), [rtv], [rcd])
                        V(lambda e: e.scalar_tensor_tensor(out=ci_[:], in0=tf[:, 0, :].unsqueeze(2).to_broadcast([128, 16, 16]), scalar=128.0, in1=tf[:, 1, :].unsqueeze(1).to_broadcast([128, 16, 16]),
                                                           op0=ALU.mult, op1=ALU.add), [rtf], [rci])
                        cdf = cd[:].rearrange("p a b -> p (a b)")
                        cdi = cdf.bitcast(I32)
                        V(lambda e: e.tensor_tensor(out=cdi, in0=cdi, in1=cint[:, 3, :], op=ALU.bitwise_and), [rcd, r_cint], [rcd])
                        V(lambda e: e.tensor_tensor(out=cdi, in0=cdi, in1=cint[:, 1, :], op=ALU.bitwise_or), [rcd, r_cint], [rcd])
                        cif = ci_[:].rearrange("p a b -> p (a b)")
                        wk, rwk = WK.get()
                        V(lambda e: e.max(out=tsa[:, h, 0:8], in_=cdf), [rcd], [rts])
                        V(lambda e: e.match_replace(out=wk[:], in_to_replace=tsa[:, h, 0:8], in_values=cdf, imm_value=-1e30), [rcd, rts], [rwk])
                        V(lambda e: e.max(out=tsa[:, h, 8:16], in_=wk[:]), [rwk], [rts])
                        for k in range(16):
                            V(lambda e, k=k: e.scalar_tensor_tensor(out=jk[:, 0:256], in0=cdf, scalar=tsa[:, h, k:k + 1], in1=cif, op0=ALU.is_equal, op1=ALU.mult,
                                                                    accum_out=ei[:, h * 16 + k:h * 16 + k + 1]), [rcd, rci, rts], ([rei] if k in (0, 15) else []))
                    ga, rga = GA.get(); sm, rsm = SM.get()
                    V(lambda e: e.tensor_tensor(out=ga[:], in0=tsa[:], in1=tsa[:, :, 0:1].to_broadcast([128, 8, 16]), op=ALU.subtract), [rts], [rga])
                    Sc(lambda e: e.activation(out=ga[:], in_=ga[:], func=ACT.Exp), [rga], [rga])
                    V(lambda e: e.tensor_reduce(out=sm[:, 0:8], in_=ga[:], axis=AX.X, op=ALU.add), [rga], [rsm])
                    V(lambda e: e.reciprocal(out=sm[:, 0:8], in_=sm[:, 0:8]), [rsm], [rsm])
                    V(lambda e: e.tensor_tensor(out=ga[:], in0=ga[:], in1=sm[:, 0:8].unsqueeze(2).to_broadcast([128, 8, 16]), op=ALU.mult), [rga, rsm], [rga])
                    eii, reii = EII.get()
                    V(lambda e: e.tensor_scalar(out=ei[:], in0=ei[:], scalar1=16383.0, scalar2=0.0, op0=ALU.min, op1=ALU.max), [rei], [rei])
                    V(lambda e: e.tensor_copy(out=eii[:], in_=ei[:]), [rei], [reii])
                    st.update(ga=ga, rga=rga, eii=eii, reii=reii)
                    return st

                def gather_phase(ti, st):
                    rows = slice(ti * 128, (ti + 1) * 128)
                    xt, rx, h2, rh2 = st['xt'], st['rx'], st['h2'], st['rh2']
                    acc, racc = ACCr.get()
                    if peer is not True:
                        V(lambda e: e.memset(acc[:], 0.0), [], [racc])
                    else:
                        ga, rga, eii, reii = st['ga'], st['rga'], st['eii'], st['reii']
                        gaf = ga[:].rearrange("p a b -> p (a b)")
                        for j4 in range(32):
                            ub, rub = UB.get()
                            kb.dma_group('pool', [(lambda e, jx=j4 * 4 + q4, q4=q4: e.indirect_dma_start(out=ub[:, q4, :], out_offset=None, in_=TBL[f'uv{l}'],
                                                   in_offset=bass.IndirectOffsetOnAxis(ap=eii[:, jx:jx + 1], axis=0))) for q4 in range(4)], [reii, r_tab], [rub])
                            a4, ra4 = A4.get()
                            for q4 in range(4):
                                V(lambda e, q4=q4: e.scalar_tensor_tensor(out=jk[:], in0=h2[:], scalar=1.0, in1=ub[:, q4, 0:1024], op0=ALU.mult, op1=ALU.mult, accum_out=a4[:, q4:q4 + 1]),
                                  [rh2, rub], ([ra4] if q4 in (0, 3) else []))
                            Sc(lambda e: e.activation(out=a4[:, 4:8], in_=a4[:, 0:4], func=ACT.Gelu), [ra4], [ra4])
                            V(lambda e, j4=j4: e.tensor_tensor(out=a4[:, 8:12], in0=a4[:, 4:8], in1=gaf[:, j4 * 4:(j4 + 1) * 4], op=ALU.mult), [ra4, rga], [ra4])
                            dg, rdg = DG.get()
                            V(lambda e: e.tensor_tensor(out=dg[:], in0=identb[:].unsqueeze(1).to_broadcast([128, 4, 128]), in1=a4[:, 8:12].unsqueeze(2).to_broadcast([128, 4, 128]), op=ALU.mult),
                              [r_idb, ra4], [rdg])
                            for q4 in range(4):
                                jx = j4 * 4 + q4
                                for hf in range(2):
                                    T(lambda e, jx=jx, q4=q4, hf=hf: e.matmul(pacc[hf][:, :], lhsT=dg[:, q4, :], rhs=ub[:, q4, 1024 + hf * 512:1024 + (hf + 1) * 512], start=(jx == 0), stop=(jx == 127)),
                                      [rdg, rub], [r_pacc[hf]])
                    sm, rsm = SM.get()
                    if peer is True:
                        for hf in range(2):
                            V(lambda e, hf=hf: e.tensor_tensor(out=acc[:, hf * 512:(hf + 1) * 512], in0=pacc[hf][:, :], in1=mbc[:, 3, hf * 512:(hf + 1) * 512], op=ALU.mult), [r_pacc[hf], r_mbc], [racc])
                    V(lambda e: e.scalar_tensor_tensor(out=xt[:], in0=xt[:], scalar=ALPHA, in1=acc[:], op0=ALU.mult, op1=ALU.add), [rx, racc], [rx])
                    ln_tm(None, xt, rx, lnbc[:, 2, :], lnbc[:, 3, :], sm, rsm)
                    if last:
                        DM('sp', g['y'][rows, :], xt[:], r=[rx])
                    else:
                        DM('sp', S['xres'][rows, :], xt[:], r=[rx], w=[xres_r[ti]])

                nxt = topk_phase(0)
                for ti in range(NT):
                    cur = nxt
                    if ti + 1 < NT:
                        nxt = topk_phase(ti + 1)
                    gather_phase(ti, cur)
                kb.barrier()
    if dbg:
        kb.barrier()
        for name in dbg:
            DM('sp', O['dbg_' + name], S[name], r=[SR[name]])
    kb.finish()
    ctx_nc.__exit__(None, None, None)
    print("instr counts", kb.etot, "waits", kb.nwaits, kb.wcnt)
    return nc


_CACHE = {}


def kernel(**inputs):
    inp = {k: np.asarray(v) for k, v in inputs.items()}
    sh = prep_shared(inp)
    in_maps = [prep_core(inp, sh, c) for c in range(8)]
    if 'nc' not in _CACHE:
        _CACHE['nc'] = build(do_p=True, do_s=True, nlayers=2)
    nc = _CACHE['nc']
    res = run_bass_kernel_spmd(nc, in_maps, core_ids=list(range(8)))
    R = res.results
    y_prompt = np.concatenate([np.asarray(R[c]['yp']).reshape(4, 256, 1024) for c in range(8)], 0).astype(np.float32)
    ys = []
    for b in range(4):
        ys.append(np.concatenate([np.asarray(R[2 * b]['ys'])[0:2048], np.asarray(R[2 * b + 1]['ys'])[2048:4096]], 0))
    y_sample = np.stack(ys, 0).astype(np.float32)
    nk = np.concatenate([np.asarray(R[c]['nk']).reshape(4, 2, 256, 4, 64) for c in range(8)], 0).astype(np.float32)
    nv = np.concatenate([np.asarray(R[c]['nv']).reshape(4, 2, 256, 4, 64) for c in range(8)], 0).astype(np.float32)
    nst = np.concatenate([np.asarray(R[c]['nst']) for c in range(8)], 0).astype(np.float32)
    return (y_prompt, y_sample, nk, nv, nst)
```

```python
from contextlib import ExitStack
import numpy as np
import concourse.bass as bass
import concourse.mybir as mybir
from concourse.bass_utils import run_bass_kernel_spmd

ACT = mybir.ActivationFunctionType
ALU = mybir.AluOpType
AX = mybir.AxisListType
F32 = mybir.dt.float32
BF16 = mybir.dt.bfloat16
I32 = mybir.dt.int32
U32 = mybir.dt.uint32

EPOCH = 12000
SAME_ENG_SKIP = False
NDSEM = {'sp': 24, 'pool': 24, 'act': 8}


class Res:
    __slots__ = ('name', 'w', 'r', 'excl', 'dram')

    def __init__(self, name='r', excl=False, dram=False):
        self.name = name
        self.w = {}
        self.r = {}
        self.excl = excl
        self.dram = dram


class KB:
    def __init__(self, nc):
        self.nc = nc
        self.engs = {'pe': nc.tensor, 'dve': nc.vector, 'act': nc.scalar, 'pool': nc.gpsimd, 'sp': nc.sync}
        self.esem = {}
        self.ecnt = {n: 0 for n in self.engs}
        self.etot = {n: 0 for n in self.engs}
        self.eepoch = {n: 0 for n in self.engs}
        self.known = {n: {} for n in self.engs}
        self.dsems = {}
        self.dval = {}
        self.dnext = {q: 0 for q in NDSEM}
        self.dpool = {}
        for q, n in NDSEM.items():
            self.dpool[q] = []
            for i in range(n):
                s = nc.alloc_semaphore(f'd_{q}_{i}')
                self.dsems[(q, i)] = s
                self.dval[(q, i)] = 0
                self.dpool[q].append((q, i))
        self.nwaits = 0
        self.wcnt = {}
        self.out_events = []

    def _esem(self, eng, ep):
        k = (eng, ep)
        if k not in self.esem:
            self.esem[k] = self.nc.alloc_semaphore(f'e_{eng}_{ep}')
        return self.esem[k]

    def _deps(self, reads, writes):
        deps = {}
        for r in reads:
            for k, v in r.w.items():
                if deps.get(k, 0) < v:
                    deps[k] = v
        for w in writes:
            for k, v in w.w.items():
                if deps.get(k, 0) < v:
                    deps[k] = v
            for k, v in w.r.items():
                if deps.get(k, 0) < v:
                    deps[k] = v
        return deps

    def _wait(self, eng, deps):
        h = self.engs[eng]
        kn = self.known[eng]
        for k, v in deps.items():
            if kn.get(k, 0) >= v:
                continue
            if k[0] == 'E':
                sem = self._esem(k[1], k[2])
            else:
                sem = self.dsems[(k[1], k[2])]
            h.wait_ge(sem, v)
            self.nwaits += 1
            self.wcnt[eng] = self.wcnt.get(eng, 0) + 1
            kn[k] = v

    def _record(self, key, val, reads, writes):
        for r in reads:
            if r.r.get(key, 0) < val:
                r.r[key] = val
        for w in writes:
            w.w = {key: val}
            w.r = {}

    def op(self, eng, fn, reads=(), writes=()):
        if any(r.excl for r in reads):
            writes = list(writes) + [r for r in reads if r.excl]
            reads = [r for r in reads if not r.excl]
        deps = self._deps(reads, writes)
        if eng == 'pe':
            for k in [k for k in deps if k[0] == 'E' and k[1] == 'pe']:
                del deps[k]
        else:
            cur_ep = self.eepoch[eng]
            for k in [k for k in deps if k[0] == 'E' and k[1] == eng]:
                if SAME_ENG_SKIP and (k[2] < cur_ep or (self.ecnt[eng] + 1 - deps[k]) >= 2):
                    del deps[k]
        self._wait(eng, deps)
        ins = fn(self.engs[eng])
        if self.ecnt[eng] >= EPOCH:
            self.eepoch[eng] += 1
            self.ecnt[eng] = 0
        self.ecnt[eng] += 1
        self.etot[eng] += 1
        ep = self.eepoch[eng]
        ins.then_inc(self._esem(eng, ep), 1)
        self._record(('E', eng, ep), self.ecnt[eng], reads, writes)
        return ins

    def dma(self, q, fn, reads=(), writes=(), is_output=False):
        deps = self._deps(reads, [w for w in writes if not w.dram])
        for w in writes:
            if w.dram:
                for k, v in w.r.items():
                    if deps.get(k, 0) < v:
                        deps[k] = v
        self._wait(q, deps)
        slot = self.dpool[q][self.dnext[q]]
        self.dnext[q] = (self.dnext[q] + 1) % len(self.dpool[q])
        prev = self.dval[slot]
        key = ('D', slot[0], slot[1])
        if prev > 0:
            self._wait(q, {key: prev})
        ins = fn(self.engs[q])
        ins.then_inc(self.dsems[slot], 16)
        self.dval[slot] = prev + 16
        self._record(key, prev + 16, reads, [w for w in writes if not w.dram])
        for w in writes:
            if w.dram:
                w.w[key] = prev + 16
                w.r = {}
        if is_output:
            self.out_events.append((key, prev + 16))
        return ins

    def dma_group(self, q, fns, reads=(), writes=()):
        deps = self._deps(reads, writes)
        self._wait(q, deps)
        slot = self.dpool[q][self.dnext[q]]
        self.dnext[q] = (self.dnext[q] + 1) % len(self.dpool[q])
        prev = self.dval[slot]
        key = ('D', slot[0], slot[1])
        if prev > 0:
            self._wait(q, {key: prev})
        for fn in fns:
            ins = fn(self.engs[q])
            ins.then_inc(self.dsems[slot], 16)
        self.dval[slot] = prev + 16 * len(fns)
        self._record(key, self.dval[slot], reads, writes)

    def finish(self):
        deps = {}
        for slot, v in self.dval.items():
            if v > 0:
                deps[('D', slot[0], slot[1])] = v
        self._wait('sp', deps)
        deps = {}
        for eng in self.engs:
            if eng == 'sp':
                continue
            if self.etot[eng] > 0:
                deps[('E', eng, self.eepoch[eng])] = self.ecnt[eng]
        self._wait('sp', deps)


def _barrier(self):
    ev = {}
    for eng in self.engs:
        if self.etot[eng] > 0:
            ev[('E', eng, self.eepoch[eng])] = self.ecnt[eng]
    for slot, v in self.dval.items():
        if v > 0:
            ev[('D', slot[0], slot[1])] = v
    for eng in self.engs:
        self._wait(eng, dict(ev))


KB.barrier = _barrier


class Ring:
    def __init__(self, alloc, name, shape, dt, n, excl=False):
        self.t = [alloc(f"{name}{i}", shape, dt) for i in range(n)]
        self.r = [Res(f"{name}{i}", excl) for i in range(n)]
        self.i = 0

    def get(self):
        k = self.i
        self.i = (self.i + 1) % len(self.t)
        return self.t[k], self.r[k]


D = 1024
O_GQ, O_GK, O_GV, O_GG, O_GA, O_CH, O_CB, O_CC, O_AQ, O_AK, O_AV, O_MG = (
    0, 512, 1024, 2048, 3072, 3104, 4128, 5152, 6176, 7200, 7456, 7712)
ALPHA = float(4 ** 0.25)
EPS = 1e-6
NBLK = 24
NOMI = False
PERM = np.concatenate([np.arange(0, 64, 2), np.arange(1, 64, 2)])
GRID_W = 64


def _blockify(W, cols):
    out = np.zeros((128, 8, 512), np.float32)
    n = len(cols)
    out[:, :, :n] = W[:, cols].reshape(8, 128, n).transpose(1, 0, 2)
    return out


def win_blocks(w):
    r = np.arange
    blks = [r(O_GQ, O_GQ + 512), r(O_GK, O_GK + 512), r(O_GG, O_GG + 512), r(O_GG + 512, O_GG + 1024),
            r(O_GA, O_GA + 32),
            r(O_CH, O_CH + 512), r(O_CH + 512, O_CH + 1024), r(O_CB, O_CB + 512), r(O_CB + 512, O_CB + 1024),
            r(O_CC, O_CC + 512), r(O_CC + 512, O_CC + 1024)]
    for half in range(2):
        blks.append(np.concatenate([O_AQ + (half * 8 + h) * 64 + PERM for h in range(8)]))
    blks.append(np.concatenate([O_AK + j * 64 + PERM for j in range(4)]))
    for i in range(6):
        blks.append(r(O_MG + i * 512, O_MG + (i + 1) * 512))
    blks += [r(O_GK, O_GK + 512), r(O_GV, O_GV + 512), r(O_GV + 512, O_GV + 1024),
             np.concatenate([r(O_AK, O_AK + 256), r(O_AV, O_AV + 256)])]
    assert len(blks) == NBLK
    return np.stack([_blockify(w, c) for c in blks])


def rope_tables(T):
    rows = T // GRID_W
    row = np.repeat(np.arange(rows, dtype=np.float32), GRID_W)
    col = np.tile(np.arange(GRID_W, dtype=np.float32), rows)
    half = 32
    inv = (np.float32(10000.0) ** (-np.arange(0, half, 2, dtype=np.float32) / np.float32(half))).astype(np.float32)
    ang = np.concatenate([row[:, None] * inv, col[:, None] * inv], -1).astype(np.float32)
    c = np.cos(ang).astype(np.float32).T
    s = np.sin(ang).astype(np.float32).T
    return np.ascontiguousarray(np.concatenate([c, c], 0)), np.ascontiguousarray(np.concatenate([s, s], 0))


def make_consts():
    i = np.arange(128)
    ident = np.eye(128, dtype=np.float32)
    Mf = (i[:, None] <= i[None, :]).astype(np.float32)
    Mb = (i[:, None] >= i[None, :]).astype(np.float32)
    Nf = (i[:, None] > i[None, :]).astype(np.float32)
    Nb = (i[:, None] < i[None, :]).astype(np.float32)
    return np.ascontiguousarray(np.stack([ident, Mf, Mb, Nf, Nb], 1))


def prep_shared(inp):
    f = lambda a: np.ascontiguousarray(np.asarray(a, dtype=np.float32))
    sh = {}
    sh['win'] = f(np.stack([win_blocks(np.asarray(inp['w_in'][l])) for l in range(2)]))
    wm = np.asarray(inp['w_mod'])
    sh['wmod'] = f(np.stack([np.stack([_blockify(wm[l], np.arange(b * 512, (b + 1) * 512)) for b in range(12)]) for l in range(2)]))
    sh['bmod'] = f(inp['b_mod'])
    wa = np.zeros((2, 33, 1024), np.float32)
    for l in range(2):
        wa[l, 0:16, 0:512] = inp['w_gla_a2'][l, 0]
        wa[l, 16:32, 512:1024] = inp['w_gla_a2'][l, 1]
        wa[l, 32, 0:512] = inp['b_gla_a'][l, 0]
        wa[l, 32, 512:1024] = inp['b_gla_a'][l, 1]
    sh['wa2'] = wa
    sh['glag'] = f(np.asarray(inp['gla_norm_g']).reshape(2, 2, 128).transpose(0, 2, 1))
    sh['convw'] = f(np.asarray(inp['conv_w']).reshape(2, 3, 8, 128).transpose(0, 3, 1, 2))
    sh['sink'] = f(inp['attn_sink'])
    wb = np.asarray(inp['w_branch'])
    sh['wb'] = f(wb.reshape(2, 3, 8, 128, 1024).transpose(0, 3, 1, 2, 4))
    sh['wout'] = f(np.asarray(inp['w_out']).reshape(2, 8, 128, 1024).transpose(0, 2, 1, 3))
    sh['wpq'] = f(np.asarray(inp['w_pq']).reshape(2, 8, 128, 2048).transpose(0, 2, 1, 3))
    pk = np.asarray(inp['peer_keys'])
    sh['pkeys'] = f(pk.reshape(2, 16, 128, 128).transpose(0, 3, 1, 2))
    for l in range(2):
        sh[f'pu{l}'] = f(np.asarray(inp['peer_u'])[l])
        sh[f'pv{l}'] = f(np.asarray(inp['peer_v'])[l])
    lnv = np.stack([np.asarray(inp[k]) for k in ('ln1_g', 'ln1_b', 'ln2_g', 'ln2_b')], 1)
    sh['lnv'] = f(lnv)
    sh['lnin'] = f(np.stack([np.asarray(inp['ln_in_g']), np.asarray(inp['ln_in_b'])]))
    sh['consts'] = make_consts()
    ci = np.zeros((128, 4, 256), np.int32)
    ci[:, 0, :] = -128
    ci[:, 1, :] = np.arange(256, dtype=np.int32)[None, :]
    ci[:, 2, :] = 127
    ci[:, 3, :] = -256
    sh['cint'] = ci
    c, s = rope_tables(4096)
    sh['cosd'] = c
    sh['sind'] = s
    return sh


def prep_core(inp, sh, core):
    f = lambda a: np.ascontiguousarray(np.asarray(a, dtype=np.float32))
    b = core // 2
    m = dict(sh)
    m['xp'] = f(np.asarray(inp['x_prompt'])[core * 4:(core + 1) * 4].reshape(1024, 1024))
    m['xs'] = f(np.asarray(inp['x_sample'])[b])
    ck = np.asarray(inp['cache_k'])[b]
    m['ck'] = f(ck[:, :, :, PERM].transpose(0, 3, 2, 1))
    m['cv'] = f(np.asarray(inp['cache_v'])[b].reshape(2, 256, 256))
    st = np.asarray(inp['state_gla'])[b]
    m['st'] = f(st.transpose(0, 1, 3, 2, 4))
    m['cc'] = f(np.stack([np.asarray(inp['c_ctx']), np.asarray(inp['c'])[b]]))
    m['hrow'] = (np.arange(2048, dtype=np.int32) + (core % 2) * 2048).reshape(2048, 1)
    return m


def _stage_ctx(flag):
    if flag:
        es = ExitStack()
        yield es
        es.close()


def build(do_p=True, do_s=True, nlayers=2, dbg=None, peer=True, TM=4096, stages='MAGTCP'):
    nc = bass.Bass("TRN2", target_bir_lowering=False)
    kb = KB(nc)
    ctx_nc = nc.allow_non_contiguous_dma(reason="small strided loads")
    ctx_nc.__enter__()

    def din(name, shape, dt=F32):
        return nc.dram_tensor(name, list(shape), dt, kind="ExternalInput").ap()

    def dout(name, shape, dt=F32):
        return nc.dram_tensor(name, list(shape), dt, kind="ExternalOutput").ap()

    def dscr(name, shape, dt):
        return nc.dram_tensor(name, list(shape), dt, kind="Internal").ap()

    NEXP = 16384 if peer else 128
    I = {}
    for name, shape in [('xp', (1024, 1024)), ('xs', (4096, 1024)), ('ck', (2, 64, 4, 256)), ('cv', (2, 256, 256)),
                        ('st', (2, 2, 128, 4, 256)), ('cc', (2, 1024)), ('win', (2, NBLK, 128, 8, 512)),
                        ('wmod', (2, 12, 128, 8, 512)), ('bmod', (2, 6144)), ('wa2', (2, 33, 1024)),
                        ('glag', (2, 128, 2)), ('convw', (2, 128, 3, 8)), ('sink', (2, 16)),
                        ('wb', (2, 128, 3, 8, 1024)), ('wout', (2, 128, 8, 1024)),
                        ('wpq', (2, 128, 8, 2048)), ('pkeys', (2, 128, 16, 128)), ('pu0', (NEXP, 1024)), ('pu1', (NEXP, 1024)),
                        ('pv0', (NEXP, 1024)), ('pv1', (NEXP, 1024)), ('lnv', (2, 4, 1024)), ('lnin', (2, 1024)),
                        ('consts', (128, 5, 128)), ('cosd', (64, 4096)), ('sind', (64, 4096))]:
        I[name] = din(name, shape)
    I['cint'] = din('cint', (128, 4, 256), I32)
    I['hrow'] = din('hrow', (2048, 1), I32)
    O = {'yp': dout('yp', (1024, 1024)), 'ys': dout('ys', (2048, 1024)),
         'nk': dout('nk', (4, 2, 256, 256)), 'nv': dout('nv', (4, 2, 256, 256)),
         'nst': dout('nst', (4, 2, 2, 4, 128, 256))}
    if peer == 'idx':
        O['dbg_ei'] = dout('dbg_ei', (2, 16, 128, 128))
        O['dbg_eii'] = dout('dbg_eii', (2, 16, 128, 128), I32)
        O['dbg_ts'] = dout('dbg_ts', (2, 16, 128, 128))
    S = {}
    for name, shape, dt in [('qT', (4, 128, TM), BF16), ('kT', (4, 128, TM), BF16), ('ggT', (8, 128, TM), BF16),
                            ('chT', (8, 128, TM), BF16), ('cbT', (8, 128, TM), BF16), ('ccT', (8, 128, TM), BF16),
                            ('qaT', (16, 64, TM), BF16), ('kaT', (4, 64, TM), BF16), ('gT', (24, 128, TM), BF16),
                            ('ktm', (TM, 512), BF16), ('vtm', (TM, 1024), BF16), ('avtm', (TM, 256), BF16),
                            ('la', (TM, 1024), F32), ('of', (8, 128, TM), F32), ('ob', (8, 128, TM), F32),
                            ('ycT', (16, 64, TM), BF16), ('qrT', (16, 64, TM), BF16), ('krT', (4, 64, TM), BF16),
                            ('xres', (TM, 1024), F32)]:
        S[name] = dscr('s_' + name, shape, dt)
    SR = {k: Res(k, dram=True) for k in S}
    xres_r = [Res(f'xres{i}') for i in range(TM // 128)]
    if dbg:
        for name in dbg:
            O['dbg_' + name] = dout('dbg_' + name, S[name].shape, S[name].dtype)

    _uid = [0]

    def uq(n):
        _uid[0] += 1
        return f"t{_uid[0]}_{n}"

    A = lambda n, s, d: nc.alloc_sbuf_tensor(uq(n), s, d)
    cst = A("cst", [128, 5, 128], F32); r_cst = Res('cst')
    kb.dma('sp', lambda e: e.dma_start(out=cst[:], in_=I['consts']), writes=[r_cst])
    ident = cst[:, 0, :]; Mf = cst[:, 1, :]; Mb = cst[:, 2, :]; Nf = cst[:, 3, :]; Nb = cst[:, 4, :]
    cint = A("cint", [128, 4, 256], I32); r_cint = Res('cint')
    kb.dma('sp', lambda e: e.dma_start(out=cint[:], in_=I['cint']), writes=[r_cint])
    ones_bf = A("ones_bf", [128, 128], BF16); r_ones = Res('ones')
    kb.op('dve', lambda e: e.memset(ones_bf[:], 1.0), writes=[r_ones])
    ms_bf = A("ms_bf", [128, 128], BF16); r_msbf = Res('msbf')
    kb.op('dve', lambda e: e.memset(ms_bf[:], 1.0 / 256.0), writes=[r_msbf])
    mcol = A("mcol", [128, 48], F32); r_mcol = Res('mcol')
    mbc = A("mbc", [128, 4, 1024], F32); r_mbc = Res('mbc')
    lnbc = A("lnbc", [128, 4, 1024], F32); r_lnbc = Res('lnbc')
    PS = Ring(nc.alloc_psum_tensor, "ps", [128, 512], F32, 6, excl=True)
    pacc = [nc.alloc_psum_tensor(f"pacc{i}", [128, 512], F32) for i in range(2)]
    r_pacc = [Res(f"pacc{i}", True) for i in range(2)]

    def V(fn, r=(), w=()):
        return kb.op('dve', fn, r, w)

    def Sc(fn, r=(), w=()):
        return kb.op('act', fn, r, w)

    def G(fn, r=(), w=()):
        return kb.op('pool', fn, r, w)

    def T(fn, r=(), w=()):
        return kb.op('pe', fn, r, w)

    def DM(q, out, in_, r=(), w=()):
        return kb.dma(q, lambda e: e.dma_start(out=out, in_=in_), r, w)

    def ln_tm(st, xt, r_xt, gi, bi, small, r_small, r_gb=None):
        r_gb = r_gb or r_lnbc
        stats = small[:, 0:12].rearrange("p (a b) -> p a b", a=2)
        V(lambda e: e.bn_stats(out=small[:, 0:6], in_=xt[:, 0:512]), [r_xt], [r_small])
        V(lambda e: e.bn_stats(out=small[:, 6:12], in_=xt[:, 512:1024]), [r_xt], [r_small])
        V(lambda e: e.bn_aggr(out=small[:, 12:14], in_=stats), [r_small], [r_small])
        Sc(lambda e: e.activation(out=small[:, 14:15], in_=small[:, 13:14], func=ACT.Sqrt, bias=EPS), [r_small], [r_small])
        V(lambda e: e.reciprocal(out=small[:, 14:15], in_=small[:, 14:15]), [r_small], [r_small])
        V(lambda e: e.tensor_scalar(out=xt[:], in0=xt[:], scalar1=small[:, 12:13], scalar2=small[:, 14:15], op0=ALU.subtract, op1=ALU.mult), [r_xt, r_small], [r_xt])
        V(lambda e: e.tensor_tensor(out=xt[:], in0=xt[:], in1=gi, op=ALU.mult), [r_xt, r_gb], [r_xt])
        V(lambda e: e.tensor_tensor(out=xt[:], in0=xt[:], in1=bi, op=ALU.add), [r_xt, r_gb], [r_xt])

    TBL = {}
    r_tab = Res('tab')
    if peer is True:
        for l_ in range(2):
            TBL[f'uv{l_}'] = dscr(f'tb_uv{l_}', (16384, 2048), BF16)
            for hv, k_ in enumerate((f'pu{l_}', f'pv{l_}')):
                for i8 in range(8):
                    DM('pool', TBL[f'uv{l_}'][i8 * 2048:(i8 + 1) * 2048, hv * 1024:(hv + 1) * 1024], I[k_][i8 * 2048:(i8 + 1) * 2048, :], w=[r_tab])
    groups = []
    if do_p:
        groups.append(dict(name='P', T=1024, L=256, nseq=4, samp=False, x=I['xp'], y=O['yp'], ci=0))
    if do_s:
        groups.append(dict(name='S', T=4096, L=4096, nseq=1, samp=True, x=I['xs'], y=O['ys'], ci=1))


    for g in groups:
        Tg, L, nseq, samp = g['T'], g['L'], g['nseq'], g['samp']
        NT = Tg // 128
        for l in range(nlayers):
            last = (l == nlayers - 1)
            kb.barrier()
            for es in _stage_ctx('M' in stages):
                AL = lambda n, s, d: es.enter_context(nc.sbuf_tensor(uq(n), s, d))
                ccol = AL("ccol", [128, 8], F32); scol = AL("scol", [128, 8], BF16); scb = AL("scb", [128, 8, 128], BF16)
                bcol = AL("bcol", [128, 48], F32); brow = AL("brow", [1, 6144], F32); browb = AL("browb", [1, 6144], BF16)
                r_m = Res('mtmp')
                DM('sp', ccol[:], I['cc'][g['ci']].rearrange("(kc p) -> p kc", p=128), w=[r_m])
                DM('sp', bcol[:], I['bmod'][l].rearrange("(j p) -> p j", p=128), w=[r_m])
                DM('sp', brow[:], I['bmod'][l:l + 1, :], w=[r_m])
                DM('sp', lnbc[:], I['lnv'][l].partition_broadcast(128), w=[r_lnbc])
                Sc(lambda e: e.activation(out=scol[:], in_=ccol[:], func=ACT.Silu), [r_m], [r_m])
                V(lambda e: e.tensor_copy(out=scb[:], in_=scol[:].unsqueeze(2).to_broadcast([128, 8, 128])), [r_m], [r_m])
                V(lambda e: e.tensor_copy(out=browb[:], in_=brow[:]), [r_m], [r_m])
                WR = Ring(AL, "wm", [128, 8, 512], BF16, 3)
                for b in range(12):
                    v = b // 2
                    half = b % 2
                    W, rW = WR.get()
                    DM('pool', W[:], I['wmod'][l, b], w=[rW])
                    if v >= 2:
                        ps, rps = PS.get()
                        for kc in range(8):
                            T(lambda e, kc=kc: e.matmul(ps[:, :], lhsT=scb[:, kc, :], rhs=W[:, kc, :], start=(kc == 0), stop=False), [r_m, rW], [rps])
                        T(lambda e: e.matmul(ps[:, :], lhsT=ones_bf[0:1, :], rhs=browb[0:1, b * 512:(b + 1) * 512], start=False, stop=True), [r_m, r_ones], [rps])
                        Sc(lambda e: e.activation(out=mbc[:, v - 2, half * 512:(half + 1) * 512], in_=ps[:, :], func=ACT.Identity,
                                                  bias=(1.0 if v == 4 else 0.0), scale=1.0), [rps], [r_mbc])
                    if v in (0, 1, 3, 4):
                        ps, rps = PS.get()
                        for jj in range(4):
                            for kc in range(8):
                                T(lambda e, kc=kc, jj=jj: e.matmul(ps[:, jj:jj + 1], lhsT=W[:, kc, jj * 128:(jj + 1) * 128], rhs=scol[:, kc:kc + 1],
                                                                   start=(kc == 0), stop=(kc == 7)), [r_m, rW], [rps])
                        V(lambda e: e.tensor_tensor(out=mcol[:, b * 4:(b + 1) * 4], in0=ps[:, 0:4], in1=bcol[:, b * 4:(b + 1) * 4], op=ALU.add), [rps, r_m], [r_mcol])
                        if v in (1, 4):
                            V(lambda e: e.tensor_scalar(out=mcol[:, b * 4:(b + 1) * 4], in0=mcol[:, b * 4:(b + 1) * 4], scalar1=1.0, scalar2=None, op0=ALU.add), [r_mcol], [r_mcol])
                kb.barrier()
            for es in _stage_ctx('A' in stages):
                AL = lambda n, s, d: es.enter_context(nc.sbuf_tensor(uq(n), s, d))
                hT = AL("hT", [128, 8, Tg], BF16)
                lnin = AL("lnin", [128, 2, 1024], F32); r_lnin = Res('lnin')
                if l == 0:
                    DM('sp', lnin[:], I['lnin'].partition_broadcast(128), w=[r_lnin])
                r_hT = [Res(f'hT{i}') for i in range(NT)]
                XR = Ring(AL, "xt", [128, 1024], F32, 3)
                SM = Ring(AL, "sm", [128, 16], F32, 3)
                for i in range(NT):
                    xt, rx = XR.get()
                    sm, rsm = SM.get()
                    rows = slice(i * 128, (i + 1) * 128)
                    if l == 0:
                        DM('sp', xt[:], g['x'][rows, :], w=[rx])
                        ln_tm(None, xt, rx, lnin[:, 0, :], lnin[:, 1, :], sm, rsm, r_lnin)
                        DM('sp', S['xres'][rows, :], xt[:], r=[rx], w=[xres_r[i]])
                    else:
                        DM('sp', xt[:], S['xres'][rows, :], r=[xres_r[i]], w=[rx])
                    for hb in range(2):
                        ps, rps = PS.get()
                        for k4 in range(4):
                            kc = hb * 4 + k4
                            T(lambda e, kc=kc, k4=k4: e.transpose(ps[:, k4 * 128:(k4 + 1) * 128], xt[:, kc * 128:(kc + 1) * 128], ident), [rx, r_cst], [rps])
                        for k4 in range(4):
                            kc = hb * 4 + k4
                            if k4 % 2 == 0:
                                Sc(lambda e, kc=kc, k4=k4: e.activation(out=hT[:, kc, rows], in_=ps[:, k4 * 128:(k4 + 1) * 128], func=ACT.Identity,
                                                                        scale=mcol[:, 8 + kc:9 + kc], bias=mcol[:, kc:kc + 1]), [rps, r_mcol], [r_hT[i]])
                            else:
                                V(lambda e, kc=kc, k4=k4: e.tensor_scalar(out=hT[:, kc, rows], in0=ps[:, k4 * 128:(k4 + 1) * 128], scalar1=mcol[:, 8 + kc:9 + kc],
                                                                          scalar2=mcol[:, kc:kc + 1], op0=ALU.mult, op1=ALU.add), [rps, r_mcol], [r_hT[i]])
                WR = Ring(AL, "wi", [128, 8, 512], BF16, 3)
                EV = Ring(AL, "ev", [128, 512], BF16, 4)
                EVF = Ring(AL, "evf", [128, 1024], F32, 2)
                aT = AL("aT", [33, Tg], BF16); r_aT = Res('aT')
                wa2 = AL("wa2", [33, 1024], BF16); r_wa2 = Res('wa2')
                DM('pool', wa2[:], I['wa2'][l], w=[r_wa2])
                G(lambda e: e.memset(aT[32:33, :], 1.0), [], [r_aT])
                NB4 = Tg // 512
                fm_jobs = [(0, [(h * 128, 128) for h in range(4)], 'qT', 0, ('scale', 128 ** -0.5)),
                           (1, [(h * 128, 128) for h in range(4)], 'kT', 0, None),
                           (2, [(h * 128, 128) for h in range(4)], 'ggT', 0, ('act', ACT.Silu)),
                           (3, [(h * 128, 128) for h in range(4)], 'ggT', 4, ('act', ACT.Silu)),
                           (4, [(0, 32)], 'aT', 0, None),
                           (5, [(h * 128, 128) for h in range(4)], 'chT', 0, None),
                           (6, [(h * 128, 128) for h in range(4)], 'chT', 4, None),
                           (7, [(h * 128, 128) for h in range(4)], 'cbT', 0, None),
                           (8, [(h * 128, 128) for h in range(4)], 'cbT', 4, None),
                           (9, [(h * 128, 128) for h in range(4)], 'ccT', 0, None),
                           (10, [(h * 128, 128) for h in range(4)], 'ccT', 4, None),
                           (11, [(h * 64, 64) for h in range(8)], 'qaT', 0, None),
                           (12, [(h * 64, 64) for h in range(8)], 'qaT', 8, None),
                           (13, [(h * 64, 64) for h in range(4)], 'kaT', 0, None)]
                for i6 in range(6):
                    fm_jobs.append((14 + i6, [(h * 128, 128) for h in range(4)], 'gT', i6 * 4, ('act', ACT.Sigmoid)))
                cnt = 0
                for (blk, chunks, dst, cbase, post) in (fm_jobs if 'b' not in stages else fm_jobs[:int(stages[stages.index('b') + 1:stages.index('b') + 3])]):
                    W, rW = WR.get()
                    DM('pool', W[:], I['win'][l, blk], w=[rW])
                    for tb in range(NB4):
                        cols = slice(tb * 512, (tb + 1) * 512)
                        rh = r_hT[tb * 4:(tb + 1) * 4]
                        for ci, (off, M) in enumerate(chunks):
                            ps, rps = PS.get()
                            for kc in range(8):
                                T(lambda e, kc=kc: e.matmul(ps[0:M, :], lhsT=W[:, kc, off:off + M], rhs=hT[:, kc, cols], start=(kc == 0), stop=(kc == 7)), [rW] + rh, [rps])
                            if dst == 'aT':
                                V(lambda e: e.tensor_copy(out=aT[0:32, cols], in_=ps[0:32, :]), [rps], [r_aT])
                                continue
                            ev, rev = EV.get()
                            cnt += 1
                            if post is None:
                                if cnt % 2 == 0:
                                    V(lambda e: e.tensor_copy(out=ev[0:M, :], in_=ps[0:M, :]), [rps], [rev])
                                else:
                                    Sc(lambda e: e.copy(out=ev[0:M, :], in_=ps[0:M, :]), [rps], [rev])
                            elif post[0] == 'scale':
                                Sc(lambda e: e.mul(out=ev[0:M, :], in_=ps[0:M, :], mul=post[1]), [rps], [rev])
                            else:
                                Sc(lambda e: e.activation(out=ev[0:M, :], in_=ps[0:M, :], func=post[1]), [rps], [rev])
                            DM('sp', S[dst][cbase + ci, :, cols], ev[0:M, :], r=[rev], w=[SR[dst]])
                for tt in range(NT if 'c' not in stages else 0):
                    rows = slice(tt * 128, (tt + 1) * 128)
                    evf, revf = EVF.get()
                    for hf in range(2):
                        ps, rps = PS.get()
                        T(lambda e: e.matmul(ps[:, :], lhsT=aT[0:33, rows], rhs=wa2[0:33, hf * 512:(hf + 1) * 512], start=True, stop=True), [r_aT, r_wa2], [rps])
                        Sc(lambda e: e.activation(out=evf[:, hf * 512:(hf + 1) * 512], in_=ps[:, :], func=ACT.Exp, scale=-1.0), [rps], [revf])
                    Sc(lambda e: e.activation(out=evf[:], in_=evf[:], func=ACT.Ln, bias=1.0), [revf], [revf])
                    G(lambda e: e.tensor_scalar(out=evf[:], in0=evf[:], scalar1=-1.0 / 16.0, scalar2=None, op0=ALU.mult), [revf], [revf])
                    DM('sp', S['la'][rows, :], evf[:], r=[revf], w=[SR['la']])
                for (blk, dst) in ([(20, 'ktm'), (21, 'vtm0'), (22, 'vtm1'), (23, 'kv')] if 'd' not in stages else []):
                    W, rW = WR.get()
                    DM('pool', W[:], I['win'][l, blk], w=[rW])
                    for tt in range(NT):
                        rows = slice(tt * 128, (tt + 1) * 128)
                        ps, rps = PS.get()
                        for kc in range(8):
                            T(lambda e, kc=kc: e.matmul(ps[:, :], lhsT=hT[:, kc, rows], rhs=W[:, kc, :], start=(kc == 0), stop=(kc == 7)), [rW, r_hT[tt]], [rps])
                        if dst == 'kv':
                            ev, rev = EV.get()
                            V(lambda e: e.tensor_copy(out=ev[:, 0:256], in_=ps[:, 256:512]), [rps], [rev])
                            DM('sp', S['avtm'][rows, :], ev[:, 0:256], r=[rev], w=[SR['avtm']])
                            if not samp:
                                evf, revf = EVF.get()
                                Sc(lambda e: e.copy(out=evf[:, 0:512], in_=ps[:, :]), [rps], [revf])
                                sq, t0 = divmod(tt * 128, L)
                                DM('sp', O['nk'][sq, l, t0:t0 + 128, :], evf[:, 0:256], r=[revf])
                                DM('sp', O['nv'][sq, l, t0:t0 + 128, :], evf[:, 256:512], r=[revf])
                        else:
                            ev, rev = EV.get()
                            if tt % 2 == 0:
                                V(lambda e: e.tensor_copy(out=ev[:], in_=ps[:, :]), [rps], [rev])
                            else:
                                Sc(lambda e: e.copy(out=ev[:], in_=ps[:, :]), [rps], [rev])
                            if dst == 'ktm':
                                DM('sp', S['ktm'][rows, :], ev[:], r=[rev], w=[SR['ktm']])
                            else:
                                c0 = 0 if dst == 'vtm0' else 512
                                DM('sp', S['vtm'][rows, c0:c0 + 512], ev[:], r=[rev], w=[SR['vtm']])
                kb.barrier()
            for es in _stage_ctx('G' in stages):
                AL = lambda n, s, d: es.enter_context(nc.sbuf_tensor(uq(n), s, d))
                St = AL("St", [128, 4, 256], F32); Sb = AL("Sb", [128, 4, 256], BF16)
                r_St = Res('St'); r_Sb = Res('Sb')
                LA = Ring(AL, "la", [128, 512], F32, 2)
                KT = Ring(AL, "kt", [128, 512], BF16, 2)
                VT = Ring(AL, "vt", [128, 1024], BF16, 2)
                QC = Ring(AL, "qc", [128, 4, 128], BF16, 2)
                KC = Ring(AL, "kc", [128, 4, 128], BF16, 2)
                ED = Ring(AL, "ed", [128, 512], F32, 2)
                KD = Ring(AL, "kd", [128, 512], BF16, 2)
                EB = Ring(AL, "eb", [128, 256], F32, 3)
                QE = Ring(AL, "qe", [128, 256], BF16, 3)
                AM = Ring(AL, "am", [128, 128], BF16, 3)
                OT = Ring(AL, "ot", [128, 8, 128], F32, 2)
                nch = L // 128
                for sq in range(nseq):
                    for d in (1, 0):
                        Md = Mf if d == 0 else Mb
                        Nd = Nf if d == 0 else Nb
                        endc = 127 if d == 0 else 0
                        odst = 'of' if d == 0 else 'ob'
                        if samp:
                            DM('sp', St[:], I['st'][l, d], w=[r_St])
                        else:
                            V(lambda e: e.memset(St[:], 0.0), [], [r_St])
                        Sc(lambda e: e.copy(out=Sb[:], in_=St[:]), [r_St], [r_Sb])
                        order = range(nch) if d == 0 else range(nch - 1, -1, -1)
                        for c in order:
                            t0 = sq * L + c * 128
                            tk = slice(t0, t0 + 128)
                            la, rla = LA.get(); kt, rkt = KT.get(); vt, rvt = VT.get(); qc, rqc = QC.get(); kc_, rkc = KC.get()
                            DM('sp', la[:], S['la'][tk, d * 512:(d + 1) * 512], r=[SR['la']], w=[rla])
                            DM('sp', kt[:], S['ktm'][tk, :], r=[SR['ktm']], w=[rkt])
                            DM('sp', vt[:], S['vtm'][tk, :], r=[SR['vtm']], w=[rvt])
                            DM('sp', qc[:], S['qT'][:, :, tk].rearrange("h p t -> p h t"), r=[SR['qT']], w=[rqc])
                            DM('sp', kc_[:], S['kT'][:, :, tk].rearrange("h p t -> p h t"), r=[SR['kT']], w=[rkc])
                            ps, rps = PS.get()
                            T(lambda e: e.matmul(ps[:, :], lhsT=Nd, rhs=la[:], start=True, stop=True), [r_cst, rla], [rps])
                            ed, red = ED.get(); kd, rkd = KD.get()
                            Sc(lambda e: e.activation(out=ed[:], in_=ps[:, :], func=ACT.Exp), [rps], [red])
                            G(lambda e: e.tensor_tensor(out=kd[:], in0=kt[:], in1=ed[:], op=ALU.mult), [rkt, red], [rkd])
                            ot, rot = OT.get()
                            for h in range(4):
                                hs = slice(h * 128, (h + 1) * 128)
                                psb, rpsb = PS.get()
                                T(lambda e: e.matmul(psb[:, 0:128], lhsT=la[:, hs], rhs=Md, start=True, stop=True), [r_cst, rla], [rpsb])
                                eb, reb = EB.get()
                                Sc(lambda e: e.activation(out=eb[:, 0:128], in_=psb[:, 0:128], func=ACT.Exp), [rpsb], [reb])
                                Sc(lambda e: e.activation(out=eb[:, 128:256], in_=psb[:, 0:128], func=ACT.Exp, scale=-1.0), [rpsb], [reb])
                                qe, rqe = QE.get()
                                V(lambda e: e.tensor_tensor(out=qe[:, 0:128], in0=qc[:, h, :], in1=eb[:, 0:128], op=ALU.mult), [rqc, reb], [rqe])
                                V(lambda e: e.tensor_tensor(out=qe[:, 128:256], in0=kc_[:, h, :], in1=eb[:, 128:256], op=ALU.mult), [rkc, reb], [rqe])
                                psa, rpsa = PS.get()
                                T(lambda e: e.matmul(psa[:, 0:128], lhsT=qe[:, 128:256], rhs=qe[:, 0:128], start=True, stop=True), [rqe], [rpsa])
                                am, ram = AM.get()
                                V(lambda e: e.tensor_tensor(out=am[:], in0=psa[:, 0:128], in1=Md, op=ALU.mult), [rpsa, r_cst], [ram])
                                pso, rpso = PS.get()
                                for dvc in range(2):
                                    vs = slice(h * 256 + dvc * 128, h * 256 + (dvc + 1) * 128)
                                    T(lambda e: e.matmul(pso[:, dvc * 128:(dvc + 1) * 128], lhsT=vt[:, vs], rhs=am[:], start=True, stop=False), [rvt, ram], [rpso])
                                    T(lambda e: e.matmul(pso[:, dvc * 128:(dvc + 1) * 128], lhsT=Sb[:, h, dvc * 128:(dvc + 1) * 128], rhs=qe[:, 0:128], start=False, stop=True), [r_Sb, rqe], [rpso])
                                Sc(lambda e: e.copy(out=ot[:, 2 * h:2 * h + 2, :], in_=pso[:, 0:256].rearrange("p (a b) -> p a b", a=2)), [rpso], [rot])
                                pss, rpss = PS.get()
                                T(lambda e: e.matmul(pss[:, 0:256], lhsT=kd[:, hs], rhs=vt[:, h * 256:(h + 1) * 256], start=True, stop=True), [rkd, rvt], [rpss])
                                V(lambda e: e.scalar_tensor_tensor(out=St[:, h, :], in0=St[:, h, :], scalar=eb[:, endc:endc + 1], in1=pss[:, 0:256], op0=ALU.mult, op1=ALU.add),
                                  [r_St, reb, rpss], [r_St])
                                G(lambda e: e.tensor_copy(out=Sb[:, h, :], in_=St[:, h, :]), [r_St], [r_Sb])
                            DM('sp', S[odst][:, :, tk].rearrange("j p t -> p j t"), ot[:], r=[rot], w=[SR[odst]])
                        if not samp:
                            DM('sp', O['nst'][sq, l, d].rearrange("h k v -> k h v"), St[:], r=[r_St])
                kb.barrier()
            for es in _stage_ctx('T' in stages):
                AL = lambda n, s, d: es.enter_context(nc.sbuf_tensor(uq(n), s, d))
                esk = AL("esk", [128, 16], F32); r_esk = Res('esk')
                DM('sp', esk[:], I['sink'][l].partition_broadcast(128), w=[r_esk])
                Sc(lambda e: e.activation(out=esk[:], in_=esk[:], func=ACT.Exp), [r_esk], [r_esk])
                nkb = Tg // 128
                VO = AL("VO", [128, nkb, 4, 128], BF16); r_VO = Res('VO')
                G(lambda e: e.memset(VO[:], 1.0), [], [r_VO])
                AVR = Ring(AL, "avr", [128, 256], BF16, 2)
                for m in range(nkb):
                    av, rav = AVR.get()
                    DM('sp', av[:], S['avtm'][m * 128:(m + 1) * 128, :], r=[SR['avtm']], w=[rav])
                    V(lambda e, m=m: e.tensor_copy(out=VO[:, m, :, 0:64], in_=av[:].rearrange("p (j d) -> p j d", j=4)), [rav, r_VO], [r_VO])
                RD = Ring(AL, "rd", [64, 4, 128], F32, 2)
                YC = Ring(AL, "yc", [64, 4, 128], BF16, 2)

                def normalize(pso, rpso, j, t0):
                    rd, rrd = RD.get()
                    yc, ryc = YC.get()
                    for hh in range(4):
                        h = j * 4 + hh
                        V(lambda e, hh=hh, h=h: e.tensor_scalar(out=rd[:, hh, :], in0=pso[64:128, hh * 128:(hh + 1) * 128], scalar1=esk[64:128, h:h + 1], scalar2=None,
                                                                op0=ALU.add), [rpso, r_esk], [rrd])
                    V(lambda e: e.reciprocal(out=rd[:], in_=rd[:]), [rrd], [rrd])
                    V(lambda e: e.tensor_tensor(out=yc[:], in0=pso[0:64, :].rearrange("p (a b) -> p a b", a=4), in1=rd[:], op=ALU.mult), [rpso, rrd], [ryc])
                    DM('sp', S['ycT'][j * 4:(j + 1) * 4, :, t0:t0 + 128].rearrange("h p t -> p h t"), yc[:], r=[ryc], w=[SR['ycT']])

                if not samp:
                    QA = Ring(AL, "qa", [64, 16, 256], BF16, 2)
                    KA = Ring(AL, "ka", [64, 4, 256], BF16, 2)
                    PT = Ring(AL, "pt", [128, 4, 256], BF16, 4)
                    for sq in range(nseq):
                        ts_ = slice(sq * L, (sq + 1) * L)
                        qa, rqa = QA.get(); ka, rka = KA.get()
                        DM('sp', qa[:], S['qaT'][:, :, ts_].rearrange("h p t -> p h t"), r=[SR['qaT']], w=[rqa])
                        DM('sp', ka[:], S['kaT'][:, :, ts_].rearrange("h p t -> p h t"), r=[SR['kaT']], w=[rka])
                        for j in range(4):
                            pts = []
                            for kbk in range(2):
                                pt, rpt = PT.get()
                                pts.append((pt, rpt))
                                for hh in range(4):
                                    ps, rps = PS.get()
                                    T(lambda e: e.matmul(ps[:, 0:256], lhsT=ka[:, j, kbk * 128:(kbk + 1) * 128], rhs=qa[:, j * 4 + hh, :], start=True, stop=True), [rka, rqa], [rps])
                                    Sc(lambda e: e.activation(out=pt[:, hh, :], in_=ps[:, 0:256], func=ACT.Exp, scale=0.125), [rps], [rpt])
                            for qt in range(2):
                                pso, rpso = PS.get()
                                for kbk in range(2):
                                    pt, rpt = pts[kbk]
                                    T(lambda e: e.matmul(pso[:, :].rearrange("p (a b) -> p a b", a=4), lhsT=VO[:, sq * 2 + kbk, j, :], rhs=pt[:, :, qt * 128:(qt + 1) * 128],
                                                         start=(kbk == 0), stop=(kbk == 1)), [r_VO, rpt], [rpso])
                                normalize(pso, rpso, j, sq * L + qt * 128)
                else:
                    with ExitStack() as es2:
                        AL2 = lambda n, s_, d: es2.enter_context(nc.sbuf_tensor(uq(n), s_, d))
                        cs = AL2("cs", [64, 2, 4096], F32); r_cs = Res('cs')
                        DM('sp', cs[:, 0, :], I['cosd'], w=[r_cs])
                        DM('sp', cs[:, 1, :], I['sind'], w=[r_cs])
                        RB = 256
                        XQ = Ring(AL2, "xq", [64, 16, RB], BF16, 2)
                        XO = Ring(AL2, "xo", [64, 16, RB], BF16, 2)
                        T1 = Ring(AL2, "t1", [64, 16, RB], F32, 1)
                        T2 = Ring(AL2, "t2", [64, 16, RB], F32, 1)
                        for (src, dstn, nh) in (('qaT', 'qrT', 16), ('kaT', 'krT', 4)):
                            for tb in range(Tg // RB):
                                cols = slice(tb * RB, (tb + 1) * RB)
                                xq, rxq = XQ.get(); xo, rxo = XO.get(); t1, rt1 = T1.get(); t2, rt2 = T2.get()
                                DM('sp', xq[:, 0:nh, :], S[src][:, :, cols].rearrange("h p t -> p h t"), r=[SR[src]], w=[rxq])
                                cb_lo = cs[0:32, 0, cols].unsqueeze(1).to_broadcast([32, nh, RB])
                                sb_lo = cs[0:32, 1, cols].unsqueeze(1).to_broadcast([32, nh, RB])
                                cb_hi = cs[32:64, 0, cols].unsqueeze(1).to_broadcast([32, nh, RB])
                                sb_hi = cs[32:64, 1, cols].unsqueeze(1).to_broadcast([32, nh, RB])
                                V(lambda e: e.tensor_tensor(out=t1[0:32, 0:nh, :], in0=xq[0:32, 0:nh, :], in1=cb_lo, op=ALU.mult), [rxq, r_cs], [rt1])
                                G(lambda e: e.tensor_tensor(out=t2[0:32, 0:nh, :], in0=xq[32:64, 0:nh, :], in1=sb_hi, op=ALU.mult), [rxq, r_cs], [rt2])
                                V(lambda e: e.tensor_tensor(out=xo[0:32, 0:nh, :], in0=t1[0:32, 0:nh, :], in1=t2[0:32, 0:nh, :], op=ALU.subtract), [rt1, rt2], [rxo])
                                V(lambda e: e.tensor_tensor(out=t1[32:64, 0:nh, :], in0=xq[0:32, 0:nh, :], in1=sb_lo, op=ALU.mult), [rxq, r_cs, rxo], [rt1])
                                G(lambda e: e.tensor_tensor(out=t2[32:64, 0:nh, :], in0=xq[32:64, 0:nh, :], in1=cb_hi, op=ALU.mult), [rxq, r_cs, rxo], [rt2])
                                V(lambda e: e.tensor_tensor(out=xo[32:64, 0:nh, :], in0=t1[32:64, 0:nh, :], in1=t2[32:64, 0:nh, :], op=ALU.add), [rt1, rt2], [rxo])
                                DM('sp', S[dstn][:, :, cols].rearrange("h p t -> p h t"), xo[:, 0:nh, :], r=[rxo], w=[SR[dstn]])
                        kb.barrier()
                    kc_t = AL("kc_t", [64, 4, 256], BF16); r_kct = Res('kct')
                    DM('pool', kc_t[:], I['ck'][l], w=[r_kct])
                    VOc = AL("VOc", [128, 2, 4, 128], BF16); r_VOc = Res('VOc')
                    G(lambda e: e.memset(VOc[:], 1.0), [], [r_VOc])
                    for cbk in range(2):
                        av, rav = AVR.get()
                        DM('pool', av[:], I['cv'][l, cbk * 128:(cbk + 1) * 128, :], w=[rav])
                        V(lambda e, cbk=cbk: e.tensor_copy(out=VOc[:, cbk, :, 0:64], in_=av[:].rearrange("p (j d) -> p j d", j=4)), [rav, r_VOc], [r_VOc])
                    QR = Ring(AL, "qr", [64, 4, 4096], BF16, 1)
                    KR = Ring(AL, "kr", [64, 4096], BF16, 2)
                    PT = Ring(AL, "pt", [128, 4, 384], BF16, 5)
                    PC = Ring(AL, "pc", [128, 2, 4, 512], BF16, 2)
                    for j in range(4):
                        qr, rqr = QR.get(); kr, rkr = KR.get()
                        DM('sp', qr[:], S['qrT'][j * 4:(j + 1) * 4].rearrange("h p t -> p h t"), r=[SR['qrT']], w=[rqr])
                        DM('sp', kr[:], S['krT'][j], r=[SR['krT']], w=[rkr])
                        pts = {}
                        pc = None

                        def pv(n):
                            pso, rpso = PS.get()
                            ms = [m for m in (n - 1, n, n + 1) if 0 <= m < nkb]
                            for ii, m in enumerate(ms):
                                pt, rpt = pts[m]
                                c0 = (n - m + 1) * 128
                                T(lambda e: e.matmul(pso[:, :].rearrange("p (a b) -> p a b", a=4), lhsT=VO[:, m, j, :], rhs=pt[:, :, c0:c0 + 128], start=(ii == 0), stop=False), [r_VO, rpt], [rpso])
                            pcc, rpcc = pc
                            for cbk in range(2):
                                c0 = (n % 4) * 128
                                T(lambda e: e.matmul(pso[:, :].rearrange("p (a b) -> p a b", a=4), lhsT=VOc[:, cbk, j, :], rhs=pcc[:, cbk, :, c0:c0 + 128], start=False, stop=(cbk == 1)), [r_VOc, rpcc], [rpso])
                            normalize(pso, rpso, j, n * 128)

                        pcs = {}
                        for m in range(nkb):
                            if m % 4 == 0:
                                pcn, rpcn = PC.get()
                                pcs[m // 4] = (pcn, rpcn)
                                for cbk in range(2):
                                    for hh in range(4):
                                        ps, rps = PS.get()
                                        T(lambda e: e.matmul(ps[:, :], lhsT=kc_t[:, j, cbk * 128:(cbk + 1) * 128], rhs=qr[:, hh, m * 128:m * 128 + 512], start=True, stop=True), [r_kct, rqr], [rps])
                                        Sc(lambda e: e.activation(out=pcn[:, cbk, hh, :], in_=ps[:, :], func=ACT.Exp, scale=0.125), [rps], [rpcn])
                            pt, rpt = PT.get()
                            pts[m] = (pt, rpt)
                            q0 = max(0, (m - 1) * 128)
                            q1 = min(Tg, (m + 2) * 128)
                            off = q0 - (m - 1) * 128
                            nq = q1 - q0
                            for hh in range(4):
                                ps, rps = PS.get()
                                T(lambda e: e.matmul(ps[:, 0:nq], lhsT=kr[:, m * 128:(m + 1) * 128], rhs=qr[:, hh, q0:q1], start=True, stop=True), [rkr, rqr], [rps])
                                Sc(lambda e: e.activation(out=pt[:, hh, off:off + nq], in_=ps[:, 0:nq], func=ACT.Exp, scale=0.125), [rps], [rpt])
                            if m > 0:
                                G(lambda e: e.tensor_tensor(out=pt[:, :, 0:128], in0=pt[:, :, 0:128], in1=Mf.unsqueeze(1).to_broadcast([128, 4, 128]), op=ALU.mult), [rpt, r_cst], [rpt])
                            if m < nkb - 1:
                                G(lambda e: e.tensor_tensor(out=pt[:, :, 256:384], in0=pt[:, :, 256:384], in1=Mb.unsqueeze(1).to_broadcast([128, 4, 128]), op=ALU.mult), [rpt, r_cst], [rpt])
                            if m >= 1:
                                pc = pcs[(m - 1) // 4]
                                pv(m - 1)
                        pc = pcs[(nkb - 1) // 4]
                        pv(nkb - 1)
                kb.barrier()
            for es in _stage_ctx('C' in stages):
                AL = lambda n, s, d: es.enter_context(nc.sbuf_tensor(uq(n), s, d))
                BS = 256
                wb01 = AL("wb01", [128, 3, 8, 1024], BF16); wo = AL("wo", [128, 8, 1024], BF16)
                r_w = Res('wC')
                for b3 in range(3):
                    DM('pool', wb01[:, b3], I['wb'][l, :, b3], w=[r_w])
                DM('pool', wo[:], I['wout'][l], w=[r_w])
                gcol = AL("gcol", [128, 2], F32); cw = AL("cw", [128, 3, 8], F32); r_gc = Res('gc')
                DM('sp', gcol[:], I['glag'][l], w=[r_gc])
                DM('sp', cw[:], I['convw'][l], w=[r_gc])
                OF = Ring(AL, "of", [128, 8, BS], F32, 1)
                OB = Ring(AL, "ob", [128, 8, BS], F32, 1)
                SQ = Ring(AL, "sq", [128, 8, BS], BF16, 1)
                GGr = Ring(AL, "gg", [128, 8, BS], BF16, 1)
                YA = Ring(AL, "ya", [128, 8, BS], BF16, 1)
                YB = Ring(AL, "yb", [128, 8, BS], BF16, 1)
                YCr = Ring(AL, "ycc", [128, 8, BS], BF16, 1)
                CH = Ring(AL, "ch", [128, 8, BS + 2], BF16, 1)
                CC = Ring(AL, "ccx", [128, 8, BS + 2], BF16, 1)
                CB = Ring(AL, "cbx", [128, 8, BS], BF16, 1)
                Z = Ring(AL, "z", [128, 8, BS + 2], F32, 1)
                ACC = Ring(AL, "acc", [128, BS], F32, 2)
                RS = Ring(AL, "rs", [128, BS], F32, 2)
                GT = Ring(AL, "gt", [128, 24, BS], BF16, 1)
                MG = Ring(AL, "mg", [128, 8, BS], BF16, 1)
                TA = Ring(AL, "ta", [128, BS], F32, 2)
                TB = Ring(AL, "tb", [128, BS], F32, 2)
                XR = Ring(AL, "xt", [128, 1024], F32, 1)
                MX = Ring(AL, "mx", [128, 1024], F32, 1)
                SM = Ring(AL, "sm", [128, 16], F32, 2)
                for bi in range(Tg // BS):
                    t0 = bi * BS
                    cols = slice(t0, t0 + BS)
                    sq0 = (t0 // L) * L
                    of, rof = OF.get(); ob, rob = OB.get(); gg, rgg = GGr.get(); ya, rya = YA.get(); sqt, rsq = SQ.get()
                    DM('sp', of[:], S['of'][:, :, cols].rearrange("j p t -> p j t"), r=[SR['of']], w=[rof])
                    DM('sp', ob[:], S['ob'][:, :, cols].rearrange("j p t -> p j t"), r=[SR['ob']], w=[rob])
                    DM('sp', gg[:], S['ggT'][:, :, cols].rearrange("j p t -> p j t"), r=[SR['ggT']], w=[rgg])
                    V(lambda e: e.tensor_tensor(out=of[:], in0=of[:], in1=ob[:], op=ALU.add), [rof, rob], [rof])
                    Sc(lambda e: e.activation(out=sqt[:], in_=of[:], func=ACT.Square), [rof], [rsq])
                    for h in range(4):
                        ps, rps = PS.get()
                        for dvc in range(2):
                            T(lambda e: e.matmul(ps[:, 0:BS], lhsT=ms_bf[:], rhs=sqt[:, 2 * h + dvc, :], start=(dvc == 0), stop=(dvc == 1)), [r_msbf, rsq], [rps])
                        rs, rrs = RS.get()
                        Sc(lambda e: e.activation(out=rs[:], in_=ps[:, 0:BS], func=ACT.Sqrt, bias=EPS), [rps], [rrs])
                        V(lambda e: e.reciprocal(out=rs[:], in_=rs[:]), [rrs], [rrs])
                        for dvc in range(2):
                            jx = 2 * h + dvc
                            V(lambda e: e.tensor_tensor(out=of[:, jx, :], in0=of[:, jx, :], in1=rs[:], op=ALU.mult), [rof, rrs], [rof])
                            V(lambda e: e.scalar_tensor_tensor(out=ya[:, jx, :], in0=of[:, jx, :], scalar=gcol[:, dvc:dvc + 1], in1=gg[:, jx, :], op0=ALU.mult, op1=ALU.mult),
                              [rof, r_gc, rgg], [rya])
                    chh, rch = CH.get(); ccx, rcc = CC.get(); cbx, rcb = CB.get(); z, rz = Z.get(); yb, ryb = YB.get()
                    lo = 1 if t0 == sq0 else 0
                    hi = 1 if t0 + BS == sq0 + L else 0
                    if lo:
                        G(lambda e: e.memset(chh[:, :, 0:1], 0.0), [], [rch])
                        G(lambda e: e.memset(ccx[:, :, 0:1], 0.0), [], [rcc])
                    if hi:
                        G(lambda e: e.memset(chh[:, :, BS + 1:BS + 2], 0.0), [], [rch])
                        G(lambda e: e.memset(ccx[:, :, BS + 1:BS + 2], 0.0), [], [rcc])
                    src = slice(t0 - 1 + lo, t0 + BS + 1 - hi)
                    dsl = slice(lo, BS + 2 - hi)
                    DM('sp', chh[:, :, dsl], S['chT'][:, :, src].rearrange("j p t -> p j t"), r=[SR['chT']], w=[rch])
                    DM('sp', ccx[:, :, dsl], S['ccT'][:, :, src].rearrange("j p t -> p j t"), r=[SR['ccT']], w=[rcc])
                    DM('sp', cbx[:], S['cbT'][:, :, cols].rearrange("j p t -> p j t"), r=[SR['cbT']], w=[rcb])
                    G(lambda e: e.tensor_tensor(out=z[:], in0=chh[:], in1=ccx[:], op=ALU.mult), [rch, rcc], [rz])
                    for jx in range(8):
                        acc, racc = ACC.get()
                        G(lambda e: e.tensor_scalar(out=acc[:], in0=z[:, jx, 0:BS], scalar1=cw[:, 0, jx:jx + 1], scalar2=None, op0=ALU.mult), [rz, r_gc], [racc])
                        V(lambda e: e.scalar_tensor_tensor(out=acc[:], in0=z[:, jx, 1:BS + 1], scalar=cw[:, 1, jx:jx + 1], in1=acc[:], op0=ALU.mult, op1=ALU.add), [rz, r_gc, racc], [racc])
                        V(lambda e: e.scalar_tensor_tensor(out=acc[:], in0=z[:, jx, 2:BS + 2], scalar=cw[:, 2, jx:jx + 1], in1=acc[:], op0=ALU.mult, op1=ALU.add), [rz, r_gc, racc], [racc])
                        G(lambda e: e.tensor_tensor(out=yb[:, jx, :], in0=acc[:], in1=cbx[:, jx, :], op=ALU.mult), [racc, rcb], [ryb])
                    ycc, rycc = YCr.get(); gt, rgt = GT.get(); mg, rmg = MG.get()
                    ycv = S['ycT'][:, :, cols].rearrange("(c two) p t -> two p c t", two=2)
                    DM('sp', ycc[0:64], ycv[0], r=[SR['ycT']], w=[rycc])
                    DM('sp', ycc[64:128], ycv[1], r=[SR['ycT']], w=[rycc])
                    DM('sp', gt[:], S['gT'][:, :, cols].rearrange("j p t -> p j t"), r=[SR['gT']], w=[rgt])
                    for dch in range(8):
                        dsl_ = slice(dch * 128, (dch + 1) * 128)
                        p0, rp0 = PS.get(); p1, rp1 = PS.get(); p2, rp2 = PS.get()
                        for kc in range(8):
                            T(lambda e, kc=kc: e.matmul(p0[:, 0:BS], lhsT=wb01[:, 0, kc, dsl_], rhs=ya[:, kc, :], start=(kc == 0), stop=(kc == 7)), [r_w, rya], [rp0])
                        for kc in range(8):
                            T(lambda e, kc=kc: e.matmul(p1[:, 0:BS], lhsT=wb01[:, 1, kc, dsl_], rhs=yb[:, kc, :], start=(kc == 0), stop=(kc == 7)), [r_w, ryb], [rp1])
                        for kc in range(8):
                            T(lambda e, kc=kc: e.matmul(p2[:, 0:BS], lhsT=wb01[:, 2, kc, dsl_], rhs=ycc[:, kc, :], start=(kc == 0), stop=(kc == 7)), [r_w, rycc], [rp2])
                        ta, rta = TA.get(); tb_, rtb = TB.get()
                        V(lambda e: e.tensor_tensor(out=ta[:], in0=p0[:, 0:BS], in1=gt[:, dch, :], op=ALU.mult), [rp0, rgt], [rta])
                        V(lambda e: e.tensor_tensor(out=tb_[:], in0=p1[:, 0:BS], in1=gt[:, 8 + dch, :], op=ALU.mult), [rp1, rgt], [rtb])
                        G(lambda e: e.tensor_tensor(out=ta[:], in0=ta[:], in1=tb_[:], op=ALU.add), [rta, rtb], [rta])
                        V(lambda e: e.tensor_tensor(out=tb_[:], in0=p2[:, 0:BS], in1=gt[:, 16 + dch, :], op=ALU.mult), [rp2, rgt, rta], [rtb])
                        G(lambda e: e.tensor_tensor(out=mg[:, dch, :], in0=ta[:], in1=tb_[:], op=ALU.add), [rta, rtb], [rmg])
                    for tt in range(BS // 128):
                        ti = (t0 // 128) + tt
                        rows = slice(ti * 128, (ti + 1) * 128)
                        xt, rx = XR.get(); mx, rmx = MX.get(); sm, rsm = SM.get()
                        DM('sp', xt[:], S['xres'][rows, :], r=[xres_r[ti]], w=[rx])
                        for hf in range(2):
                            ps, rps = PS.get()
                            for kc in range(8):
                                T(lambda e, kc=kc: e.matmul(ps[:, :], lhsT=mg[:, kc, tt * 128:(tt + 1) * 128], rhs=wo[:, kc, hf * 512:(hf + 1) * 512], start=(kc == 0), stop=(kc == 7)), [rmg, r_w], [rps])
                            V(lambda e: e.tensor_tensor(out=mx[:, hf * 512:(hf + 1) * 512], in0=ps[:, :], in1=mbc[:, 0, hf * 512:(hf + 1) * 512], op=ALU.mult), [rps, r_mbc], [rmx])
                        V(lambda e: e.scalar_tensor_tensor(out=xt[:], in0=xt[:], scalar=ALPHA, in1=mx[:], op0=ALU.mult, op1=ALU.add), [rx, rmx], [rx])
                        ln_tm(None, xt, rx, lnbc[:, 0, :], lnbc[:, 1, :], sm, rsm)
                        DM('sp', S['xres'][rows, :], xt[:], r=[rx], w=[xres_r[ti]])
                kb.barrier()
            for es in _stage_ctx('P' in stages):
                AL = lambda n, s, d: es.enter_context(nc.sbuf_tensor(uq(n), s, d))
                wpq = AL("wpq", [128, 8, 2048], BF16); r_wpq = Res('wpq')
                DM('pool', wpq[:], I['wpq'][l], w=[r_wpq])
                pk = AL("pk", [128, 16, 128], F32); r_pk = Res('pk')
                DM('sp', pk[:], I['pkeys'][l], w=[r_pk])
                identb = AL("identb", [128, 128], BF16); r_idb = Res('identb')
                V(lambda e: e.tensor_copy(out=identb[:], in_=ident), [r_cst], [r_idb])
                XR = Ring(AL, "xt", [128, 1024], F32, 2)
                H2 = Ring(AL, "h2", [128, 1024], F32, 2)
                H2T = Ring(AL, "h2T", [128, 8, 128], BF16, 1)
                QT = Ring(AL, "qT", [128, 16, 128], F32, 1)
                SCr = Ring(AL, "scr", [128, 16, 128], F32, 1)
                WK = Ring(AL, "wk", [128, 256], F32, 2)
                TV = Ring(AL, "tv", [128, 2, 16], F32, 2)
                TI = Ring(AL, "ti", [128, 2, 16], I32, 2)
                TF = Ring(AL, "tf", [128, 2, 16], F32, 2)
                CD = Ring(AL, "cd", [128, 16, 16], F32, 2)
                CI = Ring(AL, "ci", [128, 16, 16], F32, 2)
                TS = Ring(AL, "ts", [128, 8, 16], F32, 2)
                EI = Ring(AL, "ei", [128, 128], F32, 2)
                EII = Ring(AL, "eii", [128, 128], I32, 2)
                GA = Ring(AL, "ga", [128, 8, 16], F32, 2)
                SM = Ring(AL, "sm", [128, 16], F32, 2)
                jk = AL("jk", [128, 1024], F32)
                UB = Ring(AL, "ub", [128, 4, 2048], BF16, 3)
                A4 = Ring(AL, "a4", [128, 12], F32, 4)
                DG = Ring(AL, "dg", [128, 4, 128], BF16, 3)
                ACCr = Ring(AL, "acc", [128, 1024], F32, 1)

                half_mode = bool(samp and last)
                NTP = NT // 2 if half_mode else NT
                HIDX = Ring(AL, "hidx", [128, 1], I32, 3)

                def topk_phase(ti):
                    st = {}
                    rows = slice(ti * 128, (ti + 1) * 128)
                    xt, rx = XR.get(); h2, rh2 = H2.get(); h2T, rh2T = H2T.get()
                    st.update(xt=xt, rx=rx, h2=h2, rh2=rh2)
                    if half_mode:
                        hi_, rhi = HIDX.get()
                        DM('sp', hi_[:], I['hrow'][rows, :], w=[rhi])
                        kb.dma('pool', lambda e: e.indirect_dma_start(out=xt[:], out_offset=None, in_=S['xres'],
                                                                      in_offset=bass.IndirectOffsetOnAxis(ap=hi_[:, 0:1], axis=0)), [rhi], [rx])
                    else:
                        DM('sp', xt[:], S['xres'][rows, :], r=[xres_r[ti]], w=[rx])
                    V(lambda e: e.tensor_tensor(out=h2[:], in0=xt[:], in1=mbc[:, 2, :], op=ALU.mult), [rx, r_mbc], [rh2])
                    V(lambda e: e.tensor_tensor(out=h2[:], in0=h2[:], in1=mbc[:, 1, :], op=ALU.add), [rh2, r_mbc], [rh2])
                    if peer is not True:
                        return st
                    for hb in range(2):
                        ps, rps = PS.get()
                        for k4 in range(4):
                            kc = hb * 4 + k4
                            T(lambda e, kc=kc, k4=k4: e.transpose(ps[:, k4 * 128:(k4 + 1) * 128], h2[:, kc * 128:(kc + 1) * 128], ident), [rh2, r_cst], [rps])
                        Sc(lambda e: e.copy(out=h2T[:, hb * 4:(hb + 1) * 4, :], in_=ps[:, :].rearrange("p (a b) -> p a b", a=4)), [rps], [rh2T])
                    qT, rqT = QT.get()
                    for c4 in range(4):
                        ps, rps = PS.get()
                        for cc_ in range(4):
                            ch = c4 * 4 + cc_
                            for kc in range(8):
                                T(lambda e, kc=kc, ch=ch, cc_=cc_: e.matmul(ps[:, cc_ * 128:(cc_ + 1) * 128], lhsT=wpq[:, kc, ch * 128:(ch + 1) * 128], rhs=h2T[:, kc, :],
                                                                            start=(kc == 0), stop=(kc == 7)), [r_wpq, rh2T], [rps])
                        Sc(lambda e: e.copy(out=qT[:, c4 * 4:(c4 + 1) * 4, :], in_=ps[:, :].rearrange("p (a b) -> p a b", a=4)), [rps], [rqT])
                    sc, rsc = SCr.get()
                    for c4 in range(4):
                        ps, rps = PS.get()
                        for cc_ in range(4):
                            ch = c4 * 4 + cc_
                            T(lambda e, ch=ch, cc_=cc_: e.matmul(ps[:, cc_ * 128:(cc_ + 1) * 128], lhsT=qT[:, ch, :], rhs=pk[:, ch, :], start=True, stop=True), [rqT, r_pk], [rps])
                        V(lambda e: e.tensor_copy(out=sc[:, c4 * 4:(c4 + 1) * 4, :], in_=ps[:, :].rearrange("p (a b) -> p a b", a=4)), [rps], [rsc])
                    sci = sc[:].bitcast(I32)
                    V(lambda e: e.tensor_tensor(out=sci, in0=sci, in1=cint[:, 0, 0:128].unsqueeze(1).to_broadcast([128, 16, 128]), op=ALU.bitwise_and), [rsc, r_cint], [rsc])
                    V(lambda e: e.tensor_tensor(out=sci, in0=sci, in1=cint[:, 1, 0:128].unsqueeze(1).to_broadcast([128, 16, 128]), op=ALU.bitwise_or), [rsc, r_cint], [rsc])
                    tsa, rts = TS.get(); ei, rei = EI.get()
                    for h in range(8):
                        tv, rtv = TV.get(); tix, rti = TI.get(); tf, rtf = TF.get()
                        for sd in range(2):
                            ch = h * 2 + sd
                            wk, rwk = WK.get()
                            V(lambda e: e.max(out=tv[:, sd, 0:8], in_=sc[:, ch, :]), [rsc], [rtv])
                            V(lambda e: e.match_replace(out=wk[:, 0:128], in_to_replace=tv[:, sd, 0:8], in_values=sc[:, ch, :], imm_value=-1e30), [rsc, rtv], [rwk])
                            V(lambda e: e.max(out=tv[:, sd, 8:16], in_=wk[:, 0:128]), [rwk], [rtv])
                        V(lambda e: e.tensor_tensor(out=tix[:], in0=tv[:].bitcast(I32), in1=cint[:, 2, 0:32].rearrange("p (a b) -> p a b", a=2), op=ALU.bitwise_and), [rtv, r_cint], [rti])
                        V(lambda e: e.tensor_copy(out=tf[:], in_=tix[:]), [rti], [rtf])
                        cd, rcd = CD.get(); ci_, rci = CI.get()
                        V(lambda e: e.tensor_tensor(out=cd[:], in0=tv[:, 0, :].unsqueeze(2).to_broadcast([128, 16, 16]), in1=tv[:, 1, :].unsqueeze(1).to_broadcast([128, 16, 16]), op=ALU.add), [rtv], [rcd])
                        V(lambda e: e.scalar_tensor_tensor(out=ci_[:], in0=tf[:, 0, :].unsqueeze(2).to_broadcast([128, 16, 16]), scalar=128.0, in1=tf[:, 1, :].unsqueeze(1).to_broadcast([128, 16, 16]),
                                                           op0=ALU.mult, op1=ALU.add), [rtf], [rci])
                        cdf = cd[:].rearrange("p a b -> p (a b)")
                        cdi = cdf.bitcast(I32)
                        V(lambda e: e.tensor_tensor(out=cdi, in0=cdi, in1=cint[:, 3, :], op=ALU.bitwise_and), [rcd, r_cint], [rcd])
                        V(lambda e: e.tensor_tensor(out=cdi, in0=cdi, in1=cint[:, 1, :], op=ALU.bitwise_or), [rcd, r_cint], [rcd])
                        cif = ci_[:].rearrange("p a b -> p (a b)")
                        wk, rwk = WK.get()
                        V(lambda e: e.max(out=tsa[:, h, 0:8], in_=cdf), [rcd], [rts])
                        V(lambda e: e.match_replace(out=wk[:], in_to_replace=tsa[:, h, 0:8], in_values=cdf, imm_value=-1e30), [rcd, rts], [rwk])
                        V(lambda e: e.max(out=tsa[:, h, 8:16], in_=wk[:]), [rwk], [rts])
                        for k in range(16):
                            V(lambda e, k=k: e.scalar_tensor_tensor(out=jk[:, 0:256], in0=cdf, scalar=tsa[:, h, k:k + 1], in1=cif, op0=ALU.is_equal, op1=ALU.mult,
                                                                    accum_out=ei[:, h * 16 + k:h * 16 + k + 1]), [rcd, rci, rts], ([rei] if k in (0, 15) else []))
                    ga, rga = GA.get(); sm, rsm = SM.get()
                    V(lambda e: e.tensor_tensor(out=ga[:], in0=tsa[:], in1=tsa[:, :, 0:1].to_broadcast([128, 8, 16]), op=ALU.subtract), [rts], [rga])
                    Sc(lambda e: e.activation(out=ga[:], in_=ga[:], func=ACT.Exp), [rga], [rga])
                    V(lambda e: e.tensor_reduce(out=sm[:, 0:8], in_=ga[:], axis=AX.X, op=ALU.add), [rga], [rsm])
                    V(lambda e: e.reciprocal(out=sm[:, 0:8], in_=sm[:, 0:8]), [rsm], [rsm])
                    V(lambda e: e.tensor_tensor(out=ga[:], in0=ga[:], in1=sm[:, 0:8].unsqueeze(2).to_broadcast([128, 8, 16]), op=ALU.mult), [rga, rsm], [rga])
                    eii, reii = EII.get()
                    V(lambda e: e.tensor_scalar(out=ei[:], in0=ei[:], scalar1=16383.0, scalar2=0.0, op0=ALU.min, op1=ALU.max), [rei], [rei])
                    V(lambda e: e.tensor_copy(out=eii[:], in_=ei[:]), [rei], [reii])
                    st.update(ga=ga, rga=rga, eii=eii, reii=reii)
                    return st

                def gather_phase(ti, st):
                    rows = slice(ti * 128, (ti + 1) * 128)
                    xt, rx, h2, rh2 = st['xt'], st['rx'], st['h2'], st['rh2']
                    acc, racc = ACCr.get()
                    if peer is not True:
                        V(lambda e: e.memset(acc[:], 0.0), [], [racc])
                    else:
                        ga, rga, eii, reii = st['ga'], st['rga'], st['eii'], st['reii']
                        gaf = ga[:].rearrange("p a b -> p (a b)")
                        for j4 in range(32):
                            ub, rub = UB.get()
                            kb.dma_group('pool', [(lambda e, jx=j4 * 4 + q4, q4=q4: e.indirect_dma_start(out=ub[:, q4, :], out_offset=None, in_=TBL[f'uv{l}'],
                                                   in_offset=bass.IndirectOffsetOnAxis(ap=eii[:, jx:jx + 1], axis=0))) for q4 in range(4)], [reii, r_tab], [rub])
                            a4, ra4 = A4.get()
                            for q4 in range(4):
                                V(lambda e, q4=q4: e.scalar_tensor_tensor(out=jk[:], in0=h2[:], scalar=1.0, in1=ub[:, q4, 0:1024], op0=ALU.mult, op1=ALU.mult, accum_out=a4[:, q4:q4 + 1]),
                                  [rh2, rub], ([ra4] if q4 in (0, 3) else []))
                            Sc(lambda e: e.activation(out=a4[:, 4:8], in_=a4[:, 0:4], func=ACT.Gelu), [ra4], [ra4])
                            V(lambda e, j4=j4: e.tensor_tensor(out=a4[:, 8:12], in0=a4[:, 4:8], in1=gaf[:, j4 * 4:(j4 + 1) * 4], op=ALU.mult), [ra4, rga], [ra4])
                            dg, rdg = DG.get()
                            V(lambda e: e.tensor_tensor(out=dg[:], in0=identb[:].unsqueeze(1).to_broadcast([128, 4, 128]), in1=a4[:, 8:12].unsqueeze(2).to_broadcast([128, 4, 128]), op=ALU.mult),
                              [r_idb, ra4], [rdg])
                            for q4 in range(4):
                                jx = j4 * 4 + q4
                                for hf in range(2):
                                    T(lambda e, jx=jx, q4=q4, hf=hf: e.matmul(pacc[hf][:, :], lhsT=dg[:, q4, :], rhs=ub[:, q4, 1024 + hf * 512:1024 + (hf + 1) * 512], start=(jx == 0), stop=(jx == 127)),
                                      [rdg, rub], [r_pacc[hf]])
                    sm, rsm = SM.get()
                    if peer is True:
                        for hf in range(2):
                            V(lambda e, hf=hf: e.tensor_tensor(out=acc[:, hf * 512:(hf + 1) * 512], in0=pacc[hf][:, :], in1=mbc[:, 3, hf * 512:(hf + 1) * 512], op=ALU.mult), [r_pacc[hf], r_mbc], [racc])
                    V(lambda e: e.scalar_tensor_tensor(out=xt[:], in0=xt[:], scalar=ALPHA, in1=acc[:], op0=ALU.mult, op1=ALU.add), [rx, racc], [rx])
                    ln_tm(None, xt, rx, lnbc[:, 2, :], lnbc[:, 3, :], sm, rsm)
                    if last:
                        DM('sp', g['y'][rows, :], xt[:], r=[rx])
                    else:
                        DM('sp', S['xres'][rows, :], xt[:], r=[rx], w=[xres_r[ti]])

                nxt = topk_phase(0)
                for ti in range(NTP):
                    cur = nxt
                    if ti + 1 < NTP:
                        nxt = topk_phase(ti + 1)
                    gather_phase(ti, cur)
                kb.barrier()
    if dbg:
        kb.barrier()
        for name in dbg:
            DM('sp', O['dbg_' + name], S[name], r=[SR[name]])
    kb.finish()
    ctx_nc.__exit__(None, None, None)
    print("instr counts", kb.etot, "waits", kb.nwaits, kb.wcnt)
    return nc


_CACHE = {}


def kernel(**inputs):
    inp = {k: np.asarray(v) for k, v in inputs.items()}
    sh = prep_shared(inp)
    in_maps = [prep_core(inp, sh, c) for c in range(8)]
    if 'nc' not in _CACHE:
        _CACHE['nc'] = build(do_p=True, do_s=True, nlayers=2)
    nc = _CACHE['nc']
    res = run_bass_kernel_spmd(nc, in_maps, core_ids=list(range(8)))
    R = res.results
    y_prompt = np.concatenate([np.asarray(R[c]['yp']).reshape(4, 256, 1024) for c in range(8)], 0).astype(np.float32)
    ys = []
    for b in range(4):
        ys.append(np.concatenate([np.asarray(R[2 * b]['ys']), np.asarray(R[2 * b + 1]['ys'])], 0))
    y_sample = np.stack(ys, 0).astype(np.float32)
    nk = np.concatenate([np.asarray(R[c]['nk']).reshape(4, 2, 256, 4, 64) for c in range(8)], 0).astype(np.float32)
    nv = np.concatenate([np.asarray(R[c]['nv']).reshape(4, 2, 256, 4, 64) for c in range(8)], 0).astype(np.float32)
    nst = np.concatenate([np.asarray(R[c]['nst']) for c in range(8)], 0).astype(np.float32)
    return (y_prompt, y_sample, nk, nv, nst)
```
